# Optimizing a Trainium2 kernel written in Bass

```python
import jax
import jax.numpy as jnp
from jax import lax
import numpy as np

D_MODEL = 2048
BATCH = 4
SEQ = 2048
DEPTH = 4
DEC_BATCH = 8
DEC_SEQ = 1
PAST_LEN = 16384
PAGE_SIZE = 128

N_MIXERS = 4
EPS = 1e-6
EXPAND = 2
D_INNER = EXPAND * D_MODEL

SSD_HEAD_DIM = 64
SSD_N_HEADS = D_INNER // SSD_HEAD_DIM
SSD_N_GROUPS = 8
SSD_D_STATE = 128
SSD_D_CONV = 4
SSD_CHUNK = 128
SSD_CONV_DIM = D_INNER + 2 * SSD_N_GROUPS * SSD_D_STATE
SSD_IN_DIM = D_INNER + SSD_CONV_DIM + SSD_N_HEADS

POOL_WINDOWS = (2, 4, 8, 16)
POOL_N_GROUPS = len(POOL_WINDOWS)
POOL_GROUP_DIM = D_INNER // POOL_N_GROUPS
POOL_HIST = max(POOL_WINDOWS) - 1

FOX_HEAD_DIM = 128
FOX_N_HEADS = D_MODEL // FOX_HEAD_DIM
FOX_WIDTH = FOX_N_HEADS * FOX_HEAD_DIM
FOX_Q_BLOCK = 128
FOX_IN_DIM = 4 * FOX_WIDTH + FOX_N_HEADS
FOX_FORGET_BIAS = 6.0
FOX_FORGET_NOISE = 0.5

SGU_CHUNK = 128
SGU_N_GROUPS = 8
SGU_GROUP_DIM = D_INNER // SGU_N_GROUPS

N_SSD_LAYERS = len(range(0, DEPTH, N_MIXERS))
N_POOL_LAYERS = len(range(1, DEPTH, N_MIXERS))
N_FOX_LAYERS = len(range(2, DEPTH, N_MIXERS))
N_SGU_LAYERS = len(range(3, DEPTH, N_MIXERS))

kernel_name = 'hybrid_ssd_pool_fox_sgu_step'


def rms_norm(x, w):
    xf = x.astype(jnp.float32)
    y = xf * lax.rsqrt(jnp.mean(xf * xf, axis=-1, keepdims=True) + EPS)
    return (y * w.astype(jnp.float32)).astype(x.dtype)


def causal_dwconv(x_ext, w, b):
    K = w.shape[0]
    S = x_ext.shape[1] - (K - 1)
    out = x_ext[:, 0:S] * w[0]
    for k in range(1, K):
        out = out + x_ext[:, k:k + S] * w[k]
    return out + b


def ssd_chunked_scan(x, dt, A, Bm, Cm, h0):
    bsz, S, H, P = x.shape
    G, N = Bm.shape[2], Bm.shape[3]
    Hg = H // G
    L = min(SSD_CHUNK, S)
    n = -(-S // L)
    pad = n * L - S

    def chunks(a):
        a = jnp.pad(a, [(0, 0), (0, pad)] + [(0, 0)] * (a.ndim - 2))
        return jnp.moveaxis(a.reshape((bsz, n, L) + a.shape[2:]), 1, 0)

    xs = chunks(x.reshape(bsz, S, G, Hg, P))
    dts = chunks(dt.reshape(bsz, S, G, Hg))
    Bs, Cs = chunks(Bm), chunks(Cm)
    Ag = A.reshape(G, Hg)
    mask = jnp.tril(jnp.ones((L, L), dtype=bool))[None, :, :, None, None]

    def step(h, inp):
        xc, dtc, Bc, Cc = inp
        cum = jnp.cumsum(dtc * Ag, axis=1)
        seg = cum[:, :, None] - cum[:, None, :]
        decay = jnp.exp(jnp.where(mask, seg, -jnp.inf))
        cb = jnp.einsum('btgn,bsgn->btsg', Cc, Bc)
        w = cb[..., None] * decay * dtc[:, None]
        y = jnp.einsum('btsgh,bsghp->btghp', w, xc)
        y = y + jnp.einsum('btgn,bghpn->btghp', Cc, h) * jnp.exp(cum)[..., None]
        tail = jnp.exp(cum[:, -1:] - cum) * dtc
        h = h * jnp.exp(cum[:, -1])[..., None, None] + jnp.einsum('bsgh,bsghp,bsgn->bghpn', tail, xc, Bc)
        return h, y

    hT, ys = lax.scan(step, h0.reshape(bsz, G, Hg, P, N), (xs, dts, Bs, Cs))
    y = jnp.moveaxis(ys, 0, 1).reshape(bsz, n * L, H, P)[:, :S]
    return y, hT.reshape(bsz, H, P, N)


def ssd_mixer(h, conv_buf, ssm_state, w_in, conv_w, conv_b, dt_bias, A_log, D_skip, norm_w, w_out):
    bsz, S, _ = h.shape
    f32 = jnp.float32
    z, xbc, dt_raw = jnp.split(h @ w_in, [D_INNER, D_INNER + SSD_CONV_DIM], axis=-1)
    ext = jnp.concatenate([conv_buf.astype(xbc.dtype), xbc], axis=1)
    new_buf = ext[:, -(SSD_D_CONV - 1):]
    xbc = jax.nn.silu(causal_dwconv(ext, conv_w, conv_b))
    xs, Bm, Cm = jnp.split(xbc, [D_INNER, D_INNER + SSD_N_GROUPS * SSD_D_STATE], axis=-1)
    xh = xs.reshape(bsz, S, SSD_N_HEADS, SSD_HEAD_DIM).astype(f32)
    dt = jax.nn.softplus(dt_raw.astype(f32) + dt_bias.astype(f32))
    A = -jnp.exp(A_log.astype(f32))
    y, h_new = ssd_chunked_scan(
        xh, dt, A,
        Bm.reshape(bsz, S, SSD_N_GROUPS, SSD_D_STATE).astype(f32),
        Cm.reshape(bsz, S, SSD_N_GROUPS, SSD_D_STATE).astype(f32),
        ssm_state.astype(f32))
    y = (y + D_skip.astype(f32)[:, None] * xh).reshape(bsz, S, D_INNER).astype(h.dtype)
    y = rms_norm(y * jax.nn.silu(z), norm_w)
    return y @ w_out, new_buf, h_new


def pool_mixer(h, hist, pos0, w_in, w_grp, scale, w_out):
    bsz, S, _ = h.shape
    p, z = jnp.split(h @ w_in, 2, axis=-1)
    ext = jnp.concatenate([hist.astype(p.dtype), p], axis=1)
    new_hist = ext[:, -POOL_HIST:]
    cs = jnp.pad(jnp.cumsum(ext.astype(jnp.float32), axis=1), ((0, 0), (1, 0), (0, 0)))
    pos = pos0 + jnp.arange(S)
    means = []
    for g, w in enumerate(POOL_WINDOWS):
        ch = slice(g * POOL_GROUP_DIM, (g + 1) * POOL_GROUP_DIM)
        win_sum = cs[:, POOL_HIST + 1:POOL_HIST + 1 + S, ch] - cs[:, POOL_HIST + 1 - w:POOL_HIST + 1 - w + S, ch]
        count = jnp.minimum(pos + 1, w).astype(jnp.float32)[None, :, None]
        means.append(win_sum / count)
    mean = jnp.stack(means, axis=2)
    diff = (mean - p.reshape(bsz, S, POOL_N_GROUPS, POOL_GROUP_DIM).astype(jnp.float32)).astype(p.dtype)
    mixed = jnp.einsum('bsgc,gcd->bsgd', diff, w_grp).reshape(bsz, S, D_INNER) * scale
    return (mixed * jax.nn.silu(z)) @ w_out, new_hist


def fox_project(h, w_in, b_f, q_norm, k_norm):
    bsz, S, _ = h.shape
    q, k, v, z, f_logit = jnp.split(h @ w_in, [FOX_WIDTH, 2 * FOX_WIDTH, 3 * FOX_WIDTH, 4 * FOX_WIDTH], axis=-1)
    q = rms_norm(q.reshape(bsz, S, FOX_N_HEADS, FOX_HEAD_DIM), q_norm)
    k = rms_norm(k.reshape(bsz, S, FOX_N_HEADS, FOX_HEAD_DIM), k_norm)
    v = v.reshape(bsz, S, FOX_N_HEADS, FOX_HEAD_DIM)
    logf = jax.nn.log_sigmoid(f_logit.astype(jnp.float32) + b_f.astype(jnp.float32))
    return q, k, v, z, logf


def fox_attend_prompt(q, k, v, logf):
    bsz, S, H, Dh = q.shape
    F = jnp.cumsum(logf, axis=1)
    Fk = jnp.transpose(F, (0, 2, 1))[:, :, None, :]
    nb = S // FOX_Q_BLOCK
    qb = jnp.moveaxis(q.reshape(bsz, nb, FOX_Q_BLOCK, H, Dh), 1, 0)
    Fb = jnp.moveaxis(F.reshape(bsz, nb, FOX_Q_BLOCK, H), 1, 0)
    kpos = jnp.arange(S)
    scale = Dh ** -0.5

    def block(args):
        qi, Fi, i = args
        s = jnp.einsum('bqhd,bkhd->bhqk', qi, k, preferred_element_type=jnp.float32) * scale
        s = s + jnp.transpose(Fi, (0, 2, 1))[..., None] - Fk
        qpos = i * FOX_Q_BLOCK + jnp.arange(FOX_Q_BLOCK)
        s = jnp.where(kpos[None, :] <= qpos[:, None], s, -jnp.inf)
        p = jax.nn.softmax(s, axis=-1)
        return jnp.einsum('bhqk,bkhd->bqhd', p.astype(v.dtype), v)

    out = lax.map(block, (qb, Fb, jnp.arange(nb)))
    return jnp.moveaxis(out, 0, 1).reshape(bsz, S, H, Dh)


def fox_attend_cached(q, k, v, logf, k_pages, v_pages, logf_pages, page_table, layer):
    bsz, S, H, Dh = q.shape
    kp = k_pages[layer, page_table].reshape(bsz, -1, H, Dh)
    vp = v_pages[layer, page_table].reshape(bsz, -1, H, Dh)
    lp = logf_pages[layer, page_table].reshape(bsz, -1, H).astype(jnp.float32)
    n_past = kp.shape[1]
    rc = lax.cumsum(lp, axis=1, reverse=True)
    past_tail = jnp.concatenate([rc[:, 1:], jnp.zeros_like(rc[:, :1])], axis=1)
    Fn = jnp.transpose(jnp.cumsum(logf, axis=1), (0, 2, 1))
    scale = Dh ** -0.5
    s_past = jnp.einsum('bqhd,bkhd->bhqk', q, kp, preferred_element_type=jnp.float32) * scale
    s_past = s_past + Fn[..., None] + jnp.transpose(past_tail, (0, 2, 1))[:, :, None, :]
    s_new = jnp.einsum('bqhd,bkhd->bhqk', q, k, preferred_element_type=jnp.float32) * scale
    s_new = s_new + Fn[..., :, None] - Fn[..., None, :]
    s_new = jnp.where(jnp.tril(jnp.ones((S, S), dtype=bool)), s_new, -jnp.inf)
    p = jax.nn.softmax(jnp.concatenate([s_past, s_new], axis=-1), axis=-1)
    out = jnp.einsum('bhqk,bkhd->bqhd', p[..., :n_past].astype(vp.dtype), vp)
    out = out + jnp.einsum('bhqk,bkhd->bqhd', p[..., n_past:].astype(v.dtype), v)
    return out.astype(q.dtype)


def fox_output(a, z, w_out):
    bsz, S = a.shape[0], a.shape[1]
    return (a.reshape(bsz, S, FOX_WIDTH) * jax.nn.silu(z)) @ w_out


def sgu_mixer(h, w_in, v_norm, w_s, b_s, w_out):
    u, v, z = jnp.split(h @ w_in, 3, axis=-1)
    v = rms_norm(v, v_norm)
    bsz, S, _ = v.shape
    L = min(SGU_CHUNK, S)
    n = -(-S // L)
    pad = n * L - S
    vc = jnp.pad(v, ((0, 0), (0, pad), (0, 0))).reshape(bsz, n, L, SGU_N_GROUPS, SGU_GROUP_DIM)
    W = jnp.tril(w_s[:, :L, :L])
    mixed = jnp.einsum('gts,bnsgc->bntgc', W, vc) + jnp.transpose(b_s[:, :L])[None, None, :, :, None]
    mixed = mixed.reshape(bsz, n * L, D_INNER)[:, :S]
    start = ((S - 1) // SGU_CHUNK) * SGU_CHUNK
    return (u * mixed * jax.nn.silu(z)) @ w_out, v[:, start:]


def setup_inputs(seed: int = 0) -> dict:
    key = jax.random.key(seed)
    ks = iter(jax.random.split(key, 48))
    f32 = jnp.float32

    def nrm(shape, s):
        return jax.random.normal(next(ks), shape, f32) * s

    n_pages = PAST_LEN // PAGE_SIZE
    n_phys = (DEC_BATCH * n_pages * 5) // 4
    x_prompt = nrm((BATCH, SEQ, D_MODEL), 1.0)
    x_sample = nrm((DEC_BATCH, DEC_SEQ, D_MODEL), 1.0)
    state_ssd_conv = nrm((N_SSD_LAYERS, DEC_BATCH, SSD_D_CONV - 1, SSD_CONV_DIM), 1.0)
    state_ssd_ssm = nrm((N_SSD_LAYERS, DEC_BATCH, SSD_N_HEADS, SSD_HEAD_DIM, SSD_D_STATE), 0.5)
    state_pool = nrm((N_POOL_LAYERS, DEC_BATCH, POOL_HIST, D_INNER), 1.0)
    cache_fox_k = nrm((N_FOX_LAYERS, n_phys, PAGE_SIZE, FOX_N_HEADS, FOX_HEAD_DIM), 1.0)
    cache_fox_v = nrm((N_FOX_LAYERS, n_phys, PAGE_SIZE, FOX_N_HEADS, FOX_HEAD_DIM), 1.0)
    cache_fox_logf = jax.nn.log_sigmoid(FOX_FORGET_BIAS + nrm((N_FOX_LAYERS, n_phys, PAGE_SIZE, FOX_N_HEADS), FOX_FORGET_NOISE))
    page_table = jax.random.permutation(next(ks), n_phys)[:DEC_BATCH * n_pages].reshape(DEC_BATCH, n_pages).astype(jnp.int32)

    norm_w = 1.0 + nrm((DEPTH, D_MODEL), 0.02)
    ssd_w_in = nrm((N_SSD_LAYERS, D_MODEL, SSD_IN_DIM), D_MODEL ** -0.5)
    ssd_conv_w = nrm((N_SSD_LAYERS, SSD_D_CONV, SSD_CONV_DIM), SSD_D_CONV ** -0.5)
    ssd_conv_b = nrm((N_SSD_LAYERS, SSD_CONV_DIM), 0.02)
    dt0 = jnp.exp(jax.random.uniform(next(ks), (N_SSD_LAYERS, SSD_N_HEADS), f32, np.log(1e-3), np.log(1e-1)))
    ssd_dt_bias = dt0 + jnp.log(-jnp.expm1(-dt0))
    ssd_A_log = jnp.log(jax.random.uniform(next(ks), (N_SSD_LAYERS, SSD_N_HEADS), f32, 1.0, 16.0))
    ssd_D = 1.0 + nrm((N_SSD_LAYERS, SSD_N_HEADS), 0.1)
    ssd_norm_w = 1.0 + nrm((N_SSD_LAYERS, D_INNER), 0.02)
    ssd_w_out = nrm((N_SSD_LAYERS, D_INNER, D_MODEL), D_INNER ** -0.5)
    pool_w_in = nrm((N_POOL_LAYERS, D_MODEL, 2 * D_INNER), D_MODEL ** -0.5)
    pool_w_grp = nrm((N_POOL_LAYERS, POOL_N_GROUPS, POOL_GROUP_DIM, POOL_GROUP_DIM), POOL_GROUP_DIM ** -0.5)
    pool_scale = 1.0 + nrm((N_POOL_LAYERS, D_INNER), 0.02)
    pool_w_out = nrm((N_POOL_LAYERS, D_INNER, D_MODEL), D_INNER ** -0.5)
    fox_w_in = nrm((N_FOX_LAYERS, D_MODEL, FOX_IN_DIM), D_MODEL ** -0.5)
    fox_b_f = FOX_FORGET_BIAS + nrm((N_FOX_LAYERS, FOX_N_HEADS), FOX_FORGET_NOISE)
    fox_q_norm = 1.0 + nrm((N_FOX_LAYERS, FOX_HEAD_DIM), 0.02)
    fox_k_norm = 1.0 + nrm((N_FOX_LAYERS, FOX_HEAD_DIM), 0.02)
    fox_w_out = nrm((N_FOX_LAYERS, FOX_WIDTH, D_MODEL), FOX_WIDTH ** -0.5)
    sgu_w_in = nrm((N_SGU_LAYERS, D_MODEL, 3 * D_INNER), D_MODEL ** -0.5)
    sgu_v_norm = 1.0 + nrm((N_SGU_LAYERS, D_INNER), 0.02)
    sgu_w_s = nrm((N_SGU_LAYERS, SGU_N_GROUPS, SGU_CHUNK, SGU_CHUNK), SGU_CHUNK ** -0.5)
    sgu_b_s = 1.0 + nrm((N_SGU_LAYERS, SGU_N_GROUPS, SGU_CHUNK), 0.02)
    sgu_w_out = nrm((N_SGU_LAYERS, D_INNER, D_MODEL), D_INNER ** -0.5)
    return {
        'x_prompt': x_prompt, 'x_sample': x_sample,
        'state_ssd_conv': state_ssd_conv, 'state_ssd_ssm': state_ssd_ssm, 'state_pool': state_pool,
        'cache_fox_k': cache_fox_k, 'cache_fox_v': cache_fox_v, 'cache_fox_logf': cache_fox_logf,
        'page_table': page_table,
        'norm_w': norm_w,
        'ssd_w_in': ssd_w_in, 'ssd_conv_w': ssd_conv_w, 'ssd_conv_b': ssd_conv_b,
        'ssd_dt_bias': ssd_dt_bias, 'ssd_A_log': ssd_A_log, 'ssd_D': ssd_D,
        'ssd_norm_w': ssd_norm_w, 'ssd_w_out': ssd_w_out,
        'pool_w_in': pool_w_in, 'pool_w_grp': pool_w_grp, 'pool_scale': pool_scale, 'pool_w_out': pool_w_out,
        'fox_w_in': fox_w_in, 'fox_b_f': fox_b_f, 'fox_q_norm': fox_q_norm, 'fox_k_norm': fox_k_norm,
        'fox_w_out': fox_w_out,
        'sgu_w_in': sgu_w_in, 'sgu_v_norm': sgu_v_norm, 'sgu_w_s': sgu_w_s, 'sgu_b_s': sgu_b_s,
        'sgu_w_out': sgu_w_out,
    }


def reference(x_prompt, x_sample, state_ssd_conv, state_ssd_ssm, state_pool,
              cache_fox_k, cache_fox_v, cache_fox_logf, page_table,
              norm_w, ssd_w_in, ssd_conv_w, ssd_conv_b, ssd_dt_bias, ssd_A_log, ssd_D,
              ssd_norm_w, ssd_w_out, pool_w_in, pool_w_grp, pool_scale, pool_w_out,
              fox_w_in, fox_b_f, fox_q_norm, fox_k_norm, fox_w_out,
              sgu_w_in, sgu_v_norm, sgu_w_s, sgu_b_s, sgu_w_out):

    def run(x, pos0, conv_buf, ssm_state, pool_hist, attend):
        conv_o, ssm_o, pool_o, k_o, v_o, lf_o, sgu_o = [], [], [], [], [], [], []
        for i in range(DEPTH):
            j = i // N_MIXERS
            kind = i % N_MIXERS
            h = rms_norm(x, norm_w[i])
            if kind == 0:
                o, cb, st = ssd_mixer(h, conv_buf[j], ssm_state[j], ssd_w_in[j], ssd_conv_w[j], ssd_conv_b[j],
                                      ssd_dt_bias[j], ssd_A_log[j], ssd_D[j], ssd_norm_w[j], ssd_w_out[j])
                conv_o.append(cb)
                ssm_o.append(st)
            elif kind == 1:
                o, ph = pool_mixer(h, pool_hist[j], pos0, pool_w_in[j], pool_w_grp[j], pool_scale[j], pool_w_out[j])
                pool_o.append(ph)
            elif kind == 2:
                q, k, v, z, logf = fox_project(h, fox_w_in[j], fox_b_f[j], fox_q_norm[j], fox_k_norm[j])
                o = fox_output(attend(j, q, k, v, logf), z, fox_w_out[j])
                k_o.append(k)
                v_o.append(v)
                lf_o.append(logf)
            else:
                o, vr = sgu_mixer(h, sgu_w_in[j], sgu_v_norm[j], sgu_w_s[j], sgu_b_s[j], sgu_w_out[j])
                sgu_o.append(vr)
            x = x + o
        return (x, jnp.stack(conv_o), jnp.stack(ssm_o), jnp.stack(pool_o),
                jnp.stack(k_o), jnp.stack(v_o), jnp.stack(lf_o), jnp.stack(sgu_o))

    bp = x_prompt.shape[0]
    zero_conv = jnp.zeros((N_SSD_LAYERS, bp, SSD_D_CONV - 1, SSD_CONV_DIM), x_prompt.dtype)
    zero_ssm = jnp.zeros((N_SSD_LAYERS, bp, SSD_N_HEADS, SSD_HEAD_DIM, SSD_D_STATE), jnp.float32)
    zero_pool = jnp.zeros((N_POOL_LAYERS, bp, POOL_HIST, D_INNER), x_prompt.dtype)

    def attend_prompt(j, q, k, v, lf):
        return fox_attend_prompt(q, k, v, lf)

    def attend_sample(j, q, k, v, lf):
        return fox_attend_cached(q, k, v, lf, cache_fox_k, cache_fox_v, cache_fox_logf, page_table, j)

    y_prompt, p_conv, p_ssm, p_pool, p_k, p_v, p_lf, p_sgu = run(
        x_prompt, 0, zero_conv, zero_ssm, zero_pool, attend_prompt)
    y_sample, s_conv, s_ssm, s_pool, s_k, s_v, s_lf, s_sgu = run(
        x_sample, PAST_LEN, state_ssd_conv, state_ssd_ssm, state_pool, attend_sample)
    return (y_prompt, y_sample, p_conv, p_ssm, p_pool, p_k, p_v, p_lf, p_sgu,
            s_conv, s_ssm, s_pool, s_k, s_v, s_lf, s_sgu)
```

```python
import contextlib
import numpy as np
import concourse.bass as bass
import concourse.mybir as mybir
from concourse.bass_utils import run_bass_kernel_spmd

F32 = mybir.dt.float32
BF16 = mybir.dt.bfloat16
I32 = mybir.dt.int32
AF = mybir.ActivationFunctionType
ALU = mybir.AluOpType
AX = mybir.AxisListType

D = 2048
DI = 4096
KC = 16
EPS = 1e-6
NSLOT = 5
WCOLS = 512
SAME_ENG_SYNC = True


class KB:
    def __init__(self, nc, stack):
        self.nc = nc
        self.stack = stack
        self.eng = {'pe': nc.tensor, 'act': nc.scalar, 'dve': nc.vector, 'pool': nc.gpsimd, 'sp': nc.sync}
        self.semh = {}
        self.val = {}
        for e in self.eng:
            self.semh[e] = stack.enter_context(nc.semaphore("sem_" + e))
            self.val[e] = 0
        self.seen = {e: {} for e in self.eng}
        self.lastw = {}
        self.readers = {}
        self.ninst = 0

    def dsem(self, name):
        sk = 'd:' + name
        if sk not in self.semh:
            self.semh[sk] = self.stack.enter_context(self.nc.semaphore("dsem_" + name))
            self.val[sk] = 0
        return sk

    def _wait(self, e, deps):
        eng = self.eng[e]
        best = {}
        for (sk, v) in deps:
            if v > best.get(sk, 0):
                best[sk] = v
        for sk, v in best.items():
            if sk == e and (e == 'pe' or not SAME_ENG_SYNC):
                continue
            if self.seen[e].get(sk, 0) >= v:
                continue
            eng.wait_ge(self.semh[sk], v)
            self.seen[e][sk] = v

    def op(self, e, fns, reads=(), writes=(), dma=None):
        if not isinstance(fns, (list, tuple)):
            fns = [fns]
        if STOPPED[0]:
            return None
        if self.ninst >= int(os.environ.get("DBG_MAXI", "100000000")):
            STOPPED[0] = True
            print("DBG_MAXI stop before:", e, "reads", list(reads), "writes", list(writes), flush=True)
            return None
        psum_reads = [k for k in reads if str(k).startswith('pA') or str(k).startswith('pB') or str(k) in ('pS', 'pF')]
        if psum_reads:
            writes = list(writes) + [k for k in psum_reads if k not in writes]
        deps = []
        for k in reads:
            if k in self.lastw:
                deps.append(self.lastw[k])
        for k in writes:
            if k in self.lastw:
                deps.append(self.lastw[k])
            for sk, v in self.readers.get(k, {}).items():
                deps.append((sk, v))
        self._wait(e, deps)
        eng = self.eng[e]
        inst = None
        for f in fns:
            inst = f(eng)
            self.ninst += 1
        if dma is not None:
            sk = self.dsem(dma)
            self.val[sk] += 16
            inst.then_inc(self.semh[sk], 16)
        else:
            sk = e
            self.val[sk] += 1
            inst.then_inc(self.semh[sk], 1)
        tok = (sk, self.val[sk])
        for k in writes:
            self.lastw[k] = tok
            self.readers[k] = {}
        for k in reads:
            r = self.readers.setdefault(k, {})
            if tok[1] > r.get(tok[0], 0):
                r[tok[0]] = tok[1]
        return tok

    def dma(self, q, out, in_, reads=(), writes=(), sem=None, **kw):
        if sem is None:
            sem = writes[0] if len(writes) > 0 and not str(writes[0]).startswith('yp') else 'st_' + str(reads[0])
        return self.op(q, lambda eng: eng.dma_start(out=out, in_=in_, **kw), reads, writes, dma=sem)

    def barrier(self):
        if STOPPED[0]:
            return
        for e in self.eng:
            for sk, h in self.semh.items():
                v = self.val[sk]
                if v > 0 and sk != e and self.seen[e].get(sk, 0) < v:
                    self.eng[e].wait_ge(h, v)
                    self.seen[e][sk] = v

    def finish(self):
        for sk, h in self.semh.items():
            if sk.startswith('d:') and self.val[sk] > 0:
                self.nc.sync.wait_ge(h, self.val[sk])
        for e in ('pe', 'act', 'dve', 'pool'):
            if self.val[e] > 0:
                self.nc.sync.wait_ge(self.semh[e], self.val[e])


import os
class StopBuild(Exception):
    pass
STOPPED = [False]
def ckpt(i):
    if int(os.environ.get("DBG_STOP", "99")) == i:
        STOPPED[0] = True


class Grp:
    def __init__(self, kind, idx, n, slot):
        self.kind, self.idx, self.n, self.slot = kind, idx, n, slot


def build_program(S=2048, NPG=128, NPHYS=1280, layers=(0, 1, 2, 3)):
    NCH = S // 128
    NT = 4
    NTILE = NCH // NT
    nc = bass.Bass("TRN2", target_bir_lowering=False)
    stack = contextlib.ExitStack()

    def din(name, shape, dt=F32):
        return nc.dram_tensor(name, list(shape), dt, kind="ExternalInput").ap()

    def dout(name, shape, dt=F32):
        return nc.dram_tensor(name, list(shape), dt, kind="ExternalOutput").ap()

    xp = din("xp", [S, D]); xs = din("xs", [1, D])
    st_conv = din("st_conv", [3, 6144]); st_ssm = din("st_ssm", [DI, 128]); st_pool = din("st_pool", [15, DI])
    ck = din("ck", [NPHYS * 128, 2048]); cv = din("cv", [NPHYS * 128, 2048]); clf = din("clf", [NPHYS, 2048])
    pt = din("pt", [1, NPG], I32)
    norm_w = din("norm_w", [4, D])
    ssd_w_in = din("ssd_w_in", [D, 10304]); ssd_conv_w = din("ssd_conv_w", [4, 6144]); ssd_conv_b = din("ssd_conv_b", [1, 6144])
    ssd_dt_bias = din("ssd_dt_bias", [1, 64]); ssd_A_log = din("ssd_A_log", [1, 64]); ssd_D = din("ssd_D", [1, 64])
    ssd_norm_w = din("ssd_norm_w", [1, DI]); ssd_w_out = din("ssd_w_out", [DI, D])
    pool_w_in = din("pool_w_in", [D, 8192]); pool_w_grp = din("pool_w_grp", [4, 1024, 1024])
    pool_scale = din("pool_scale", [1, DI]); pool_w_out = din("pool_w_out", [DI, D])
    fox_w_in = din("fox_w_in", [D, 8208]); fox_b_f = din("fox_b_f", [1, 16]); fox_q_norm = din("fox_q_norm", [1, 128])
    fox_k_norm = din("fox_k_norm", [1, 128]); fox_w_out = din("fox_w_out", [D, D])
    sgu_w_in = din("sgu_w_in", [D, 12288]); sgu_v_norm = din("sgu_v_norm", [1, DI]); sgu_w_s = din("sgu_w_s", [8, 128, 128])
    sgu_b_s = din("sgu_b_s", [8, 128]); sgu_w_out = din("sgu_w_out", [DI, D])
    c_ident = din("c_ident", [128, 128]); c_tri = din("c_tri", [128, 128])
    c_pool = din("c_pool", [3, 4, 128, 128]); c_pool_s = din("c_pool_s", [2, 16, 4]); c_sel = din("c_sel", [128, 128]); c_iota = din("c_iota", [128, 1])

    yp = dout("yp", [S, D]); ys = dout("ys", [1, D])
    p_conv = dout("p_conv", [3, 6144]); p_ssm = dout("p_ssm", [DI, 128]); p_pool = dout("p_pool", [15, DI])
    p_k = dout("p_k", [S, D]); p_v = dout("p_v", [S, D]); p_lf = dout("p_lf", [S, 16]); p_sgu = dout("p_sgu", [128, DI])
    s_conv = dout("s_conv", [3, 6144]); s_ssm = dout("s_ssm", [DI, 128]); s_pool = dout("s_pool", [15, DI])
    s_k = dout("s_k", [1, D]); s_v = dout("s_v", [1, D]); s_lf = dout("s_lf", [1, 16]); s_sgu = dout("s_sgu", [1, DI])
    kT_scr = nc.dram_tensor("kT_scr", [16, 128, S], BF16, kind="Internal").ap()
    v_scr = nc.dram_tensor("v_scr", [S, D], BF16, kind="Internal").ap()

    kb = KB(nc, stack)
    nsb = [0]

    def sb(name, shape, dt=F32):
        nsb[0] += 1
        return stack.enter_context(nc.sbuf_tensor(name, list(shape), dt))

    def ps(name, shape, dt=F32):
        return stack.enter_context(nc.psum_tensor(name, list(shape), dt))

    pA = [ps("pA%d" % i, [128, 512]) for i in range(4)]
    pS = ps("pS", [128, 512])
    pF = ps("pF", [128, 512])
    pB = [ps("pB%d" % i, [128, 1024], BF16) for i in range(2)]
    pbi = [0]

    ident_f = sb("ident_f", [128, 128]); ident_b = sb("ident_b", [128, 128], BF16)
    tri_f = sb("tri_f", [128, 128])
    hT = sb("hT", [128, 4, KC, 128], BF16); hTs = sb("hTs", [128, KC, 2], BF16)
    xs_res = sb("xs_res", [1, D])
    nw = sb("nw", [128, D])
    xb = [sb("xb%d" % i, [128, D]) for i in range(2)]
    xbi = [0]
    xo = [sb("xo%d" % i, [128, 512]) for i in range(4)]
    xoi = [0]
    NS = 2
    wsl = [sb("wsl%d" % i, [128, KC, WCOLS], BF16) for i in range(NS)]
    junk = sb("junk", [128, D], BF16)
    sm = sb("sm", [128, NSLOT, 8])
    rstd_x = sb("rstd_x", [128, NSLOT, 2])

    kb.dma('sp', ident_f[:], c_ident[:, :], writes=['ident_f'])
    kb.dma('sp', tri_f[:], c_tri[:, :], writes=['tri_f'])
    kb.op('dve', lambda e: e.tensor_copy(out=ident_b[:], in_=ident_f[:]), reads=['ident_f'], writes=['ident_b'])
    kb.dma('sp', xs_res[:], xs[:, :], writes=['xs_res'])

    wstate = {'i': 0}

    def wget(src, kc, ncols):
        i = wstate['i']; wstate['i'] += 1
        t = wsl[i % NS]; key = 'w%d' % (i % NS)
        kb.dma('pool', t[:, 0:kc, 0:ncols], src.rearrange("(k p) n -> p k n", p=128), writes=[key], sem=key)
        return t, key

    def hT_l(g, k):
        if g.kind == 'p':
            return hT[:, g.slot, k, :]
        return hTs[:, k, 0:1]

    def hkey(g):
        return 'hT%d' % g.slot

    def pacc(g):
        return (pA[g.slot], 'pA%d' % g.slot) if g.kind == 'p' else (pS, 'pS')

    def proj_tok(groups, lhs_fn, lhs_keys_fn, kchunks, wt, wkey, ncols, first=True, last=True, koff=0):
        for g in groups:
            pt_, pk = pacc(g)
            fns = []
            for k in range(kchunks):
                fns.append(lambda e, k=k, g=g, pt_=pt_: e.matmul(
                    pt_[:g.n, 0:ncols], lhsT=lhs_fn(g, koff + k), rhs=wt[:, k, 0:ncols],
                    start=(first and k == 0), stop=(last and k == kchunks - 1)))
            kb.op('pe', fns, reads=[wkey] + lhs_keys_fn(g), writes=[pk])

    def next_pb():
        i = pbi[0] % 2; pbi[0] += 1
        return pB[i], 'pB%d' % i

    def transpose_to(src, n, m, dst, skeys, dkeys, ev='act', mul=None):
        pb, pk = next_pb()
        fns = []
        for i in range(m):
            fns.append(lambda e, i=i: e.transpose(out=pb[:, i * 128:i * 128 + n], in_=src[:, i * 128:(i + 1) * 128],
                                                  identity=ident_b[0:n, 0:n]))
        kb.op('pe', fns, reads=list(skeys) + ['ident_b'], writes=[pk])
        pv = pb[:, 0:m * 128].rearrange("p (m t) -> p m t", t=128)[:, :, 0:n]
        if mul is not None:
            kb.op('dve', lambda e: e.tensor_tensor(out=dst, in0=pv, in1=mul.unsqueeze(2).broadcast_to([128, m, n]), op=ALU.mult),
                  reads=[pk, 'scT', 'nwT'], writes=list(dkeys))
        elif ev == 'act':
            kb.op('act', lambda e: e.activation(out=dst, in_=pv, func=AF.Copy), reads=[pk], writes=list(dkeys))
        else:
            kb.op('dve', lambda e: e.tensor_copy(out=dst, in_=pv), reads=[pk], writes=list(dkeys))

    def rstd_from(ss, n, count, out, keys_r, keys_w):
        kb.op('dve', lambda e: e.tensor_scalar(out=out, in0=ss, scalar1=1.0 / count, scalar2=EPS, op0=ALU.mult, op1=ALU.add),
              reads=keys_r, writes=keys_w)
        kb.op('act', lambda e: e.activation(out=out, in_=out, func=AF.Sqrt), reads=keys_w, writes=keys_w)
        kb.op('dve', lambda e: e.reciprocal(out=out, in_=out), reads=keys_w, writes=keys_w)

    def next_xb():
        i = xbi[0] % 2; xbi[0] += 1
        return xb[i], 'xb%d' % i

    def x_src(layer_pos, g):
        if layer_pos == 0:
            return xp[g.idx * 128:(g.idx + 1) * 128, :]
        return yp[g.idx * 128:(g.idx + 1) * 128, :]

    def load_x(layer_pos, g):
        if g.kind == 's':
            return xs_res[0:1, :], 'xs_res'
        t, k = next_xb()
        rk = ['yp%d_%d' % (g.idx, cb) for cb in range(4)] if layer_pos > 0 else []
        kb.dma('sp', t[:, :], x_src(layer_pos, g), reads=rk, writes=[k])
        return t[:, :], k

    def norm_and_transpose(layer, layer_pos, groups):
        for g in groups:
            n = g.n
            xa, xk = load_x(layer_pos, g)
            ss = sm[0:n, g.slot, 0:1]; rs = rstd_x[0:n, g.slot, 0:1]
            kb.op('act', lambda e: e.activation(out=junk[0:n, :], in_=xa, func=AF.Square, accum_out=ss),
                  reads=[xk], writes=['junk', 'sm%d' % g.slot])
            rstd_from(ss, n, D, rs, ['sm%d' % g.slot], ['rstdx%d' % g.slot])
            kb.op('dve', lambda e: e.scalar_tensor_tensor(out=junk[0:n, :], in0=xa, scalar=rs, in1=nw[0:n, :],
                                                          op0=ALU.mult, op1=ALU.mult),
                  reads=[xk, 'rstdx%d' % g.slot, 'nw'], writes=['junk'])
            for half in range(2):
                if g.kind == 'p':
                    dst = hT[:, g.slot, half * 8:(half + 1) * 8, :]
                else:
                    dst = hTs[:, half * 8:(half + 1) * 8, 0:1]
                transpose_to(junk[0:n, half * 1024:(half + 1) * 1024], n, 8, dst, ['junk'], [hkey(g)])

    def out_proj(layer_pos, groups, w_out, kchunks_total, yT_fn, ykeys_fn, rs_fn=None):
        nhalf = (kchunks_total + KC - 1) // KC
        for cb in range(4):
            for kh in range(nhalf):
                kcn = min(KC, kchunks_total - kh * KC)
                wt, wk = wget(w_out[kh * KC * 128:(kh * KC + kcn) * 128, cb * 512:(cb + 1) * 512], kcn, 512)
                proj_tok(groups, yT_fn, ykeys_fn, kcn, wt, wk, 512, first=(kh == 0), last=(kh == nhalf - 1), koff=kh * KC)
            for g in groups:
                pt_, pk = pacc(g)
                if g.kind == 'p':
                    i = xoi[0] % 4; xoi[0] += 1
                    xt = xo[i]; xk = 'xo%d' % i
                    src = (xp if layer_pos == 0 else yp)[g.idx * 128:(g.idx + 1) * 128, cb * 512:(cb + 1) * 512]
                    rk = ['yp%d_%d' % (g.idx, cb)] if layer_pos > 0 else []
                    kb.dma('sp', xt[:, :], src, reads=rk, writes=[xk])
                    dst = xt[:, :]
                else:
                    dst = xs_res[0:1, cb * 512:(cb + 1) * 512]; xk = 'xs_res'
                if rs_fn is None:
                    kb.op('dve', lambda e, dst=dst, pt_=pt_, g=g: e.tensor_tensor(out=dst, in0=pt_[0:g.n, :], in1=dst, op=ALU.add),
                          reads=[pk, xk], writes=[xk])
                else:
                    rs, rk2 = rs_fn(g)
                    kb.op('dve', lambda e, dst=dst, pt_=pt_, g=g, rs=rs: e.scalar_tensor_tensor(
                        out=dst, in0=pt_[0:g.n, :], scalar=rs, in1=dst, op0=ALU.mult, op1=ALU.add),
                        reads=[pk, xk, rk2], writes=[xk])
                if g.kind == 'p':
                    kb.dma('sp', yp[g.idx * 128:(g.idx + 1) * 128, cb * 512:(cb + 1) * 512], dst, reads=[xk],
                           writes=['yp%d_%d' % (g.idx, cb)], sem='st_' + xk)

    def load_row_bcast(dst, src_row, key, n=128):
        kb.dma('sp', dst, src_row.partition_broadcast(n), writes=[key])

    def layer_sgu(layer_pos):
        with contextlib.ExitStack() as ls:
            def lsb(name, shape, dt=F32):
                return ls.enter_context(nc.sbuf_tensor(name, list(shape), dt))
            yT = lsb("g_yT", [128, 4, 32, 128], BF16); yTs = lsb("g_yTs", [128, 32, 2], BF16)
            vbuf = lsb("g_v", [128, NSLOT, DI], BF16)
            vnw = lsb("g_vnw", [128, DI], BF16)
            vnw_f = lsb("g_vnwf", [128, 512])
            wsT = lsb("g_wsT", [128, 8, 128], BF16)
            wtmp = lsb("g_wtmp", [128, 128]); wtmp_b = lsb("g_wtmpb", [128, 128], BF16)
            bsT = lsb("g_bsT", [128, 8]); bs_tm = lsb("g_bstm", [8, 128])
            uall = lsb("g_uall", [128, NSLOT, 512], BF16); szb = lsb("g_sz", [128, 512], BF16); ytm = lsb("g_ytm", [128, 512], BF16)
            vo = lsb("g_vo", [128, 512])
            ssv = lsb("g_ssv", [128, NSLOT, 8]); rsv = lsb("g_rsv", [128, NSLOT, 2])

            load_row_bcast(nw[:, :], norm_w[3, :], 'nw')
            for b in range(8):
                kb.dma('sp', vnw_f[:, :], sgu_v_norm[0, b * 512:(b + 1) * 512].partition_broadcast(128), writes=['vnw_f'])
                kb.op('dve', lambda e, b=b: e.tensor_copy(out=vnw[:, b * 512:(b + 1) * 512], in_=vnw_f[:, :]),
                      reads=['vnw_f'], writes=['vnw'])
            for gi in range(8):
                kb.dma('sp', wtmp[:, :], sgu_w_s[gi, :, :], writes=['wtmp'])
                kb.op('pe', lambda e: e.transpose(out=pF[:, 0:128], in_=wtmp[:, :], identity=ident_f[:, :]),
                      reads=['wtmp', 'ident_f'], writes=['pF'])
                kb.op('dve', lambda e, gi=gi: e.tensor_tensor(out=wsT[:, gi, :], in0=pF[:, 0:128], in1=tri_f[:, :], op=ALU.mult),
                      reads=['pF', 'tri_f'], writes=['wsT'])
            kb.dma('sp', bs_tm[:, :], sgu_b_s[:, :], writes=['bs_tm'])
            kb.op('pe', lambda e: e.transpose(out=pF[:, 0:8], in_=bs_tm[:, :], identity=ident_f[0:8, 0:8]),
                  reads=['bs_tm', 'ident_f'], writes=['pF'])
            kb.op('dve', lambda e: e.tensor_copy(out=bsT[:, :], in_=pF[:, 0:8]), reads=['pF'], writes=['bsT'])

            def yT_l(g, k):
                return yT[:, g.slot, k, :] if g.kind == 'p' else yTs[:, k, 0:1]
            ckpt(1)

            for ti in range(NTILE):
                groups = [Grp('p', ti * NT + j, 128, j) for j in range(NT)]
                if ti == 0 and not os.environ.get("DBG_NOSAMPLE"):
                    groups.append(Grp('s', 0, 1, 4))
                norm_and_transpose(3, layer_pos, groups)
                ckpt(2)
                for b in range(8):
                    wt, wk = wget(sgu_w_in[:, DI + b * 512:DI + (b + 1) * 512], KC, 512)
                    ckpt(5)
                    proj_tok(groups, hT_l, lambda g: [hkey(g)], KC, wt, wk, 512)
                    ckpt(6)
                    for g in groups:
                        pt_, pk = pacc(g)
                        if not os.environ.get("DBG_NOACC"):
                            kb.op('act', lambda e, g=g, pt_=pt_, b=b: e.activation(out=junk[0:g.n, 0:512], in_=pt_[0:g.n, :], func=AF.Square,
                                                                        accum_out=ssv[0:g.n, g.slot, b:b + 1]),
                              reads=[pk], writes=['junk', 'ssv%d' % g.slot])
                        if b == int(os.environ.get("DBG_B", "99")): STOPPED[0] = True
                        if os.environ.get("DBG_VCOPY") == "act":
                            kb.op('act', lambda e, g=g, pt_=pt_, b=b: e.activation(out=vbuf[0:g.n, g.slot, b * 512:(b + 1) * 512], in_=pt_[0:g.n, :], func=AF.Copy),
                                  reads=[pk], writes=['v%d' % g.slot])
                        elif os.environ.get("DBG_VCOPY") == "f32":
                            kb.op('dve', lambda e, g=g, pt_=pt_, b=b: e.tensor_copy(out=vo[0:g.n, :], in_=pt_[0:g.n, :]),
                                  reads=[pk], writes=['vo'])
                        else:
                            kb.op('dve', lambda e, g=g, pt_=pt_, b=b: e.tensor_copy(out=vbuf[0:g.n, g.slot, b * 512:(b + 1) * 512], in_=pt_[0:g.n, :]),
                                  reads=[pk], writes=['v%d' % g.slot])
                for g in groups:
                    n = g.n
                    kb.op('dve', lambda e, g=g: e.tensor_reduce(out=rsv[0:g.n, g.slot, 1:2], in_=ssv[0:g.n, g.slot, :], axis=AX.X, op=ALU.add),
                          reads=['ssv%d' % g.slot], writes=['rsv%d' % g.slot])
                    rstd_from(rsv[0:n, g.slot, 1:2], n, DI, rsv[0:n, g.slot, 0:1], ['rsv%d' % g.slot], ['rsv%d' % g.slot])
                    kb.op('dve', lambda e, g=g: e.scalar_tensor_tensor(out=vbuf[0:g.n, g.slot, :], in0=vbuf[0:g.n, g.slot, :],
                                                                      scalar=rsv[0:g.n, g.slot, 0:1], in1=vnw[0:g.n, :],
                                                                      op0=ALU.mult, op1=ALU.mult),
                          reads=['v%d' % g.slot, 'rsv%d' % g.slot, 'vnw'], writes=['v%d' % g.slot])
                    is_out = (g.kind == 's') or (g.idx == NCH - 1)
                    if is_out:
                        for b in range(8):
                            kb.op('dve', lambda e, g=g, b=b: e.tensor_copy(out=vo[0:g.n, :], in_=vbuf[0:g.n, g.slot, b * 512:(b + 1) * 512]),
                                  reads=['v%d' % g.slot], writes=['vo'])
                            dst = (s_sgu if g.kind == 's' else p_sgu)[0:n, b * 512:(b + 1) * 512]
                            kb.dma('sp', dst, vo[0:n, :], reads=['vo'])
                ckpt(3)
                for gi in range(8):
                    wtu, wku = wget(sgu_w_in[:, gi * 512:(gi + 1) * 512], KC, 512)
                    proj_tok(groups, hT_l, lambda g: [hkey(g)], KC, wtu, wku, 512)
                    for g in groups:
                        pt_, pk = pacc(g)
                        kb.op('act', lambda e, g=g, pt_=pt_: e.activation(out=uall[0:g.n, g.slot, :], in_=pt_[0:g.n, :], func=AF.Copy),
                              reads=[pk], writes=['u%d' % g.slot])
                    wtz, wkz = wget(sgu_w_in[:, 2 * DI + gi * 512:2 * DI + (gi + 1) * 512], KC, 512)
                    proj_tok(groups, hT_l, lambda g: [hkey(g)], KC, wtz, wkz, 512)
                    for g in groups:
                        n = g.n
                        pt_, pk = pacc(g)
                        kb.op('act', lambda e, g=g, pt_=pt_: e.activation(out=szb[0:g.n, :], in_=pt_[0:g.n, :], func=AF.Silu),
                              reads=[pk], writes=['szb'])
                        kb.op('pe', lambda e, g=g, gi=gi: e.matmul(pF[0:g.n, :], lhsT=wsT[0:g.n, gi, 0:g.n],
                                                                 rhs=vbuf[0:g.n, g.slot, gi * 512:(gi + 1) * 512], start=True, stop=True),
                              reads=['wsT', 'v%d' % g.slot], writes=['pF'])
                        kb.op('dve', lambda e, g=g, gi=gi: e.scalar_tensor_tensor(out=ytm[0:g.n, :], in0=pF[0:g.n, :], scalar=bsT[0:g.n, gi:gi + 1],
                                                                           in1=uall[0:g.n, g.slot, :], op0=ALU.add, op1=ALU.mult),
                              reads=['pF', 'bsT', 'u%d' % g.slot], writes=['ytm'])
                        kb.op('dve', lambda e, g=g: e.tensor_tensor(out=ytm[0:g.n, :], in0=ytm[0:g.n, :], in1=szb[0:g.n, :], op=ALU.mult),
                              reads=['ytm', 'szb'], writes=['ytm'])
                        dst = yT[:, g.slot, gi * 4:(gi + 1) * 4, :] if g.kind == 'p' else yTs[:, gi * 4:(gi + 1) * 4, 0:1]
                        transpose_to(ytm[0:n, :], n, 4, dst, ['ytm'], ['yT%d' % g.slot])
                ckpt(4)
                out_proj(layer_pos, groups, sgu_w_out, 32, yT_l, lambda g: ['yT%d' % g.slot])
            kb.barrier()

    def layer_ssd(layer_pos):
        with contextlib.ExitStack() as ls:
            def lsb(name, shape, dt=F32):
                return ls.enter_context(nc.sbuf_tensor(name, list(shape), dt))
            yT = lsb("d_yT", [128, 4, 32, 128], BF16)
            hst = lsb("d_hst", [128, DI]); hst_b = lsb("d_hstb", [128, DI], BF16)
            chist = lsb("d_chist", [128, 48, 4])
            cw = lsb("d_cw", [128, 48, 4]); cbias = lsb("d_cb", [128, 48]); nwT = lsb("d_nwT", [128, 32])
            tm4 = lsb("d_tm4", [4, 128]); tm32 = lsb("d_tm32", [32, 128])
            dtb = lsb("d_dtb", [128, 64]); Arow = lsb("d_Arow", [128, 64]); Drow = lsb("d_Drow", [128, 64])
            dtv = lsb("d_dt", [128, 4, 64]); cum = lsb("d_cum", [128, 4, 64]); ncum = lsb("d_ncum", [128, 4, 64])
            ecum = lsb("d_ecum", [128, 4, 64]); tail = lsb("d_tail", [128, 4, 64]); tmp64 = lsb("d_tmp64", [128, 64]); tmp64b = lsb("d_tmp64b", [128, 64])
            BT = lsb("d_BT", [128, 8, 512], BF16); CT = lsb("d_CT", [128, 8, 512], BF16); Btm = lsb("d_Btm", [128, 4, 8, 128], BF16)
            xc = lsb("d_xc", [128, 516]); acc = lsb("d_acc", [128, 512]); xsT = lsb("d_xsT", [128, 512], BF16)
            xg = lsb("d_xg", [128, 4, 512], BF16); xdt = lsb("d_xdt", [128, 512], BF16); xtl = lsb("d_xtl", [128, 512], BF16)
            cbm = lsb("d_cbm", [128, 128]); segt = lsb("d_segt", [128, 128]); wT = lsb("d_wT", [128, 128], BF16)
            yg = lsb("d_yg", [128, 4, 512]); t2 = acc; ecx = xc
            szb = lsb("d_szb", [128, 512], BF16); gz = lsb("d_gz", [128, 512], BF16)
            ssg = lsb("d_ssg", [128, 4, 8]); rsg = lsb("d_rsg", [128, 4, 2])
            st3 = lsb("d_st3", [4, 128]); sto = lsb("d_sto", [128, 128])
            selL = lsb("d_selL", [128, 128])

            load_row_bcast(nw[:, :], norm_w[0, :], 'nw')
            load_row_bcast(dtb[:, :], ssd_dt_bias[0, :], 'dtb')
            load_row_bcast(Arow[:, :], ssd_A_log[0, :], 'Arow')
            load_row_bcast(Drow[:, :], ssd_D[0, :], 'Drow')
            kb.dma('sp', selL[:, :], c_sel[:, :], writes=['selL'])
            kb.op('act', lambda e: e.activation(out=Arow[:, :], in_=Arow[:, :], func=AF.Exp), reads=['Arow'], writes=['Arow'])
            kb.op('dve', lambda e: e.tensor_scalar(out=Arow[:, :], in0=Arow[:, :], scalar1=-1.0, scalar2=None, op0=ALU.mult),
                  reads=['Arow'], writes=['Arow'])
            for kc in range(48):
                kb.dma('sp', tm4[0:4, :], ssd_conv_w[:, kc * 128:(kc + 1) * 128], writes=['tm4'])
                kb.op('pe', lambda e, kc=kc: e.transpose(out=pF[:, 0:4], in_=tm4[0:4, :], identity=ident_f[0:4, 0:4]),
                      reads=['tm4', 'ident_f'], writes=['pF'])
                kb.op('dve', lambda e, kc=kc: e.tensor_copy(out=cw[:, kc, :], in_=pF[:, 0:4]), reads=['pF'], writes=['cw'])
            for kc in range(48):
                kb.dma('sp', tm4[0:1, :], ssd_conv_b[:, kc * 128:(kc + 1) * 128], writes=['tm4'])
                kb.op('pe', lambda e, kc=kc: e.transpose(out=pF[:, 0:1], in_=tm4[0:1, :], identity=ident_f[0:1, 0:1]),
                      reads=['tm4', 'ident_f'], writes=['pF'])
                kb.op('dve', lambda e, kc=kc: e.tensor_copy(out=cbias[:, kc:kc + 1], in_=pF[:, 0:1]), reads=['pF'], writes=['cbias'])
            kb.dma('sp', tm32[:, :], ssd_norm_w[0, :].rearrange("(k p) -> k p", p=128), writes=['tm32'])
            kb.op('pe', lambda e: e.transpose(out=pF[:, 0:32], in_=tm32[:, :], identity=ident_f[0:32, 0:32]), reads=['tm32', 'ident_f'], writes=['pF'])
            kb.op('dve', lambda e: e.tensor_copy(out=nwT[:, :], in_=pF[:, 0:32]), reads=['pF'], writes=['nwT'])

            def yT_l(g, k):
                return yT[:, g.slot, k, 0:g.n]

            def hTm(g, k):
                return hT[:, g.slot, k, 0:g.n] if g.kind == 'p' else hTs[:, k, 0:1]

            def state_io(load, dram):
                for c in range(32):
                    if load:
                        kb.dma('sp', sto[:, :], dram[c * 128:(c + 1) * 128, :], writes=['sto'])
                        kb.op('pe', lambda e: e.transpose(out=pF[:, 0:128], in_=sto[:, :], identity=ident_f[:, :]), reads=['sto', 'ident_f'], writes=['pF'])
                        kb.op('dve', lambda e, c=c: e.tensor_copy(out=hst[:, c * 128:(c + 1) * 128], in_=pF[:, 0:128]), reads=['pF'], writes=['hst%d' % (c // 4)])
                    else:
                        kb.op('pe', lambda e, c=c: e.transpose(out=pF[:, 0:128], in_=hst[:, c * 128:(c + 1) * 128], identity=ident_f[:, :]),
                              reads=['hst%d' % (c // 4), 'ident_f'], writes=['pF'])
                        kb.op('dve', lambda e: e.tensor_copy(out=sto[:, :], in_=pF[:, 0:128]), reads=['pF'], writes=['sto'])
                        kb.dma('sp', dram[c * 128:(c + 1) * 128, :], sto[:, :], reads=['sto'])

            passes = [('s', None)] + [('p', ti) for ti in range(NTILE)]
            for (pkind, ti) in passes:
                if pkind == 's':
                    groups = [Grp('s', 0, 1, 0)]
                    state_io(True, st_ssm)
                    for kc in range(48):
                        kb.dma('sp', tm4[0:3, :], st_conv[:, kc * 128:(kc + 1) * 128], writes=['tm4'])
                        kb.op('pe', lambda e, kc=kc: e.transpose(out=pF[:, 0:3], in_=tm4[0:3, :], identity=ident_f[0:3, 0:3]),
                              reads=['tm4', 'ident_f'], writes=['pF'])
                        kb.op('dve', lambda e, kc=kc: e.tensor_copy(out=chist[:, kc, 0:3], in_=pF[:, 0:3]), reads=['pF'], writes=['chist'])
                else:
                    groups = [Grp('p', ti * NT + j, 128, j) for j in range(NT)]
                    if ti == 0:
                        kb.op('dve', lambda e: e.memset(hst[:, :], 0.0), reads=[], writes=['hst%d' % i for i in range(8)])
                        kb.op('dve', lambda e: e.memset(chist[:, :, :], 0.0), reads=[], writes=['chist'])
                for gi in range(8):
                    kb.op('act', lambda e, gi=gi: e.activation(out=hst_b[:, gi * 512:(gi + 1) * 512], in_=hst[:, gi * 512:(gi + 1) * 512], func=AF.Copy),
                          reads=['hst%d' % gi], writes=['hstb%d' % gi])
                N = sum(g.n for g in groups)
                n0 = groups[0].n
                norm_and_transpose(0, layer_pos, groups)
                wt, wk = wget(ssd_w_in[:, 10240:10304], KC, 64)
                proj_tok(groups, hTm, lambda g: [hkey(g)], KC, wt, wk, 64)
                for g in groups:
                    n = g.n; sl = g.slot
                    pt_, pk = pacc(g)
                    dk = 'dt%d' % sl
                    kb.op('dve', lambda e, g=g, pt_=pt_: e.tensor_tensor(out=tmp64[0:g.n, :], in0=pt_[0:g.n, 0:64], in1=dtb[0:g.n, :], op=ALU.add),
                          reads=[pk, 'dtb'], writes=['tmp64'])
                    kb.op('act', lambda e, n=n: e.activation(out=tmp64b[0:n, :], in_=tmp64[0:n, :], func=AF.Abs),
                          reads=['tmp64'], writes=['tmp64b'])
                    kb.op('act', lambda e, n=n: e.activation(out=tmp64b[0:n, :], in_=tmp64b[0:n, :], func=AF.Exp, scale=-1.0), reads=['tmp64b'], writes=['tmp64b'])
                    kb.op('dve', lambda e, n=n: e.tensor_scalar(out=tmp64b[0:n, :], in0=tmp64b[0:n, :], scalar1=1.0, scalar2=None, op0=ALU.add),
                          reads=['tmp64b'], writes=['tmp64b'])
                    kb.op('act', lambda e, n=n: e.activation(out=tmp64b[0:n, :], in_=tmp64b[0:n, :], func=AF.Ln), reads=['tmp64b'], writes=['tmp64b'])
                    kb.op('dve', lambda e, n=n, sl=sl: e.scalar_tensor_tensor(out=dtv[0:n, sl, :], in0=tmp64[0:n, :], scalar=0.0, in1=tmp64b[0:n, :],
                                                                          op0=ALU.max, op1=ALU.add), reads=['tmp64', 'tmp64b'], writes=[dk])
                    kb.op('dve', lambda e, n=n, sl=sl: e.tensor_tensor(out=tmp64[0:n, :], in0=dtv[0:n, sl, :], in1=Arow[0:n, :], op=ALU.mult),
                          reads=[dk, 'Arow'], writes=['tmp64'])
                    kb.op('pe', lambda e, n=n: e.matmul(pF[0:n, 0:64], lhsT=tri_f[0:n, 0:n], rhs=tmp64[0:n, :], start=True, stop=True),
                          reads=['tri_f', 'tmp64'], writes=['pF'])
                    kb.op('dve', lambda e, n=n, sl=sl: e.tensor_copy(out=cum[0:n, sl, :], in_=pF[0:n, 0:64]), reads=['pF'], writes=[dk + 'c'])
                    kb.op('dve', lambda e, n=n, sl=sl: e.tensor_scalar(out=ncum[0:n, sl, :], in0=cum[0:n, sl, :], scalar1=-1.0, scalar2=None, op0=ALU.mult),
                          reads=[dk + 'c'], writes=[dk + 'n'])
                    kb.op('act', lambda e, n=n, sl=sl: e.activation(out=ecum[0:n, sl, :], in_=cum[0:n, sl, :], func=AF.Exp), reads=[dk + 'c'], writes=[dk + 'e'])
                    sel = selL[0:n, 0:n] if n == 128 else tri_f[0:1, 0:1]
                    kb.op('pe', lambda e, n=n, sl=sl, sel=sel: e.matmul(pF[0:n, 0:64], lhsT=sel, rhs=cum[0:n, sl, :], start=True, stop=True),
                          reads=['selL', 'tri_f', dk + 'c'], writes=['pF'])
                    kb.op('dve', lambda e, n=n, sl=sl: e.tensor_tensor(out=tmp64[0:n, :], in0=pF[0:n, 0:64], in1=cum[0:n, sl, :], op=ALU.subtract),
                          reads=['pF', dk + 'c'], writes=['tmp64'])
                    kb.op('act', lambda e, n=n: e.activation(out=tmp64[0:n, :], in_=tmp64[0:n, :], func=AF.Exp), reads=['tmp64'], writes=['tmp64'])
                    kb.op('dve', lambda e, n=n, sl=sl: e.tensor_tensor(out=tail[0:n, sl, :], in0=tmp64[0:n, :], in1=dtv[0:n, sl, :], op=ALU.mult),
                          reads=['tmp64', dk], writes=[dk + 't'])

                def conv_chunk(wt, wk, cc, kc):
                    fns = []
                    if groups[0].kind == 'p':
                        rhs_fn = lambda k: hT[:, :, k, :]
                    else:
                        rhs_fn = lambda k: hTs[:, k, 0:1]
                    outp = pA[0][:, 0:N] if groups[0].kind == 's' else pA[0][:, 0:512].rearrange("p (s t) -> p s t", t=128)
                    for k in range(KC):
                        fns.append(lambda e, k=k: e.matmul(outp, lhsT=wt[:, k, cc * 128:(cc + 1) * 128], rhs=rhs_fn(k), start=(k == 0), stop=(k == KC - 1)))
                    kb.op('pe', fns, reads=[wk] + [hkey(g) for g in groups], writes=['pA0'])
                    kb.op('act', lambda e: e.activation(out=xc[:, 3:3 + N], in_=pA[0][:, 0:N], func=AF.Copy), reads=['pA0'], writes=['xc'])
                    kb.op('dve', lambda e: e.tensor_copy(out=xc[:, 0:3], in_=chist[:, kc, 0:3]), reads=['chist'], writes=['xc'])
                    kb.op('dve', lambda e: e.tensor_copy(out=chist[:, kc, 0:3], in_=xc[:, N:N + 3]), reads=['xc'], writes=['chist'])
                    kb.op('dve', lambda e: e.tensor_scalar(out=acc[:, 0:N], in0=xc[:, 0:N], scalar1=cw[:, kc, 0:1], scalar2=None, op0=ALU.mult),
                          reads=['xc', 'cw'], writes=['acc'])
                    for j in range(1, 4):
                        kb.op('dve', lambda e, j=j: e.scalar_tensor_tensor(out=acc[:, 0:N], in0=xc[:, j:j + N], scalar=cw[:, kc, j:j + 1], in1=acc[:, 0:N],
                                                                          op0=ALU.mult, op1=ALU.add), reads=['xc', 'cw', 'acc'], writes=['acc'])
                    kb.op('act', lambda e: e.activation(out=xsT[:, 0:N], in_=acc[:, 0:N], func=AF.Silu, bias=cbias[:, kc:kc + 1]),
                          reads=['acc', 'cbias'], writes=['xsT'])
                    last = (pkind == 's') or (ti == NTILE - 1)
                    if last:
                        kb.op('pe', lambda e: e.transpose(out=pF[0:3, 0:128], in_=xc[:, N:N + 3], identity=ident_f[:, :]), reads=['xc', 'ident_f'], writes=['pF'])
                        kb.op('dve', lambda e: e.tensor_copy(out=st3[0:3, :], in_=pF[0:3, 0:128]), reads=['pF'], writes=['st3'])
                        kb.dma('sp', (s_conv if pkind == 's' else p_conv)[:, kc * 128:(kc + 1) * 128], st3[0:3, :], reads=['st3'])

                for which, base_kc, dst in (('B', 32, BT), ('C', 40, CT)):
                    for blk in range(2):
                        col0 = DI + base_kc * 128 + blk * 512
                        wt, wk = wget(ssd_w_in[:, col0:col0 + 512], KC, 512)
                        for cc in range(4):
                            gg = blk * 4 + cc
                            conv_chunk(wt, wk, cc, base_kc + gg)
                            kb.op('dve', lambda e, gg=gg, dst=dst: e.tensor_copy(out=dst[:, gg, 0:N], in_=xsT[:, 0:N]), reads=['xsT'], writes=[which + 'T%d' % gg])
                            if which == 'B':
                                for g in groups:
                                    pb, pk2 = next_pb()
                                    kb.op('pe', lambda e, g=g, pb=pb: e.transpose(out=pb[0:g.n, 0:128], in_=xsT[:, g.slot * 128:g.slot * 128 + g.n],
                                                                              identity=ident_b[:, :]), reads=['xsT', 'ident_b'], writes=[pk2])
                                    kb.op('act', lambda e, g=g, pb=pb, gg=gg: e.activation(out=Btm[0:g.n, g.slot, gg, :], in_=pb[0:g.n, 0:128], func=AF.Copy),
                                          reads=[pk2], writes=['Btm%d' % gg])
                for gi in range(8):
                    wt, wk = wget(ssd_w_in[:, DI + gi * 512:DI + (gi + 1) * 512], KC, 512)
                    for cc in range(4):
                        conv_chunk(wt, wk, cc, gi * 4 + cc)
                        for g in groups:
                            pb, pk2 = next_pb()
                            kb.op('pe', lambda e, g=g, pb=pb: e.transpose(out=pb[0:g.n, 0:128], in_=xsT[:, g.slot * 128:g.slot * 128 + g.n],
                                                                      identity=ident_b[:, :]), reads=['xsT', 'ident_b'], writes=[pk2])
                            kb.op('act', lambda e, g=g, pb=pb, cc=cc: e.activation(out=xg[0:g.n, g.slot, cc * 128:(cc + 1) * 128], in_=pb[0:g.n, 0:128], func=AF.Copy),
                                  reads=[pk2], writes=['xg%d' % g.slot])
                    for g in groups:
                        n = g.n; sl = g.slot
                        hs = slice(gi * 8, gi * 8 + 8)
                        dk = 'dt%d' % sl
                        def bc(ap):
                            return ap.unsqueeze(2).broadcast_to([n, 8, 64])
                        x3 = xg[0:n, sl, :].rearrange("p (h d) -> p h d", d=64)
                        kb.op('dve', lambda e: e.tensor_tensor(out=xdt[0:n, :].rearrange("p (h d) -> p h d", d=64), in0=x3, in1=bc(dtv[0:n, sl, hs]), op=ALU.mult),
                              reads=['xg%d' % sl, dk], writes=['xdt'])
                        kb.op('dve', lambda e: e.tensor_tensor(out=xtl[0:n, :].rearrange("p (h d) -> p h d", d=64), in0=x3, in1=bc(tail[0:n, sl, hs]), op=ALU.mult),
                              reads=['xg%d' % sl, dk + 't'], writes=['xtl'])
                        kb.op('pe', lambda e: e.matmul(pF[0:n, 0:n], lhsT=BT[:, gi, sl * 128:sl * 128 + n], rhs=CT[:, gi, sl * 128:sl * 128 + n], start=True, stop=True),
                              reads=['BT%d' % gi, 'CT%d' % gi], writes=['pF'])
                        kb.op('dve', lambda e: e.tensor_tensor(out=cbm[0:n, 0:n], in0=pF[0:n, 0:n], in1=tri_f[0:n, 0:n], op=ALU.mult),
                              reads=['pF', 'tri_f'], writes=['cbm'])
                        for hh in range(8):
                            h = gi * 8 + hh
                            kb.op('pe', lambda e, h=h: e.matmul(pS[0:n, 0:n], lhsT=cum[0:n, sl, h:h + 1].broadcast_to([n, n]), rhs=ident_f[0:n, 0:n], start=True, stop=True),
                                  reads=[dk + 'c', 'ident_f'], writes=['pS'])
                            kb.op('dve', lambda e, h=h: e.tensor_scalar(out=segt[0:n, 0:n], in0=pS[0:n, 0:n], scalar1=ncum[0:n, sl, h:h + 1], scalar2=0.0,
                                                                       op0=ALU.add, op1=ALU.min), reads=['pS', dk + 'n'], writes=['segt'])
                            kb.op('act', lambda e: e.activation(out=segt[0:n, 0:n], in_=segt[0:n, 0:n], func=AF.Exp), reads=['segt'], writes=['segt'])
                            kb.op('dve', lambda e: e.tensor_tensor(out=wT[0:n, 0:n], in0=segt[0:n, 0:n], in1=cbm[0:n, 0:n], op=ALU.mult),
                                  reads=['segt', 'cbm'], writes=['wT'])
                            kb.op('pe', lambda e, hh=hh: e.matmul(pA[1][0:n, hh * 64:(hh + 1) * 64], lhsT=wT[0:n, 0:n], rhs=xdt[0:n, hh * 64:(hh + 1) * 64], start=True, stop=True),
                                  reads=['wT', 'xdt'], writes=['pA1'])
                        kb.op('pe', lambda e: e.matmul(pA[2][0:n, :], lhsT=CT[:, gi, sl * 128:sl * 128 + n], rhs=hst_b[:, gi * 512:(gi + 1) * 512], start=True, stop=True),
                              reads=['CT%d' % gi, 'hstb%d' % gi], writes=['pA2'])
                        y3 = yg[0:n, sl, :].rearrange("p (h d) -> p h d", d=64)
                        kb.op('dve', lambda e: e.tensor_tensor(out=y3, in0=pA[2][0:n, :].rearrange("p (h d) -> p h d", d=64), in1=bc(ecum[0:n, sl, hs]), op=ALU.mult),
                              reads=['pA2', dk + 'e'], writes=['yg%d' % sl])
                        kb.op('dve', lambda e: e.tensor_tensor(out=yg[0:n, sl, :], in0=yg[0:n, sl, :], in1=pA[1][0:n, :], op=ALU.add),
                              reads=['pA1', 'yg%d' % sl], writes=['yg%d' % sl])
                        kb.op('dve', lambda e: e.tensor_tensor(out=t2[0:n, :].rearrange("p (h d) -> p h d", d=64), in0=x3, in1=bc(Drow[0:n, hs]), op=ALU.mult),
                              reads=['xg%d' % sl, 'Drow'], writes=['acc'])
                        kb.op('dve', lambda e: e.tensor_tensor(out=yg[0:n, sl, :], in0=yg[0:n, sl, :], in1=t2[0:n, :], op=ALU.add),
                              reads=['acc', 'yg%d' % sl], writes=['yg%d' % sl])
                        kb.op('dve', lambda e: e.tensor_copy(out=ecx[0:n, 0:512].rearrange("p (h d) -> p h d", d=64), in_=bc(ecum[0:n, sl, hs])),
                              reads=[dk + 'e'], writes=['xc'])
                        sel = selL[0:n, :] if n == 128 else tri_f[0:1, :]
                        kb.op('pe', lambda e, sel=sel: e.matmul(pA[3][:, :], lhsT=sel, rhs=ecx[0:n, 0:512], start=True, stop=True), reads=['selL', 'tri_f', 'xc'], writes=['pA3'])
                        kb.op('dve', lambda e: e.tensor_tensor(out=hst[:, gi * 512:(gi + 1) * 512], in0=hst[:, gi * 512:(gi + 1) * 512], in1=pA[3][:, :], op=ALU.mult),
                              reads=['pA3', 'hst%d' % gi, 'hstb%d' % gi], writes=['hst%d' % gi])
                        kb.op('pe', lambda e: e.matmul(pA[3][:, :], lhsT=Btm[0:n, sl, gi, :], rhs=xtl[0:n, :], start=True, stop=True), reads=['Btm%d' % gi, 'xtl'], writes=['pA3'])
                        kb.op('dve', lambda e: e.tensor_tensor(out=hst[:, gi * 512:(gi + 1) * 512], in0=hst[:, gi * 512:(gi + 1) * 512], in1=pA[3][:, :], op=ALU.add),
                              reads=['pA3', 'hst%d' % gi], writes=['hst%d' % gi])
                        kb.op('act', lambda e: e.activation(out=hst_b[:, gi * 512:(gi + 1) * 512], in_=hst[:, gi * 512:(gi + 1) * 512], func=AF.Copy),
                              reads=['hst%d' % gi], writes=['hstb%d' % gi])
                    wtz, wkz = wget(ssd_w_in[:, gi * 512:(gi + 1) * 512], KC, 512)
                    proj_tok(groups, hTm, lambda g: [hkey(g)], KC, wtz, wkz, 512)
                    for g in groups:
                        n = g.n; sl = g.slot
                        pt_, pk = pacc(g)
                        kb.op('act', lambda e, pt_=pt_, n=n: e.activation(out=szb[0:n, :], in_=pt_[0:n, :], func=AF.Silu), reads=[pk], writes=['szb'])
                        kb.op('dve', lambda e, n=n, sl=sl: e.tensor_tensor(out=gz[0:n, :], in0=yg[0:n, sl, :], in1=szb[0:n, :], op=ALU.mult),
                              reads=['szb', 'yg%d' % sl], writes=['gz'])
                        kb.op('act', lambda e, n=n, sl=sl: e.activation(out=junk[0:n, 0:512], in_=gz[0:n, :], func=AF.Square, accum_out=ssg[0:n, sl, gi:gi + 1]),
                              reads=['gz'], writes=['junk', 'ssg%d' % sl])
                        transpose_to(gz[0:n, :], n, 4, yT[:, sl, gi * 4:(gi + 1) * 4, 0:n], ['gz'], ['yT%d' % sl], mul=nwT[:, gi * 4:(gi + 1) * 4])
                for g in groups:
                    n = g.n; sl = g.slot
                    kb.op('dve', lambda e, n=n, sl=sl: e.tensor_reduce(out=rsg[0:n, sl, 1:2], in_=ssg[0:n, sl, :], axis=AX.X, op=ALU.add),
                          reads=['ssg%d' % sl], writes=['rsg%d' % sl])
                    rstd_from(rsg[0:n, sl, 1:2], n, DI, rsg[0:n, sl, 0:1], ['rsg%d' % sl], ['rsg%d' % sl])
                out_proj(layer_pos, groups, ssd_w_out, 32, yT_l, lambda g: ['yT%d' % g.slot],
                         rs_fn=lambda g: (rsg[0:g.n, g.slot, 0:1], 'rsg%d' % g.slot))
                if pkind == 's':
                    state_io(False, s_ssm)
                elif ti == NTILE - 1:
                    state_io(False, p_ssm)
            kb.barrier()

    def layer_pool(layer_pos):
        with contextlib.ExitStack() as ls:
            def lsb(name, shape, dt=F32):
                return ls.enter_context(nc.sbuf_tensor(name, list(shape), dt))
            yT = lsb("o_yT", [128, 4, 32, 128], BF16); yTs = lsb("o_yTs", [128, 32, 2], BF16)
            dT = lsb("o_dT", [128, 4, 8, 128], BF16); dTs = lsb("o_dTs", [128, 8, 2], BF16)
            pcur = lsb("o_pcur", [128, NSLOT, 512], BF16); pprev = lsb("o_pprev", [128, DI], BF16)
            szb = lsb("o_szb", [128, NSLOT, 512], BF16); ymid = lsb("o_ymid", [128, 512], BF16)
            pf32 = lsb("o_pf32", [128, 512])
            hist_f = lsb("o_histf", [15, DI]); hist_b = lsb("o_histb", [15, DI], BF16)
            cpb = lsb("o_cpb", [128, 12, 128], BF16); cps_f = lsb("o_cpsf", [16, 2, 4]); cps_b = lsb("o_cpsb", [16, 2, 4], BF16)
            sc_tm = lsb("o_sctm", [32, 128]); scT = lsb("o_scT", [128, 32])

            load_row_bcast(nw[:, :], norm_w[1, :], 'nw')
            kb.dma('pool', cpb[:, :, :], c_pool.rearrange("a g s t -> s (a g) t"), writes=['cpb'])
            kb.dma('sp', cps_f[:, :, :], c_pool_s.rearrange("a i g -> i a g"), writes=['cps_f'])
            kb.op('dve', lambda e: e.tensor_copy(out=cps_b[:, :, :], in_=cps_f[:, :, :]), reads=['cps_f'], writes=['cps_b'])
            kb.dma('sp', hist_f[:, :], st_pool[:, :], writes=['hist_f'])
            kb.op('dve', lambda e: e.tensor_copy(out=hist_b[:, :], in_=hist_f[:, :]), reads=['hist_f'], writes=['hist_b'])
            kb.dma('sp', s_pool[0:14, :], st_pool[1:15, :], reads=[], writes=['s_pool_dd'], sem='s_pool_dd')
            kb.dma('sp', sc_tm[:, :], pool_scale[0, :].rearrange("(k p) -> k p", p=128), writes=['sc_tm'])
            kb.op('pe', lambda e: e.transpose(out=pF[:, 0:32], in_=sc_tm[:, :], identity=ident_f[0:32, 0:32]),
                  reads=['sc_tm', 'ident_f'], writes=['pF'])
            kb.op('dve', lambda e: e.tensor_copy(out=scT[:, :], in_=pF[:, 0:32]), reads=['pF'], writes=['scT'])

            def yT_l(g, k):
                return yT[:, g.slot, k, :] if g.kind == 'p' else yTs[:, k, 0:1]

            def dT_l(g, k):
                return dT[:, g.slot, k, :] if g.kind == 'p' else dTs[:, k, 0:1]

            for ti in range(NTILE):
                groups = [Grp('p', ti * NT + j, 128, j) for j in range(NT)]
                if ti == 0:
                    groups.append(Grp('s', 0, 1, 4))
                norm_and_transpose(1, layer_pos, groups)
                for gi in range(4):
                    for bb in range(2):
                        b = gi * 2 + bb
                        wt, wk = wget(pool_w_in[:, b * 512:(b + 1) * 512], KC, 512)
                        proj_tok(groups, hT_l, lambda g: [hkey(g)], KC, wt, wk, 512)
                        for g in groups:
                            pt_, pk = pacc(g)
                            is_out = (g.kind == 's') or (g.idx == NCH - 1)
                            kb.op('act', lambda e, g=g, pt_=pt_: e.activation(out=pcur[0:g.n, g.slot, :], in_=pt_[0:g.n, :], func=AF.Copy),
                                  reads=[pk], writes=['pcur%d' % g.slot])
                            if is_out:
                                kb.op('dve', lambda e, g=g, pt_=pt_: e.tensor_copy(out=pf32[0:g.n, :], in_=pt_[0:g.n, :]),
                                      reads=[pk], writes=['pf32'])
                                if g.kind == 's':
                                    kb.dma('sp', s_pool[14:15, b * 512:(b + 1) * 512], pf32[0:1, :], reads=['pf32'])
                                else:
                                    kb.dma('sp', p_pool[0:15, b * 512:(b + 1) * 512], pf32[113:128, :], reads=['pf32'])
                        for g in groups:
                            n = g.n
                            for i in range(4):
                                cc = bb * 4 + i
                                fns = []
                                if g.kind == 's':
                                    fns.append(lambda e, i=i: e.matmul(pF[:, 0:1], lhsT=hist_b[0:15, b * 512 + i * 128:b * 512 + (i + 1) * 128],
                                                                     rhs=cps_b[0:15, 0, gi:gi + 1], start=True, stop=False))
                                    fns.append(lambda e, i=i: e.matmul(pF[:, 0:1], lhsT=pcur[0:1, 4, i * 128:(i + 1) * 128],
                                                                     rhs=cps_b[0:1, 1, gi:gi + 1], start=False, stop=True))
                                    rkeys = ['hist_b', 'cps_b', 'pcur4']
                                elif g.idx == 0:
                                    fns.append(lambda e, i=i, g=g: e.matmul(pF[:, 0:128], lhsT=pcur[:, g.slot, i * 128:(i + 1) * 128],
                                                                          rhs=cpb[:, 8 + gi, :], start=True, stop=True))
                                    rkeys = ['cpb', 'pcur%d' % g.slot]
                                else:
                                    if g.slot == 0:
                                        prev = pprev[:, b * 512 + i * 128:b * 512 + (i + 1) * 128]; pkey = 'pprev%d' % b
                                    else:
                                        prev = pcur[:, g.slot - 1, i * 128:(i + 1) * 128]; pkey = 'pcur%d' % (g.slot - 1)
                                    fns.append(lambda e, i=i, g=g: e.matmul(pF[:, 0:128], lhsT=pcur[:, g.slot, i * 128:(i + 1) * 128],
                                                                          rhs=cpb[:, 0 + gi, :], start=True, stop=False))
                                    fns.append(lambda e, prev=prev: e.matmul(pF[:, 0:128], lhsT=prev, rhs=cpb[:, 4 + gi, :], start=False, stop=True))
                                    rkeys = ['cpb', 'pcur%d' % g.slot, pkey]
                                kb.op('pe', fns, reads=rkeys, writes=['pF'])
                                dst = dT[:, g.slot, cc, :] if g.kind == 'p' else dTs[:, cc, 0:1]
                                kb.op('act', lambda e, dst=dst, n=n: e.activation(out=dst, in_=pF[:, 0:n], func=AF.Copy),
                                      reads=['pF'], writes=['dT%d' % g.slot])
                        kb.op('dve', lambda e, b=b: e.tensor_copy(out=pprev[:, b * 512:(b + 1) * 512], in_=pcur[:, 3, :]),
                              reads=['pcur3'], writes=['pprev%d' % b])
                    for dbb in range(2):
                        db = gi * 2 + dbb
                        wtz, wkz = wget(pool_w_in[:, DI + db * 512:DI + (db + 1) * 512], KC, 512)
                        proj_tok(groups, hT_l, lambda g: [hkey(g)], KC, wtz, wkz, 512)
                        for g in groups:
                            pt_, pk = pacc(g)
                            kb.op('act', lambda e, g=g, pt_=pt_: e.activation(out=szb[0:g.n, g.slot, :], in_=pt_[0:g.n, :], func=AF.Silu),
                                  reads=[pk], writes=['szb%d' % g.slot])
                        wtg, wkg = wget(pool_w_grp[gi, :, dbb * 512:(dbb + 1) * 512], 8, 512)
                        proj_tok(groups, dT_l, lambda g: ['dT%d' % g.slot], 8, wtg, wkg, 512)
                        for g in groups:
                            n = g.n
                            pt_, pk = pacc(g)
                            kb.op('dve', lambda e, g=g, pt_=pt_: e.tensor_tensor(out=ymid[0:g.n, :], in0=pt_[0:g.n, :], in1=szb[0:g.n, g.slot, :], op=ALU.mult),
                                  reads=[pk, 'szb%d' % g.slot], writes=['ymid'])
                            dst = yT[:, g.slot, db * 4:(db + 1) * 4, :] if g.kind == 'p' else yTs[:, db * 4:(db + 1) * 4, 0:1]
                            transpose_to(ymid[0:n, :], n, 4, dst, ['ymid'], ['yT%d' % g.slot], mul=scT[:, db * 4:(db + 1) * 4])
                out_proj(layer_pos, groups, pool_w_out, 32, yT_l, lambda g: ['yT%d' % g.slot])
            kb.barrier()

    def layer_fox(layer_pos):
        SCALE = 128.0 ** -0.5
        with contextlib.ExitStack() as ls:
            def lsb(name, shape, dt=F32):
                return ls.enter_context(nc.sbuf_tensor(name, list(shape), dt))
            yT = lsb("f_yT", [128, 4, 16, 128], BF16); yTs = lsb("f_yTs", [128, 16, 2], BF16)
            qT = lsb("f_qT", [128, 16, 512], BF16); qTs = lsb("f_qTs", [128, 16, 2], BF16)
            szb = lsb("f_szb", [128, NSLOT, D], BF16)
            kst = lsb("f_kst", [128, 4, 512], BF16)
            sq = lsb("f_sq", [128, 512]); nrm = lsb("f_nrm", [128, 512]); nrb = lsb("f_nrb", [128, 512], BF16)
            ss4 = lsb("f_ss4", [128, 8])
            qnw = lsb("f_qnw", [128, 128]); knw = lsb("f_knw", [128, 128]); bfr = lsb("f_bfr", [128, 16])
            Fall = lsb("f_Fall", [128, NCH, 16]); negF = lsb("f_negF", [128, NCH, 16])
            lf = lsb("f_lf", [128, NSLOT, 16]); t16 = lsb("f_t16", [128, 16]); t16b = lsb("f_t16b", [128, 16])
            qb = lsb("f_qb", [128, D]); enew = lsb("f_enew", [1, 16])
            ustr = lsb("f_ustr", [128, 128]); tri_b = lsb("f_trib", [128, 128], BF16)
            selL = lsb("f_selL", [128, 128])

            load_row_bcast(nw[:, :], norm_w[2, :], 'nw')
            load_row_bcast(qnw[:, :], fox_q_norm[0, :], 'qnw')
            load_row_bcast(knw[:, :], fox_k_norm[0, :], 'knw')
            load_row_bcast(bfr[:, :], fox_b_f[0, :], 'bfr')
            kb.dma('sp', selL[:, :], c_sel[:, :], writes=['selL'])
            kb.op('dve', lambda e: e.tensor_scalar(out=ustr[:, :], in0=tri_f[:, :], scalar1=-1.0, scalar2=1.0, op0=ALU.mult, op1=ALU.add),
                  reads=['tri_f'], writes=['ustr'])
            kb.op('dve', lambda e: e.tensor_copy(out=tri_b[:, :], in_=tri_f[:, :]), reads=['tri_f'], writes=['tri_b'])

            def yT_l(g, k):
                return yT[:, g.slot, k, :] if g.kind == 'p' else yTs[:, k, 0:1]

            def qk_block(g, pt_, pk, nwt, nwk):
                n = g.n
                kb.op('act', lambda e: e.activation(out=sq[0:n, :], in_=pt_[0:n, :], func=AF.Square), reads=[pk], writes=['sq'])
                kb.op('dve', lambda e: e.tensor_reduce(out=ss4[0:n, 0:4], in_=sq[0:n, :].rearrange("p (h d) -> p h d", d=128), axis=AX.X, op=ALU.add),
                      reads=['sq'], writes=['ss4'])
                rstd_from(ss4[0:n, 0:4], n, 128, ss4[0:n, 4:8], ['ss4'], ['ss4'])
                kb.op('dve', lambda e: e.tensor_tensor(out=nrm[0:n, :].rearrange("p (h d) -> p h d", d=128), in0=pt_[0:n, :].rearrange("p (h d) -> p h d", d=128),
                                                      in1=ss4[0:n, 4:8].unsqueeze(2).broadcast_to([n, 4, 128]), op=ALU.mult),
                      reads=[pk, 'ss4'], writes=['nrm'])
                kb.op('dve', lambda e: e.tensor_tensor(out=nrm[0:n, :].rearrange("p (h d) -> p h d", d=128), in0=nrm[0:n, :].rearrange("p (h d) -> p h d", d=128),
                                                      in1=nwt[0:n, :].unsqueeze(1).broadcast_to([n, 4, 128]), op=ALU.mult),
                      reads=['nrm', nwk], writes=['nrm'])

            for ti in range(NTILE):
                groups = [Grp('p', ti * NT + j, 128, j) for j in range(NT)]
                if ti == 0:
                    groups.append(Grp('s', 0, 1, 4))
                norm_and_transpose(2, layer_pos, groups)
                wt, wk = wget(fox_w_in[:, 8192:8208], KC, 16)
                proj_tok(groups, hT_l, lambda g: [hkey(g)], KC, wt, wk, 16)
                for g in groups:
                    n = g.n; sl = g.slot
                    pt_, pk = pacc(g)
                    kb.op('dve', lambda e: e.tensor_tensor(out=t16[0:n, :], in0=pt_[0:n, 0:16], in1=bfr[0:n, :], op=ALU.add), reads=[pk, 'bfr'], writes=['t16'])
                    kb.op('act', lambda e: e.activation(out=t16b[0:n, :], in_=t16[0:n, :], func=AF.Abs), reads=['t16'], writes=['t16b'])
                    kb.op('act', lambda e: e.activation(out=t16b[0:n, :], in_=t16b[0:n, :], func=AF.Exp, scale=-1.0), reads=['t16b'], writes=['t16b'])
                    kb.op('dve', lambda e: e.tensor_scalar(out=t16b[0:n, :], in0=t16b[0:n, :], scalar1=1.0, scalar2=None, op0=ALU.add), reads=['t16b'], writes=['t16b'])
                    kb.op('act', lambda e: e.activation(out=t16b[0:n, :], in_=t16b[0:n, :], func=AF.Ln), reads=['t16b'], writes=['t16b'])
                    kb.op('dve', lambda e: e.scalar_tensor_tensor(out=lf[0:n, sl, :], in0=t16[0:n, :], scalar=0.0, in1=t16b[0:n, :], op0=ALU.min, op1=ALU.subtract),
                          reads=['t16', 't16b'], writes=['lf%d' % sl])
                    if g.kind == 's':
                        kb.dma('sp', s_lf[0:1, :], lf[0:1, sl, :], reads=['lf%d' % sl], sem='st_lfs')
                    else:
                        kb.dma('sp', p_lf[g.idx * 128:(g.idx + 1) * 128, :], lf[0:n, sl, :], reads=['lf%d' % sl], sem='st_lf%d' % sl)
                        fns = [lambda e: e.matmul(pF[0:128, 0:16], lhsT=tri_f[:, :], rhs=lf[0:128, sl, :], start=True, stop=(g.idx == 0))]
                        rk = ['tri_f', 'lf%d' % sl]
                        if g.idx > 0:
                            fns.append(lambda e: e.matmul(pF[0:128, 0:16], lhsT=selL[:, :], rhs=Fall[:, g.idx - 1, :], start=False, stop=True))
                            rk += ['selL', 'Fall']
                        kb.op('pe', fns, reads=rk, writes=['pF'])
                        kb.op('dve', lambda e: e.tensor_copy(out=Fall[:, g.idx, :], in_=pF[0:128, 0:16]), reads=['pF'], writes=['Fall'])
                        kb.op('dve', lambda e: e.tensor_scalar(out=negF[:, g.idx, :], in0=Fall[:, g.idx, :], scalar1=-1.0, scalar2=None, op0=ALU.mult),
                              reads=['Fall'], writes=['negF'])
                for b in range(4):
                    wt, wk = wget(fox_w_in[:, b * 512:(b + 1) * 512], KC, 512)
                    proj_tok(groups, hT_l, lambda g: [hkey(g)], KC, wt, wk, 512)
                    for g in groups:
                        n = g.n
                        pt_, pk = pacc(g)
                        qk_block(g, pt_, pk, qnw, 'qnw')
                        kb.op('act', lambda e: e.activation(out=nrb[0:n, :], in_=nrm[0:n, :], func=AF.Copy), reads=['nrm'], writes=['nrb'])
                        if g.kind == 's':
                            kb.op('pe', lambda e: e.matmul(pF[:, :], lhsT=tri_f[0:1, :], rhs=nrm[0:1, :], start=True, stop=True), reads=['tri_f', 'nrm'], writes=['pF'])
                            kb.op('dve', lambda e: e.tensor_copy(out=qb[:, b * 512:(b + 1) * 512], in_=pF[:, :]), reads=['pF'], writes=['qb'])
                            dst = qTs[:, b * 4:(b + 1) * 4, 0:1]
                        else:
                            dst = qT[:, b * 4:(b + 1) * 4, g.slot * 128:(g.slot + 1) * 128]
                        transpose_to(nrb[0:n, :], n, 4, dst, ['nrb'], ['qT'])
                for b in range(4):
                    wt, wk = wget(fox_w_in[:, D + b * 512:D + (b + 1) * 512], KC, 512)
                    proj_tok(groups, hT_l, lambda g: [hkey(g)], KC, wt, wk, 512)
                    for g in groups:
                        n = g.n
                        pt_, pk = pacc(g)
                        qk_block(g, pt_, pk, knw, 'knw')
                        if g.kind == 's':
                            kb.dma('sp', s_k[0:1, b * 512:(b + 1) * 512], nrm[0:1, :], reads=['nrm'], sem='st_nrm')
                            kb.op('dve', lambda e: e.tensor_tensor(out=sq[0:1, :], in0=nrm[0:1, :], in1=qb[0:1, b * 512:(b + 1) * 512], op=ALU.mult), reads=['nrm', 'qb'], writes=['sq'])
                            kb.op('dve', lambda e: e.tensor_reduce(out=enew[0:1, b * 4:(b + 1) * 4], in_=sq[0:1, :].rearrange("p (h d) -> p h d", d=128), axis=AX.X, op=ALU.add),
                                  reads=['sq'], writes=['enew'])
                        else:
                            kb.dma('sp', p_k[g.idx * 128:(g.idx + 1) * 128, b * 512:(b + 1) * 512], nrm[0:n, :], reads=['nrm'], sem='st_nrm')
                            kb.op('act', lambda e: e.activation(out=nrb[0:n, :], in_=nrm[0:n, :], func=AF.Copy), reads=['nrm'], writes=['nrb'])
                            transpose_to(nrb[0:n, :], n, 4, kst[:, :, g.slot * 128:(g.slot + 1) * 128], ['nrb'], ['kst'])
                    for hh in range(4):
                        kb.dma('sp', kT_scr[b * 4 + hh, :, ti * 512:(ti + 1) * 512], kst[:, hh, :], reads=['kst'], writes=['kT_scr'], sem='st_kst')
                for b in range(4):
                    wt, wk = wget(fox_w_in[:, 2 * D + b * 512:2 * D + (b + 1) * 512], KC, 512)
                    proj_tok(groups, hT_l, lambda g: [hkey(g)], KC, wt, wk, 512)
                    for g in groups:
                        n = g.n
                        pt_, pk = pacc(g)
                        kb.op('act', lambda e: e.activation(out=nrm[0:n, :], in_=pt_[0:n, :], func=AF.Copy), reads=[pk], writes=['nrm'])
                        if g.kind == 's':
                            kb.dma('sp', s_v[0:1, b * 512:(b + 1) * 512], nrm[0:1, :], reads=['nrm'], writes=['s_v%d' % b], sem='st_nrm')
                        else:
                            kb.dma('sp', p_v[g.idx * 128:(g.idx + 1) * 128, b * 512:(b + 1) * 512], nrm[0:n, :], reads=['nrm'], sem='st_nrm')
                            kb.op('dve', lambda e: e.tensor_copy(out=nrb[0:n, :], in_=nrm[0:n, :]), reads=['nrm'], writes=['nrb'])
                            kb.dma('sp', v_scr[g.idx * 128:(g.idx + 1) * 128, b * 512:(b + 1) * 512], nrb[0:n, :], reads=['nrb'], writes=['v_scr'], sem='st_nrb')
                for b in range(4):
                    wt, wk = wget(fox_w_in[:, 3 * D + b * 512:3 * D + (b + 1) * 512], KC, 512)
                    proj_tok(groups, hT_l, lambda g: [hkey(g)], KC, wt, wk, 512)
                    for g in groups:
                        pt_, pk = pacc(g)
                        kb.op('act', lambda e, g=g, pt_=pt_: e.activation(out=szb[0:g.n, g.slot, b * 512:(b + 1) * 512], in_=pt_[0:g.n, :], func=AF.Silu),
                              reads=[pk], writes=['szb%d' % g.slot])

                if ti == 0:
                    with contextlib.ExitStack() as ss_:
                        def asb(name, shape, dt=F32):
                            return ss_.enter_context(nc.sbuf_tensor(name, list(shape), dt))
                        tails = asb("a_tails", [128, 16, NPG]); sc = asb("a_sc", [128, NPG, 16])
                        kp = asb("a_kp", [128, D])
                        idx_i = asb("a_idxi", [128, NPG], I32); idx_f = asb("a_idxf", [128, NPG]); ptb = asb("a_ptb", [128, NPG], I32)
                        iot = asb("a_iot", [128, 1]); ptc = asb("a_ptc", [128, 1], I32)
                        psm = asb("a_psm", [128, 16]); tpg = asb("a_tpg", [128, 16])
                        lfnb = asb("a_lfnb", [128, 16]); esum = asb("a_esum", [128, 16]); den = asb("a_den", [16, 2])
                        ones = asb("a_ones", [128, 2]); osc = qb[0:16, :]
                        lpT = asb("a_lpT", [128, NPG]); orb = junk
                        kb.op('dve', lambda e: e.memset(ones[:, :], 1.0), writes=['ones'])
                        kb.dma('sp', iot[:, :], c_iota[:, :], writes=['iot'])
                        kb.dma('sp', ptb[:, :], pt[0, :].partition_broadcast(128), writes=['ptb'])
                        kb.dma('sp', ptc[0:NPG, :], pt.rearrange("o n -> n o"), writes=['ptc'], allow_slow_non_contiguous=True)
                        kb.op('dve', lambda e: e.tensor_copy(out=idx_f[:, :], in_=ptb[:, :]), reads=['ptb'], writes=['idx_f'])
                        kb.op('dve', lambda e: e.tensor_scalar(out=idx_f[:, :], in0=idx_f[:, :], scalar1=128.0, scalar2=iot[:, 0:1], op0=ALU.mult, op1=ALU.add),
                              reads=['idx_f', 'iot'], writes=['idx_f'])
                        kb.op('dve', lambda e: e.tensor_copy(out=idx_i[:, :], in_=idx_f[:, :]), reads=['idx_f'], writes=['idx_i'])
                        with contextlib.ExitStack() as s2:
                            lfp = s2.enter_context(nc.sbuf_tensor("a_lfp", [128, 128, 16], F32))
                            kb.op('pool', lambda e: e.indirect_dma_start(out=lfp[0:NPG, :, :].rearrange("p t h -> p (t h)"), out_offset=None, in_=clf[:, :],
                                                                        in_offset=bass.IndirectOffsetOnAxis(ap=ptc[0:NPG, 0:1], axis=0)),
                                  reads=['ptc'], writes=['lfp'], dma='lfp')
                            kb.op('dve', lambda e: e.tensor_reduce(out=psm[0:NPG, :], in_=lfp[0:NPG, :, :].rearrange("p t h -> p h t"), axis=AX.X, op=ALU.add),
                                  reads=['lfp'], writes=['psm'])
                            kb.op('pe', lambda e: e.matmul(pF[0:NPG, 0:16], lhsT=ustr[0:NPG, 0:NPG], rhs=psm[0:NPG, :], start=True, stop=True),
                                  reads=['ustr', 'psm'], writes=['pF'])
                            kb.op('dve', lambda e: e.tensor_copy(out=tpg[0:NPG, :], in_=pF[0:NPG, 0:16]), reads=['pF'], writes=['tpg'])
                            for h in range(16):
                                kb.op('pe', lambda e, h=h: e.transpose(out=pF[:, 0:NPG], in_=lfp[0:NPG, :, h], identity=ident_f[0:NPG, 0:NPG]),
                                      reads=['lfp', 'ident_f'], writes=['pF'])
                                kb.op('dve', lambda e: e.tensor_copy(out=lpT[:, :], in_=pF[:, 0:NPG]), reads=['pF'], writes=['lpT'])
                                kb.op('pe', [lambda e: e.matmul(pS[:, 0:NPG], lhsT=ustr[:, :], rhs=lpT[:, :], start=True, stop=False),
                                             lambda e, h=h: e.matmul(pS[:, 0:NPG], lhsT=tpg[0:NPG, h:h + 1].broadcast_to([NPG, 128]), rhs=ident_f[0:NPG, 0:NPG],
                                                                     start=False, stop=True)],
                                      reads=['ustr', 'lpT', 'tpg', 'ident_f'], writes=['pS'])
                                kb.op('dve', lambda e, h=h: e.tensor_copy(out=tails[:, h, :], in_=pS[:, 0:NPG]), reads=['pS'], writes=['tails'])
                            kb.barrier()
                        kb.op('pe', lambda e: e.matmul(pF[:, 0:16], lhsT=tri_f[0:1, :], rhs=lf[0:1, 4, :], start=True, stop=True), reads=['tri_f', 'lf4'], writes=['pF'])
                        kb.op('dve', lambda e: e.tensor_copy(out=lfnb[:, :], in_=pF[:, 0:16]), reads=['pF'], writes=['lfnb'])
                        for j in range(NPG):
                            kb.op('pool', lambda e, j=j: e.indirect_dma_start(out=kp[:, :], out_offset=None, in_=ck[:, :],
                                                                             in_offset=bass.IndirectOffsetOnAxis(ap=idx_i[:, j:j + 1], axis=0)),
                                  reads=['idx_i'], writes=['kp'], dma='kp')
                            kb.op('dve', lambda e: e.tensor_tensor(out=kp[:, :], in0=kp[:, :], in1=qb[:, :], op=ALU.mult), reads=['kp', 'qb'], writes=['kp'])
                            kb.op('dve', lambda e, j=j: e.tensor_reduce(out=sc[:, j, :], in_=kp[:, :].rearrange("p (h d) -> p h d", d=128), axis=AX.X, op=ALU.add),
                                  reads=['kp'], writes=['sc'])
                        kb.op('dve', lambda e: e.scalar_tensor_tensor(out=sc[:, :, :], in0=sc[:, :, :], scalar=SCALE, in1=tails[:, :, :].rearrange("p h j -> p j h"),
                                                                      op0=ALU.mult, op1=ALU.add), reads=['sc', 'tails'], writes=['sc'])
                        kb.op('dve', lambda e: e.tensor_tensor(out=sc[:, :, :], in0=sc[:, :, :], in1=lfnb[:, :].unsqueeze(1).broadcast_to([128, NPG, 16]), op=ALU.add),
                              reads=['sc', 'lfnb'], writes=['sc'])
                        kb.op('act', lambda e: e.activation(out=sc[:, :, :], in_=sc[:, :, :], func=AF.Exp), reads=['sc'], writes=['sc'])
                        kb.op('dve', lambda e: e.tensor_reduce(out=esum[:, :], in_=sc[:, :, :].rearrange("p j h -> p h j"), axis=AX.X, op=ALU.add),
                              reads=['sc'], writes=['esum'])
                        kb.op('act', lambda e: e.activation(out=enew[0:1, :], in_=enew[0:1, :], func=AF.Exp, scale=SCALE), reads=['enew'], writes=['enew'])
                        kb.op('pe', [lambda e: e.matmul(pF[0:16, 0:1], lhsT=esum[:, :], rhs=ones[:, 0:1], start=True, stop=False),
                                     lambda e: e.matmul(pF[0:16, 0:1], lhsT=enew[0:1, :], rhs=ones[0:1, 0:1], start=False, stop=True)],
                              reads=['esum', 'enew', 'ones'], writes=['pF'])
                        kb.op('dve', lambda e: e.reciprocal(out=den[:, 0:1], in_=pF[0:16, 0:1]), reads=['pF'], writes=['den'])
                        for j in range(NPG):
                            kb.op('pool', lambda e, j=j: e.indirect_dma_start(out=kp[:, :], out_offset=None, in_=cv[:, :],
                                                                             in_offset=bass.IndirectOffsetOnAxis(ap=idx_i[:, j:j + 1], axis=0)),
                                  reads=['idx_i'], writes=['kp'], dma='kp')
                            for b in range(4):
                                kb.op('pe', lambda e, j=j, b=b: e.matmul(pA[b][0:16, :], lhsT=sc[:, j, :], rhs=kp[:, b * 512:(b + 1) * 512], start=(j == 0), stop=False),
                                      reads=['sc', 'kp'], writes=['pA%d' % b])
                        kb.dma('sp', kp[0:1, :], s_v[0:1, :], reads=['s_v%d' % b for b in range(4)], writes=['kp'], sem='kp_hw')
                        for b in range(4):
                            kb.op('pe', lambda e, b=b: e.matmul(pA[b][0:16, :], lhsT=enew[0:1, :], rhs=kp[0:1, b * 512:(b + 1) * 512], start=False, stop=True),
                                  reads=['enew', 'kp'], writes=['pA%d' % b])
                            kb.op('dve', lambda e, b=b: e.tensor_scalar(out=osc[:, b * 512:(b + 1) * 512], in0=pA[b][0:16, :], scalar1=den[:, 0:1], scalar2=None, op0=ALU.mult),
                                  reads=['pA%d' % b, 'den'], writes=['qb'])
                        kb.op('dve', lambda e: e.tensor_tensor(out=osc.rearrange("p (h d) -> p h d", d=128), in0=osc.rearrange("p (h d) -> p h d", d=128),
                                                              in1=ident_f[0:16, 0:16].unsqueeze(2).broadcast_to([16, 16, 128]), op=ALU.mult),
                              reads=['qb', 'ident_f'], writes=['qb'])
                        for b in range(4):
                            kb.op('pe', lambda e, b=b: e.matmul(pF[0:1, :], lhsT=ones[0:16, 0:1], rhs=osc[:, b * 512:(b + 1) * 512], start=True, stop=True),
                                  reads=['ones', 'qb'], writes=['pF'])
                            kb.op('dve', lambda e, b=b: e.tensor_tensor(out=orb[0:1, b * 512:(b + 1) * 512], in0=pF[0:1, :], in1=szb[0:1, 4, b * 512:(b + 1) * 512], op=ALU.mult),
                                  reads=['pF', 'szb4'], writes=['junk'])
                        for half in range(2):
                            transpose_to(orb[0:1, half * 1024:(half + 1) * 1024], 1, 8, yTs[:, half * 8:(half + 1) * 8, 0:1], ['junk'], ['yT4'])
                        kb.barrier()

                with contextlib.ExitStack() as ps_:
                    def asb(name, shape, dt=F32):
                        return ps_.enter_context(nc.sbuf_tensor(name + "_%d" % ti, list(shape), dt))
                    attn = asb("b_attn", [128, 4, D], BF16)
                    kTh = asb("b_kTh", [128, S], BF16); Vh = asb("b_Vh", [128, NCH, 130], BF16)
                    PT = asb("b_PT", [128, 512], BF16); rec = asb("b_rec", [128, 4])
                    kb.op('dve', lambda e: e.memset(Vh[:, :, 128:130], 1.0), writes=['Vh'])
                    nk = (ti + 1) * 4
                    for h in range(16):
                        kb.dma('sp', kTh[:, 0:nk * 128], kT_scr[h, :, 0:nk * 128], reads=['kT_scr'], writes=['kTh'])
                        kb.dma('sp', Vh[:, 0:nk, 0:128], v_scr[0:nk * 128, h * 128:(h + 1) * 128].rearrange("(c t) d -> t c d", t=128), reads=['v_scr'], writes=['Vh'])
                        for j in range(nk):
                            jj = j - ti * 4
                            q0 = max(jj, 0)
                            c0 = q0 * 128
                            kb.op('pe', lambda e: e.matmul(pS[:, c0:512], lhsT=kTh[:, j * 128:(j + 1) * 128], rhs=qT[:, h, c0:512], start=True, stop=True),
                                  reads=['kTh', 'qT'], writes=['pS'])
                            kb.op('act', lambda e: e.activation(out=PT[:, c0:512], in_=pS[:, c0:512], func=AF.Exp, bias=negF[:, j, h:h + 1], scale=SCALE),
                                  reads=['pS', 'negF'], writes=['PT'])
                            if jj >= 0:
                                kb.op('dve', lambda e: e.tensor_tensor(out=PT[:, c0:c0 + 128], in0=PT[:, c0:c0 + 128], in1=tri_b[:, :], op=ALU.mult),
                                      reads=['PT', 'tri_b'], writes=['PT'])
                            for q in range(q0, 4):
                                kb.op('pe', lambda e, q=q: e.matmul(pA[q][:, 0:129], lhsT=PT[:, q * 128:(q + 1) * 128], rhs=Vh[:, j, 0:129],
                                                                   start=(j == 0), stop=(j == ti * 4 + q)),
                                      reads=['PT', 'Vh'], writes=['pA%d' % q])
                        for q in range(4):
                            kb.op('dve', lambda e, q=q: e.reciprocal(out=rec[:, q:q + 1], in_=pA[q][:, 128:129]), reads=['pA%d' % q], writes=['rec'])
                            kb.op('dve', lambda e, q=q: e.tensor_scalar(out=attn[:, q, h * 128:(h + 1) * 128], in0=pA[q][:, 0:128], scalar1=rec[:, q:q + 1], scalar2=None, op0=ALU.mult),
                                  reads=['pA%d' % q, 'rec'], writes=['attn%d' % q])
                    for q in range(4):
                        kb.op('dve', lambda e, q=q: e.tensor_tensor(out=attn[:, q, :], in0=attn[:, q, :], in1=szb[:, q, :], op=ALU.mult),
                              reads=['attn%d' % q, 'szb%d' % q], writes=['attn%d' % q])
                        for half in range(2):
                            transpose_to(attn[:, q, half * 1024:(half + 1) * 1024], 128, 8, yT[:, q, half * 8:(half + 1) * 8, :], ['attn%d' % q], ['yT%d' % q])
                    kb.barrier()
                out_proj(layer_pos, groups, fox_w_out, 16, yT_l, lambda g: ['yT%d' % g.slot])
            kb.barrier()

    fns = {0: layer_ssd, 1: layer_pool, 2: layer_fox, 3: layer_sgu}
    try:
        for lp, L in enumerate(layers):
            fns[L](lp)
    except StopBuild:
        print('DBG: stopped early')
    STOPPED[0] = False
    kb.dma('sp', ys[:, :], xs_res[:, :], reads=['xs_res'])
    kb.finish()
    stack.close()
    print("program built: %d instructions, %d semaphores" % (kb.ninst, len(kb.semh)), flush=True)
    return nc


LAYERS = (0, 1, 2, 3)
POOL_WINDOWS = (2, 4, 8, 16)


def make_consts():
    ident = np.eye(128, dtype=np.float32)
    s = np.arange(128)[:, None]; t = np.arange(128)[None, :]
    tri = (s <= t).astype(np.float32)
    cp = np.zeros((3, 4, 128, 128), np.float32)
    cps = np.zeros((2, 16, 4), np.float32)
    for gi, w in enumerate(POOL_WINDOWS):
        band = ((s <= t) & (s >= t - w + 1)).astype(np.float32)
        cp[0, gi] = band / w - ident
        cp[1, gi] = ((s - 128) >= (t - w + 1)).astype(np.float32) / w
        cp[2, gi] = band / np.minimum(t + 1, w) - ident
        for i in range(15):
            if i >= 16 - w:
                cps[0, i, gi] = 1.0 / w
        cps[1, 0, gi] = 1.0 / w - 1.0
    sel = np.zeros((128, 128), np.float32); sel[127, :] = 1.0
    return {'c_ident': ident, 'c_tri': tri, 'c_pool': cp, 'c_pool_s': cps, 'c_sel': sel, 'c_iota': np.arange(128, dtype=np.float32).reshape(128, 1)}


def kernel(**inp):
    f = lambda a: np.ascontiguousarray(np.asarray(a), dtype=np.float32)
    S = inp['x_prompt'].shape[1]
    NPG = inp['page_table'].shape[1]
    NPHYS = inp['cache_fox_k'].shape[1]
    nc = build_program(S, NPG, NPHYS, LAYERS)
    consts = make_consts()
    shared = {
        'ck': f(inp['cache_fox_k'][0]).reshape(NPHYS * 128, 2048), 'cv': f(inp['cache_fox_v'][0]).reshape(NPHYS * 128, 2048),
        'clf': f(inp['cache_fox_logf'][0]).reshape(NPHYS, 2048),
        'norm_w': f(inp['norm_w']),
        'ssd_w_in': f(inp['ssd_w_in'][0]), 'ssd_conv_w': f(inp['ssd_conv_w'][0]), 'ssd_conv_b': f(inp['ssd_conv_b']),
        'ssd_dt_bias': f(inp['ssd_dt_bias']), 'ssd_A_log': f(inp['ssd_A_log']), 'ssd_D': f(inp['ssd_D']),
        'ssd_norm_w': f(inp['ssd_norm_w']), 'ssd_w_out': f(inp['ssd_w_out'][0]),
        'pool_w_in': f(inp['pool_w_in'][0]), 'pool_w_grp': f(inp['pool_w_grp'][0]), 'pool_scale': f(inp['pool_scale']),
        'pool_w_out': f(inp['pool_w_out'][0]),
        'fox_w_in': f(inp['fox_w_in'][0]), 'fox_b_f': f(inp['fox_b_f']), 'fox_q_norm': f(inp['fox_q_norm']),
        'fox_k_norm': f(inp['fox_k_norm']), 'fox_w_out': f(inp['fox_w_out'][0]),
        'sgu_w_in': f(inp['sgu_w_in'][0]), 'sgu_v_norm': f(inp['sgu_v_norm']), 'sgu_w_s': f(inp['sgu_w_s'][0]),
        'sgu_b_s': f(inp['sgu_b_s'][0]), 'sgu_w_out': f(inp['sgu_w_out'][0]),
    }
    shared.update(consts)
    in_maps = []
    for c in range(8):
        b = c % 4
        m = dict(shared)
        m['xp'] = f(inp['x_prompt'][b]); m['xs'] = f(inp['x_sample'][c])
        m['st_conv'] = f(inp['state_ssd_conv'][0, c]); m['st_ssm'] = f(inp['state_ssd_ssm'][0, c]).reshape(4096, 128)
        m['st_pool'] = f(inp['state_pool'][0, c])
        m['pt'] = np.ascontiguousarray(np.asarray(inp['page_table'][c:c + 1]), dtype=np.int32)
        in_maps.append(m)
    res = run_bass_kernel_spmd(nc, in_maps, core_ids=list(range(8)))
    r = res.results
    P = lambda k: np.stack([np.asarray(r[b][k]) for b in range(4)])
    Sm = lambda k: np.stack([np.asarray(r[c][k]) for c in range(8)])
    out = (
        P('yp'), Sm('ys'),
        P('p_conv')[None], P('p_ssm').reshape(4, 64, 64, 128)[None], P('p_pool')[None],
        P('p_k').reshape(4, S, 16, 128)[None], P('p_v').reshape(4, S, 16, 128)[None], P('p_lf')[None], P('p_sgu')[None],
        Sm('s_conv')[None], Sm('s_ssm').reshape(8, 64, 64, 128)[None], Sm('s_pool')[None],
        Sm('s_k').reshape(8, 1, 16, 128)[None], Sm('s_v').reshape(8, 1, 16, 128)[None], Sm('s_lf')[None], Sm('s_sgu')[None],
    )
    return tuple(np.ascontiguousarray(o, dtype=np.float32) for o in out)
```

```python
import contextlib
import numpy as np
import concourse.bass as bass
import concourse.mybir as mybir
from concourse.bass_utils import run_bass_kernel_spmd

F32 = mybir.dt.float32
BF16 = mybir.dt.bfloat16
I32 = mybir.dt.int32
AF = mybir.ActivationFunctionType
ALU = mybir.AluOpType
AX = mybir.AxisListType

D = 2048
DI = 4096
KC = 16
EPS = 1e-6
NSLOT = 5
WCOLS = 512
SAME_ENG_SYNC = True


class KB:
    def __init__(self, nc, stack):
        self.nc = nc
        self.stack = stack
        self.eng = {'pe': nc.tensor, 'act': nc.scalar, 'dve': nc.vector, 'pool': nc.gpsimd, 'sp': nc.sync}
        self.semh = {}
        self.val = {}
        for e in self.eng:
            self.semh[e] = stack.enter_context(nc.semaphore("sem_" + e))
            self.val[e] = 0
        self.seen = {e: {} for e in self.eng}
        self.lastw = {}
        self.readers = {}
        self.ninst = 0

    def dsem(self, name):
        sk = 'd:' + name
        if sk not in self.semh:
            self.semh[sk] = self.stack.enter_context(self.nc.semaphore("dsem_" + name))
            self.val[sk] = 0
        return sk

    def _wait(self, e, deps):
        eng = self.eng[e]
        best = {}
        for (sk, v) in deps:
            if v > best.get(sk, 0):
                best[sk] = v
        for sk, v in best.items():
            if sk == e and (e == 'pe' or not SAME_ENG_SYNC):
                continue
            if self.seen[e].get(sk, 0) >= v:
                continue
            eng.wait_ge(self.semh[sk], v)
            self.seen[e][sk] = v

    def op(self, e, fns, reads=(), writes=(), dma=None):
        if not isinstance(fns, (list, tuple)):
            fns = [fns]
        if STOPPED[0]:
            return None
        if self.ninst >= int(os.environ.get("DBG_MAXI", "100000000")):
            STOPPED[0] = True
            print("DBG_MAXI stop before:", e, "reads", list(reads), "writes", list(writes), flush=True)
            return None
        psum_reads = [k for k in reads if str(k).startswith('pA') or str(k).startswith('pB') or str(k) in ('pS', 'pF')]
        if psum_reads:
            writes = list(writes) + [k for k in psum_reads if k not in writes]
        deps = []
        for k in reads:
            if k in self.lastw:
                deps.append(self.lastw[k])
        for k in writes:
            if k in self.lastw:
                deps.append(self.lastw[k])
            for sk, v in self.readers.get(k, {}).items():
                deps.append((sk, v))
        self._wait(e, deps)
        eng = self.eng[e]
        inst = None
        for f in fns:
            inst = f(eng)
            self.ninst += 1
        if dma is not None:
            sk = self.dsem(dma)
            self.val[sk] += 16
            inst.then_inc(self.semh[sk], 16)
        else:
            sk = e
            self.val[sk] += 1
            inst.then_inc(self.semh[sk], 1)
        tok = (sk, self.val[sk])
        for k in writes:
            self.lastw[k] = tok
            self.readers[k] = {}
        for k in reads:
            r = self.readers.setdefault(k, {})
            if tok[1] > r.get(tok[0], 0):
                r[tok[0]] = tok[1]
        return tok

    def dma(self, q, out, in_, reads=(), writes=(), sem=None, **kw):
        if sem is None:
            sem = writes[0] if len(writes) > 0 and not str(writes[0]).startswith('yp') else 'st_' + str(reads[0])
        return self.op(q, lambda eng: eng.dma_start(out=out, in_=in_, **kw), reads, writes, dma=sem)

    def barrier(self):
        if STOPPED[0]:
            return
        for e in self.eng:
            for sk, h in self.semh.items():
                v = self.val[sk]
                if v > 0 and sk != e and self.seen[e].get(sk, 0) < v:
                    self.eng[e].wait_ge(h, v)
                    self.seen[e][sk] = v

    def finish(self):
        for sk, h in self.semh.items():
            if sk.startswith('d:') and self.val[sk] > 0:
                self.nc.sync.wait_ge(h, self.val[sk])
        for e in ('pe', 'act', 'dve', 'pool'):
            if self.val[e] > 0:
                self.nc.sync.wait_ge(self.semh[e], self.val[e])


import os
class StopBuild(Exception):
    pass
STOPPED = [False]
def ckpt(i):
    if int(os.environ.get("DBG_STOP", "99")) == i:
        STOPPED[0] = True


class Grp:
    def __init__(self, kind, idx, n, slot):
        self.kind, self.idx, self.n, self.slot = kind, idx, n, slot


def build_program(S=2048, NPG=128, NPHYS=1280, layers=(0, 1, 2, 3)):
    NCH = S // 128
    NT = 4
    NTILE = NCH // NT
    nc = bass.Bass("TRN2", target_bir_lowering=False)
    stack = contextlib.ExitStack()

    def din(name, shape, dt=F32):
        return nc.dram_tensor(name, list(shape), dt, kind="ExternalInput").ap()

    def dout(name, shape, dt=F32):
        return nc.dram_tensor(name, list(shape), dt, kind="ExternalOutput").ap()

    xp = din("xp", [S, D]); xs = din("xs", [1, D])
    st_conv = din("st_conv", [3, 6144]); st_ssm = din("st_ssm", [DI, 128]); st_pool = din("st_pool", [15, DI])
    ck = din("ck", [NPHYS * 128, 2048]); cv = din("cv", [NPHYS * 128, 2048]); clf = din("clf", [NPHYS, 2048])
    pt = din("pt", [1, NPG], I32)
    norm_w = din("norm_w", [4, D])
    ssd_w_in = din("ssd_w_in", [D, 10304]); ssd_conv_w = din("ssd_conv_w", [4, 6144]); ssd_conv_b = din("ssd_conv_b", [1, 6144])
    ssd_dt_bias = din("ssd_dt_bias", [1, 64]); ssd_A_log = din("ssd_A_log", [1, 64]); ssd_D = din("ssd_D", [1, 64])
    ssd_norm_w = din("ssd_norm_w", [1, DI]); ssd_w_out = din("ssd_w_out", [DI, D])
    pool_w_in = din("pool_w_in", [D, 8192]); pool_w_grp = din("pool_w_grp", [4, 1024, 1024])
    pool_scale = din("pool_scale", [1, DI]); pool_w_out = din("pool_w_out", [DI, D])
    fox_w_in = din("fox_w_in", [D, 8208]); fox_b_f = din("fox_b_f", [1, 16]); fox_q_norm = din("fox_q_norm", [1, 128])
    fox_k_norm = din("fox_k_norm", [1, 128]); fox_w_out = din("fox_w_out", [D, D])
    sgu_w_in = din("sgu_w_in", [D, 12288]); sgu_v_norm = din("sgu_v_norm", [1, DI]); sgu_w_s = din("sgu_w_s", [8, 128, 128])
    sgu_b_s = din("sgu_b_s", [8, 128]); sgu_w_out = din("sgu_w_out", [DI, D])
    c_ident = din("c_ident", [128, 128]); c_tri = din("c_tri", [128, 128])
    c_pool = din("c_pool", [3, 4, 128, 128]); c_pool_s = din("c_pool_s", [2, 16, 4]); c_sel = din("c_sel", [128, 128]); c_iota = din("c_iota", [128, 1])

    yp = dout("yp", [S, D]); ys = dout("ys", [1, D])
    p_conv = dout("p_conv", [3, 6144]); p_ssm = dout("p_ssm", [DI, 128]); p_pool = dout("p_pool", [15, DI])
    p_k = dout("p_k", [S, D]); p_v = dout("p_v", [S, D]); p_lf = dout("p_lf", [S, 16]); p_sgu = dout("p_sgu", [128, DI])
    s_conv = dout("s_conv", [3, 6144]); s_ssm = dout("s_ssm", [DI, 128]); s_pool = dout("s_pool", [15, DI])
    s_k = dout("s_k", [1, D]); s_v = dout("s_v", [1, D]); s_lf = dout("s_lf", [1, 16]); s_sgu = dout("s_sgu", [1, DI])
    kT_scr = nc.dram_tensor("kT_scr", [16, 128, S], BF16, kind="Internal").ap()
    v_scr = nc.dram_tensor("v_scr", [S, D], BF16, kind="Internal").ap()

    kb = KB(nc, stack)
    nsb = [0]

    def sb(name, shape, dt=F32):
        nsb[0] += 1
        return stack.enter_context(nc.sbuf_tensor(name, list(shape), dt))

    def ps(name, shape, dt=F32):
        return stack.enter_context(nc.psum_tensor(name, list(shape), dt))

    pA = [ps("pA%d" % i, [128, 512]) for i in range(4)]
    pS = ps("pS", [128, 512])
    pF = ps("pF", [128, 512])
    pB = [ps("pB%d" % i, [128, 1024], BF16) for i in range(2)]
    pbi = [0]

    ident_f = sb("ident_f", [128, 128]); ident_b = sb("ident_b", [128, 128], BF16)
    tri_f = sb("tri_f", [128, 128])
    hT = sb("hT", [128, 4, KC, 128], BF16); hTs = sb("hTs", [128, KC, 2], BF16)
    xs_res = sb("xs_res", [1, D])
    nw = sb("nw", [128, D])
    xb = [sb("xb%d" % i, [128, D]) for i in range(2)]
    xbi = [0]
    xo = [sb("xo%d" % i, [128, 512]) for i in range(4)]
    xoi = [0]
    NS = 2
    wsl = [sb("wsl%d" % i, [128, KC, WCOLS], BF16) for i in range(NS)]
    junk = sb("junk", [128, D], BF16)
    sm = sb("sm", [128, NSLOT, 8])
    rstd_x = sb("rstd_x", [128, NSLOT, 2])

    kb.dma('sp', ident_f[:], c_ident[:, :], writes=['ident_f'])
    kb.dma('sp', tri_f[:], c_tri[:, :], writes=['tri_f'])
    kb.op('dve', lambda e: e.tensor_copy(out=ident_b[:], in_=ident_f[:]), reads=['ident_f'], writes=['ident_b'])
    kb.dma('sp', xs_res[:], xs[:, :], writes=['xs_res'])

    wstate = {'i': 0}

    def wget(src, kc, ncols):
        i = wstate['i']; wstate['i'] += 1
        t = wsl[i % NS]; key = 'w%d' % (i % NS)
        kb.dma('pool', t[:, 0:kc, 0:ncols], src.rearrange("(k p) n -> p k n", p=128), writes=[key], sem=key)
        return t, key

    def hT_l(g, k):
        if g.kind == 'p':
            return hT[:, g.slot, k, :]
        return hTs[:, k, 0:1]

    def hkey(g):
        return 'hT%d' % g.slot

    def pacc(g):
        return (pA[g.slot], 'pA%d' % g.slot) if g.kind == 'p' else (pS, 'pS')

    def proj_tok(groups, lhs_fn, lhs_keys_fn, kchunks, wt, wkey, ncols, first=True, last=True, koff=0):
        for g in groups:
            pt_, pk = pacc(g)
            fns = []
            for k in range(kchunks):
                fns.append(lambda e, k=k, g=g, pt_=pt_: e.matmul(
                    pt_[:g.n, 0:ncols], lhsT=lhs_fn(g, koff + k), rhs=wt[:, k, 0:ncols],
                    start=(first and k == 0), stop=(last and k == kchunks - 1)))
            kb.op('pe', fns, reads=[wkey] + lhs_keys_fn(g), writes=[pk])

    def next_pb():
        i = pbi[0] % 2; pbi[0] += 1
        return pB[i], 'pB%d' % i

    def transpose_to(src, n, m, dst, skeys, dkeys, ev='act', mul=None):
        pb, pk = next_pb()
        fns = []
        for i in range(m):
            fns.append(lambda e, i=i: e.transpose(out=pb[:, i * 128:i * 128 + n], in_=src[:, i * 128:(i + 1) * 128],
                                                  identity=ident_b[0:n, 0:n]))
        kb.op('pe', fns, reads=list(skeys) + ['ident_b'], writes=[pk])
        pv = pb[:, 0:m * 128].rearrange("p (m t) -> p m t", t=128)[:, :, 0:n]
        if mul is not None:
            kb.op('dve', lambda e: e.tensor_tensor(out=dst, in0=pv, in1=mul.unsqueeze(2).broadcast_to([128, m, n]), op=ALU.mult),
                  reads=[pk, 'scT', 'nwT'], writes=list(dkeys))
        elif ev == 'act':
            kb.op('act', lambda e: e.activation(out=dst, in_=pv, func=AF.Copy), reads=[pk], writes=list(dkeys))
        else:
            kb.op('dve', lambda e: e.tensor_copy(out=dst, in_=pv), reads=[pk], writes=list(dkeys))

    def rstd_from(ss, n, count, out, keys_r, keys_w):
        kb.op('dve', lambda e: e.tensor_scalar(out=out, in0=ss, scalar1=1.0 / count, scalar2=EPS, op0=ALU.mult, op1=ALU.add),
              reads=keys_r, writes=keys_w)
        kb.op('act', lambda e: e.activation(out=out, in_=out, func=AF.Sqrt), reads=keys_w, writes=keys_w)
        kb.op('dve', lambda e: e.reciprocal(out=out, in_=out), reads=keys_w, writes=keys_w)

    def next_xb():
        i = xbi[0] % 2; xbi[0] += 1
        return xb[i], 'xb%d' % i

    def x_src(layer_pos, g):
        if layer_pos == 0:
            return xp[g.idx * 128:(g.idx + 1) * 128, :]
        return yp[g.idx * 128:(g.idx + 1) * 128, :]

    def load_x(layer_pos, g):
        if g.kind == 's':
            return xs_res[0:1, :], 'xs_res'
        t, k = next_xb()
        rk = ['yp%d_%d' % (g.idx, cb) for cb in range(4)] if layer_pos > 0 else []
        kb.dma('sp', t[:, :], x_src(layer_pos, g), reads=rk, writes=[k])
        return t[:, :], k

    def norm_and_transpose(layer, layer_pos, groups):
        for g in groups:
            n = g.n
            xa, xk = load_x(layer_pos, g)
            ss = sm[0:n, g.slot, 0:1]; rs = rstd_x[0:n, g.slot, 0:1]
            kb.op('act', lambda e: e.activation(out=junk[0:n, :], in_=xa, func=AF.Square, accum_out=ss),
                  reads=[xk], writes=['junk', 'sm%d' % g.slot])
            rstd_from(ss, n, D, rs, ['sm%d' % g.slot], ['rstdx%d' % g.slot])
            kb.op('dve', lambda e: e.scalar_tensor_tensor(out=junk[0:n, :], in0=xa, scalar=rs, in1=nw[0:n, :],
                                                          op0=ALU.mult, op1=ALU.mult),
                  reads=[xk, 'rstdx%d' % g.slot, 'nw'], writes=['junk'])
            for half in range(2):
                if g.kind == 'p':
                    dst = hT[:, g.slot, half * 8:(half + 1) * 8, :]
                else:
                    dst = hTs[:, half * 8:(half + 1) * 8, 0:1]
                transpose_to(junk[0:n, half * 1024:(half + 1) * 1024], n, 8, dst, ['junk'], [hkey(g)])

    def out_proj(layer_pos, groups, w_out, kchunks_total, yT_fn, ykeys_fn, rs_fn=None):
        nhalf = (kchunks_total + KC - 1) // KC
        for cb in range(4):
            for kh in range(nhalf):
                kcn = min(KC, kchunks_total - kh * KC)
                wt, wk = wget(w_out[kh * KC * 128:(kh * KC + kcn) * 128, cb * 512:(cb + 1) * 512], kcn, 512)
                proj_tok(groups, yT_fn, ykeys_fn, kcn, wt, wk, 512, first=(kh == 0), last=(kh == nhalf - 1), koff=kh * KC)
            for g in groups:
                pt_, pk = pacc(g)
                if g.kind == 'p':
                    i = xoi[0] % 4; xoi[0] += 1
                    xt = xo[i]; xk = 'xo%d' % i
                    src = (xp if layer_pos == 0 else yp)[g.idx * 128:(g.idx + 1) * 128, cb * 512:(cb + 1) * 512]
                    rk = ['yp%d_%d' % (g.idx, cb)] if layer_pos > 0 else []
                    kb.dma('sp', xt[:, :], src, reads=rk, writes=[xk])
                    dst = xt[:, :]
                else:
                    dst = xs_res[0:1, cb * 512:(cb + 1) * 512]; xk = 'xs_res'
                if rs_fn is None:
                    kb.op('dve', lambda e, dst=dst, pt_=pt_, g=g: e.tensor_tensor(out=dst, in0=pt_[0:g.n, :], in1=dst, op=ALU.add),
                          reads=[pk, xk], writes=[xk])
                else:
                    rs, rk2 = rs_fn(g)
                    kb.op('dve', lambda e, dst=dst, pt_=pt_, g=g, rs=rs: e.scalar_tensor_tensor(
                        out=dst, in0=pt_[0:g.n, :], scalar=rs, in1=dst, op0=ALU.mult, op1=ALU.add),
                        reads=[pk, xk, rk2], writes=[xk])
                if g.kind == 'p':
                    kb.dma('sp', yp[g.idx * 128:(g.idx + 1) * 128, cb * 512:(cb + 1) * 512], dst, reads=[xk],
                           writes=['yp%d_%d' % (g.idx, cb)], sem='st_' + xk)

    def load_row_bcast(dst, src_row, key, n=128):
        kb.dma('sp', dst, src_row.partition_broadcast(n), writes=[key])

    def layer_sgu(layer_pos):
        with contextlib.ExitStack() as ls:
            def lsb(name, shape, dt=F32):
                return ls.enter_context(nc.sbuf_tensor(name, list(shape), dt))
            yT = lsb("g_yT", [128, 4, 32, 128], BF16); yTs = lsb("g_yTs", [128, 32, 2], BF16)
            vbuf = lsb("g_v", [128, NSLOT, DI], BF16)
            vnw = lsb("g_vnw", [128, DI], BF16)
            vnw_f = lsb("g_vnwf", [128, 512])
            wsT = lsb("g_wsT", [128, 8, 128], BF16)
            wtmp = lsb("g_wtmp", [128, 128]); wtmp_b = lsb("g_wtmpb", [128, 128], BF16)
            bsT = lsb("g_bsT", [128, 8]); bs_tm = lsb("g_bstm", [8, 128])
            uall = lsb("g_uall", [128, NSLOT, 512], BF16); szb = lsb("g_sz", [128, 512], BF16); ytm = lsb("g_ytm", [128, 512], BF16)
            vo = lsb("g_vo", [128, 512])
            ssv = lsb("g_ssv", [128, NSLOT, 8]); rsv = lsb("g_rsv", [128, NSLOT, 2])

            load_row_bcast(nw[:, :], norm_w[3, :], 'nw')
            for b in range(8):
                kb.dma('sp', vnw_f[:, :], sgu_v_norm[0, b * 512:(b + 1) * 512].partition_broadcast(128), writes=['vnw_f'])
                kb.op('dve', lambda e, b=b: e.tensor_copy(out=vnw[:, b * 512:(b + 1) * 512], in_=vnw_f[:, :]),
                      reads=['vnw_f'], writes=['vnw'])
            for gi in range(8):
                kb.dma('sp', wtmp[:, :], sgu_w_s[gi, :, :], writes=['wtmp'])
                kb.op('pe', lambda e: e.transpose(out=pF[:, 0:128], in_=wtmp[:, :], identity=ident_f[:, :]),
                      reads=['wtmp', 'ident_f'], writes=['pF'])
                kb.op('dve', lambda e, gi=gi: e.tensor_tensor(out=wsT[:, gi, :], in0=pF[:, 0:128], in1=tri_f[:, :], op=ALU.mult),
                      reads=['pF', 'tri_f'], writes=['wsT'])
            kb.dma('sp', bs_tm[:, :], sgu_b_s[:, :], writes=['bs_tm'])
            kb.op('pe', lambda e: e.transpose(out=pF[:, 0:8], in_=bs_tm[:, :], identity=ident_f[0:8, 0:8]),
                  reads=['bs_tm', 'ident_f'], writes=['pF'])
            kb.op('dve', lambda e: e.tensor_copy(out=bsT[:, :], in_=pF[:, 0:8]), reads=['pF'], writes=['bsT'])

            def yT_l(g, k):
                return yT[:, g.slot, k, :] if g.kind == 'p' else yTs[:, k, 0:1]
            ckpt(1)

            for ti in range(NTILE):
                groups = [Grp('p', ti * NT + j, 128, j) for j in range(NT)]
                if ti == 0 and not os.environ.get("DBG_NOSAMPLE"):
                    groups.append(Grp('s', 0, 1, 4))
                norm_and_transpose(3, layer_pos, groups)
                ckpt(2)
                for b in range(8):
                    wt, wk = wget(sgu_w_in[:, DI + b * 512:DI + (b + 1) * 512], KC, 512)
                    ckpt(5)
                    proj_tok(groups, hT_l, lambda g: [hkey(g)], KC, wt, wk, 512)
                    ckpt(6)
                    for g in groups:
                        pt_, pk = pacc(g)
                        if not os.environ.get("DBG_NOACC"):
                            kb.op('act', lambda e, g=g, pt_=pt_, b=b: e.activation(out=junk[0:g.n, 0:512], in_=pt_[0:g.n, :], func=AF.Square,
                                                                        accum_out=ssv[0:g.n, g.slot, b:b + 1]),
                              reads=[pk], writes=['junk', 'ssv%d' % g.slot])
                        if b == int(os.environ.get("DBG_B", "99")): STOPPED[0] = True
                        if os.environ.get("DBG_VCOPY") == "act":
                            kb.op('act', lambda e, g=g, pt_=pt_, b=b: e.activation(out=vbuf[0:g.n, g.slot, b * 512:(b + 1) * 512], in_=pt_[0:g.n, :], func=AF.Copy),
                                  reads=[pk], writes=['v%d' % g.slot])
                        elif os.environ.get("DBG_VCOPY") == "f32":
                            kb.op('dve', lambda e, g=g, pt_=pt_, b=b: e.tensor_copy(out=vo[0:g.n, :], in_=pt_[0:g.n, :]),
                                  reads=[pk], writes=['vo'])
                        else:
                            kb.op('dve', lambda e, g=g, pt_=pt_, b=b: e.tensor_copy(out=vbuf[0:g.n, g.slot, b * 512:(b + 1) * 512], in_=pt_[0:g.n, :]),
                                  reads=[pk], writes=['v%d' % g.slot])
                for g in groups:
                    n = g.n
                    kb.op('dve', lambda e, g=g: e.tensor_reduce(out=rsv[0:g.n, g.slot, 1:2], in_=ssv[0:g.n, g.slot, :], axis=AX.X, op=ALU.add),
                          reads=['ssv%d' % g.slot], writes=['rsv%d' % g.slot])
                    rstd_from(rsv[0:n, g.slot, 1:2], n, DI, rsv[0:n, g.slot, 0:1], ['rsv%d' % g.slot], ['rsv%d' % g.slot])
                    kb.op('dve', lambda e, g=g: e.scalar_tensor_tensor(out=vbuf[0:g.n, g.slot, :], in0=vbuf[0:g.n, g.slot, :],
                                                                      scalar=rsv[0:g.n, g.slot, 0:1], in1=vnw[0:g.n, :],
                                                                      op0=ALU.mult, op1=ALU.mult),
                          reads=['v%d' % g.slot, 'rsv%d' % g.slot, 'vnw'], writes=['v%d' % g.slot])
                    is_out = (g.kind == 's') or (g.idx == NCH - 1)
                    if is_out:
                        for b in range(8):
                            kb.op('dve', lambda e, g=g, b=b: e.tensor_copy(out=vo[0:g.n, :], in_=vbuf[0:g.n, g.slot, b * 512:(b + 1) * 512]),
                                  reads=['v%d' % g.slot], writes=['vo'])
                            dst = (s_sgu if g.kind == 's' else p_sgu)[0:n, b * 512:(b + 1) * 512]
                            kb.dma('sp', dst, vo[0:n, :], reads=['vo'])
                ckpt(3)
                for gi in range(8):
                    wtu, wku = wget(sgu_w_in[:, gi * 512:(gi + 1) * 512], KC, 512)
                    proj_tok(groups, hT_l, lambda g: [hkey(g)], KC, wtu, wku, 512)
                    for g in groups:
                        pt_, pk = pacc(g)
                        kb.op('act', lambda e, g=g, pt_=pt_: e.activation(out=uall[0:g.n, g.slot, :], in_=pt_[0:g.n, :], func=AF.Copy),
                              reads=[pk], writes=['u%d' % g.slot])
                    wtz, wkz = wget(sgu_w_in[:, 2 * DI + gi * 512:2 * DI + (gi + 1) * 512], KC, 512)
                    proj_tok(groups, hT_l, lambda g: [hkey(g)], KC, wtz, wkz, 512)
                    for g in groups:
                        n = g.n
                        pt_, pk = pacc(g)
                        kb.op('act', lambda e, g=g, pt_=pt_: e.activation(out=szb[0:g.n, :], in_=pt_[0:g.n, :], func=AF.Silu),
                              reads=[pk], writes=['szb'])
                        kb.op('pe', lambda e, g=g, gi=gi: e.matmul(pF[0:g.n, :], lhsT=wsT[0:g.n, gi, 0:g.n],
                                                                 rhs=vbuf[0:g.n, g.slot, gi * 512:(gi + 1) * 512], start=True, stop=True),
                              reads=['wsT', 'v%d' % g.slot], writes=['pF'])
                        kb.op('dve', lambda e, g=g, gi=gi: e.scalar_tensor_tensor(out=ytm[0:g.n, :], in0=pF[0:g.n, :], scalar=bsT[0:g.n, gi:gi + 1],
                                                                           in1=uall[0:g.n, g.slot, :], op0=ALU.add, op1=ALU.mult),
                              reads=['pF', 'bsT', 'u%d' % g.slot], writes=['ytm'])
                        kb.op('dve', lambda e, g=g: e.tensor_tensor(out=ytm[0:g.n, :], in0=ytm[0:g.n, :], in1=szb[0:g.n, :], op=ALU.mult),
                              reads=['ytm', 'szb'], writes=['ytm'])
                        dst = yT[:, g.slot, gi * 4:(gi + 1) * 4, :] if g.kind == 'p' else yTs[:, gi * 4:(gi + 1) * 4, 0:1]
                        transpose_to(ytm[0:n, :], n, 4, dst, ['ytm'], ['yT%d' % g.slot])
                ckpt(4)
                out_proj(layer_pos, groups, sgu_w_out, 32, yT_l, lambda g: ['yT%d' % g.slot])
            kb.barrier()

    def layer_ssd(layer_pos):
        with contextlib.ExitStack() as ls:
            def lsb(name, shape, dt=F32):
                return ls.enter_context(nc.sbuf_tensor(name, list(shape), dt))
            yT = lsb("d_yT", [128, 4, 32, 128], BF16)
            hst = lsb("d_hst", [128, DI]); hst_b = lsb("d_hstb", [128, DI], BF16)
            chist = lsb("d_chist", [128, 48, 4])
            cw = lsb("d_cw", [128, 48, 4]); cbias = lsb("d_cb", [128, 48]); nwT = lsb("d_nwT", [128, 32])
            tm4 = lsb("d_tm4", [4, 128])
            dtb = lsb("d_dtb", [128, 64]); Arow = lsb("d_Arow", [128, 64]); Drow = lsb("d_Drow", [128, 64])
            dtv = lsb("d_dt", [128, 4, 64]); cum = lsb("d_cum", [128, 4, 64]); ncum = lsb("d_ncum", [128, 4, 64])
            ecum = lsb("d_ecum", [128, 4, 64]); tail = lsb("d_tail", [128, 4, 64]); tmp64 = lsb("d_tmp64", [128, 64]); tmp64b = lsb("d_tmp64b", [128, 64])
            BT = lsb("d_BT", [128, 8, 512], BF16); CT = lsb("d_CT", [128, 8, 512], BF16); Btm = lsb("d_Btm", [128, 4, 8, 128], BF16)
            xc = lsb("d_xc", [128, 516]); acc = lsb("d_acc", [128, 512]); xsT = lsb("d_xsT", [128, 512], BF16)
            xg = lsb("d_xg", [128, 4, 512], BF16); xdt = lsb("d_xdt", [128, 512], BF16); xtl = lsb("d_xtl", [128, 512], BF16)
            cbm = lsb("d_cbm", [128, 128])
            segt3 = [lsb("d_segt%d" % i, [128, 128]) for i in range(3)]; wT3 = [lsb("d_wT%d" % i, [128, 128], BF16) for i in range(3)]
            pss3 = [(pS, 'pS'), (pF, 'pF'), (pA[0], 'pA0')]
            yg = lsb("d_yg", [128, 4, 512]); t2 = acc; ecx = xc
            szb = lsb("d_szb", [128, 512], BF16); gz = lsb("d_gz", [128, 512], BF16)
            ssg = lsb("d_ssg", [128, 4, 8]); rsg = lsb("d_rsg", [128, 4, 2])
            st3 = lsb("d_st3", [4, 128]); sto = lsb("d_sto", [128, 128]); tm32 = sto[0:32, :]
            selL = lsb("d_selL", [128, 128])

            load_row_bcast(nw[:, :], norm_w[0, :], 'nw')
            load_row_bcast(dtb[:, :], ssd_dt_bias[0, :], 'dtb')
            load_row_bcast(Arow[:, :], ssd_A_log[0, :], 'Arow')
            load_row_bcast(Drow[:, :], ssd_D[0, :], 'Drow')
            kb.dma('sp', selL[:, :], c_sel[:, :], writes=['selL'])
            kb.op('act', lambda e: e.activation(out=Arow[:, :], in_=Arow[:, :], func=AF.Exp), reads=['Arow'], writes=['Arow'])
            kb.op('dve', lambda e: e.tensor_scalar(out=Arow[:, :], in0=Arow[:, :], scalar1=-1.0, scalar2=None, op0=ALU.mult),
                  reads=['Arow'], writes=['Arow'])
            for kc in range(48):
                kb.dma('sp', tm4[0:4, :], ssd_conv_w[:, kc * 128:(kc + 1) * 128], writes=['tm4'])
                kb.op('pe', lambda e, kc=kc: e.transpose(out=pF[:, 0:4], in_=tm4[0:4, :], identity=ident_f[0:4, 0:4]),
                      reads=['tm4', 'ident_f'], writes=['pF'])
                kb.op('dve', lambda e, kc=kc: e.tensor_copy(out=cw[:, kc, :], in_=pF[:, 0:4]), reads=['pF'], writes=['cw'])
            for kc in range(48):
                kb.dma('sp', tm4[0:1, :], ssd_conv_b[:, kc * 128:(kc + 1) * 128], writes=['tm4'])
                kb.op('pe', lambda e, kc=kc: e.transpose(out=pF[:, 0:1], in_=tm4[0:1, :], identity=ident_f[0:1, 0:1]),
                      reads=['tm4', 'ident_f'], writes=['pF'])
                kb.op('dve', lambda e, kc=kc: e.tensor_copy(out=cbias[:, kc:kc + 1], in_=pF[:, 0:1]), reads=['pF'], writes=['cbias'])
            kb.dma('sp', tm32, ssd_norm_w[0, :].rearrange("(k p) -> k p", p=128), writes=['sto'])
            kb.op('pe', lambda e: e.transpose(out=pF[:, 0:32], in_=tm32, identity=ident_f[0:32, 0:32]), reads=['sto', 'ident_f'], writes=['pF'])
            kb.op('dve', lambda e: e.tensor_copy(out=nwT[:, :], in_=pF[:, 0:32]), reads=['pF'], writes=['nwT'])

            def yT_l(g, k):
                return yT[:, g.slot, k, 0:g.n]

            def hTm(g, k):
                return hT[:, g.slot, k, 0:g.n] if g.kind == 'p' else hTs[:, k, 0:1]

            def state_io(load, dram):
                for c in range(32):
                    if load:
                        kb.dma('sp', sto[:, :], dram[c * 128:(c + 1) * 128, :], writes=['sto'])
                        kb.op('pe', lambda e: e.transpose(out=pF[:, 0:128], in_=sto[:, :], identity=ident_f[:, :]), reads=['sto', 'ident_f'], writes=['pF'])
                        kb.op('dve', lambda e, c=c: e.tensor_copy(out=hst[:, c * 128:(c + 1) * 128], in_=pF[:, 0:128]), reads=['pF'], writes=['hst%d' % (c // 4)])
                    else:
                        kb.op('pe', lambda e, c=c: e.transpose(out=pF[:, 0:128], in_=hst[:, c * 128:(c + 1) * 128], identity=ident_f[:, :]),
                              reads=['hst%d' % (c // 4), 'ident_f'], writes=['pF'])
                        kb.op('dve', lambda e: e.tensor_copy(out=sto[:, :], in_=pF[:, 0:128]), reads=['pF'], writes=['sto'])
                        kb.dma('sp', dram[c * 128:(c + 1) * 128, :], sto[:, :], reads=['sto'])

            passes = [('s', None)] + [('p', ti) for ti in range(NTILE)]
            for (pkind, ti) in passes:
                if pkind == 's':
                    groups = [Grp('s', 0, 1, 0)]
                    state_io(True, st_ssm)
                    for kc in range(48):
                        kb.dma('sp', tm4[0:3, :], st_conv[:, kc * 128:(kc + 1) * 128], writes=['tm4'])
                        kb.op('pe', lambda e, kc=kc: e.transpose(out=pF[:, 0:3], in_=tm4[0:3, :], identity=ident_f[0:3, 0:3]),
                              reads=['tm4', 'ident_f'], writes=['pF'])
                        kb.op('dve', lambda e, kc=kc: e.tensor_copy(out=chist[:, kc, 0:3], in_=pF[:, 0:3]), reads=['pF'], writes=['chist'])
                else:
                    groups = [Grp('p', ti * NT + j, 128, j) for j in range(NT)]
                    if ti == 0:
                        kb.op('dve', lambda e: e.memset(hst[:, :], 0.0), reads=[], writes=['hst%d' % i for i in range(8)])
                        kb.op('dve', lambda e: e.memset(chist[:, :, :], 0.0), reads=[], writes=['chist'])
                for gi in range(8):
                    kb.op('act', lambda e, gi=gi: e.activation(out=hst_b[:, gi * 512:(gi + 1) * 512], in_=hst[:, gi * 512:(gi + 1) * 512], func=AF.Copy),
                          reads=['hst%d' % gi], writes=['hstb%d' % gi])
                N = sum(g.n for g in groups)
                n0 = groups[0].n
                norm_and_transpose(0, layer_pos, groups)
                wt, wk = wget(ssd_w_in[:, 10240:10304], KC, 64)
                proj_tok(groups, hTm, lambda g: [hkey(g)], KC, wt, wk, 64)
                for g in groups:
                    n = g.n; sl = g.slot
                    pt_, pk = pacc(g)
                    dk = 'dt%d' % sl
                    kb.op('dve', lambda e, g=g, pt_=pt_: e.tensor_tensor(out=tmp64[0:g.n, :], in0=pt_[0:g.n, 0:64], in1=dtb[0:g.n, :], op=ALU.add),
                          reads=[pk, 'dtb'], writes=['tmp64'])
                    kb.op('act', lambda e, n=n: e.activation(out=tmp64b[0:n, :], in_=tmp64[0:n, :], func=AF.Abs),
                          reads=['tmp64'], writes=['tmp64b'])
                    kb.op('act', lambda e, n=n: e.activation(out=tmp64b[0:n, :], in_=tmp64b[0:n, :], func=AF.Exp, scale=-1.0), reads=['tmp64b'], writes=['tmp64b'])
                    kb.op('dve', lambda e, n=n: e.tensor_scalar(out=tmp64b[0:n, :], in0=tmp64b[0:n, :], scalar1=1.0, scalar2=None, op0=ALU.add),
                          reads=['tmp64b'], writes=['tmp64b'])
                    kb.op('act', lambda e, n=n: e.activation(out=tmp64b[0:n, :], in_=tmp64b[0:n, :], func=AF.Ln), reads=['tmp64b'], writes=['tmp64b'])
                    kb.op('dve', lambda e, n=n, sl=sl: e.scalar_tensor_tensor(out=dtv[0:n, sl, :], in0=tmp64[0:n, :], scalar=0.0, in1=tmp64b[0:n, :],
                                                                          op0=ALU.max, op1=ALU.add), reads=['tmp64', 'tmp64b'], writes=[dk])
                    kb.op('dve', lambda e, n=n, sl=sl: e.tensor_tensor(out=tmp64[0:n, :], in0=dtv[0:n, sl, :], in1=Arow[0:n, :], op=ALU.mult),
                          reads=[dk, 'Arow'], writes=['tmp64'])
                    kb.op('pe', lambda e, n=n: e.matmul(pF[0:n, 0:64], lhsT=tri_f[0:n, 0:n], rhs=tmp64[0:n, :], start=True, stop=True),
                          reads=['tri_f', 'tmp64'], writes=['pF'])
                    kb.op('dve', lambda e, n=n, sl=sl: e.tensor_copy(out=cum[0:n, sl, :], in_=pF[0:n, 0:64]), reads=['pF'], writes=[dk + 'c'])
                    kb.op('dve', lambda e, n=n, sl=sl: e.tensor_scalar(out=ncum[0:n, sl, :], in0=cum[0:n, sl, :], scalar1=-1.0, scalar2=None, op0=ALU.mult),
                          reads=[dk + 'c'], writes=[dk + 'n'])
                    kb.op('act', lambda e, n=n, sl=sl: e.activation(out=ecum[0:n, sl, :], in_=cum[0:n, sl, :], func=AF.Exp), reads=[dk + 'c'], writes=[dk + 'e'])
                    sel = selL[0:n, 0:n] if n == 128 else tri_f[0:1, 0:1]
                    kb.op('pe', lambda e, n=n, sl=sl, sel=sel: e.matmul(pF[0:n, 0:64], lhsT=sel, rhs=cum[0:n, sl, :], start=True, stop=True),
                          reads=['selL', 'tri_f', dk + 'c'], writes=['pF'])
                    kb.op('dve', lambda e, n=n, sl=sl: e.tensor_tensor(out=tmp64[0:n, :], in0=pF[0:n, 0:64], in1=cum[0:n, sl, :], op=ALU.subtract),
                          reads=['pF', dk + 'c'], writes=['tmp64'])
                    kb.op('act', lambda e, n=n: e.activation(out=tmp64[0:n, :], in_=tmp64[0:n, :], func=AF.Exp), reads=['tmp64'], writes=['tmp64'])
                    kb.op('dve', lambda e, n=n, sl=sl: e.tensor_tensor(out=tail[0:n, sl, :], in0=tmp64[0:n, :], in1=dtv[0:n, sl, :], op=ALU.mult),
                          reads=['tmp64', dk], writes=[dk + 't'])

                def conv_chunk(wt, wk, cc, kc):
                    fns = []
                    if groups[0].kind == 'p':
                        rhs_fn = lambda k: hT[:, :, k, :]
                    else:
                        rhs_fn = lambda k: hTs[:, k, 0:1]
                    outp = pA[0][:, 0:N] if groups[0].kind == 's' else pA[0][:, 0:512].rearrange("p (s t) -> p s t", t=128)
                    for k in range(KC):
                        fns.append(lambda e, k=k: e.matmul(outp, lhsT=wt[:, k, cc * 128:(cc + 1) * 128], rhs=rhs_fn(k), start=(k == 0), stop=(k == KC - 1)))
                    kb.op('pe', fns, reads=[wk] + [hkey(g) for g in groups], writes=['pA0'])
                    kb.op('act', lambda e: e.activation(out=xc[:, 3:3 + N], in_=pA[0][:, 0:N], func=AF.Copy), reads=['pA0'], writes=['xc'])
                    kb.op('dve', lambda e: e.tensor_copy(out=xc[:, 0:3], in_=chist[:, kc, 0:3]), reads=['chist'], writes=['xc'])
                    kb.op('dve', lambda e: e.tensor_copy(out=chist[:, kc, 0:3], in_=xc[:, N:N + 3]), reads=['xc'], writes=['chist'])
                    kb.op('dve', lambda e: e.tensor_scalar(out=acc[:, 0:N], in0=xc[:, 0:N], scalar1=cw[:, kc, 0:1], scalar2=None, op0=ALU.mult),
                          reads=['xc', 'cw'], writes=['acc'])
                    for j in range(1, 4):
                        kb.op('dve', lambda e, j=j: e.scalar_tensor_tensor(out=acc[:, 0:N], in0=xc[:, j:j + N], scalar=cw[:, kc, j:j + 1], in1=acc[:, 0:N],
                                                                          op0=ALU.mult, op1=ALU.add), reads=['xc', 'cw', 'acc'], writes=['acc'])
                    kb.op('act', lambda e: e.activation(out=xsT[:, 0:N], in_=acc[:, 0:N], func=AF.Silu, bias=cbias[:, kc:kc + 1]),
                          reads=['acc', 'cbias'], writes=['xsT'])
                    last = (pkind == 's') or (ti == NTILE - 1)
                    if last:
                        kb.op('pe', lambda e: e.transpose(out=pF[0:3, 0:128], in_=xc[:, N:N + 3], identity=ident_f[:, :]), reads=['xc', 'ident_f'], writes=['pF'])
                        kb.op('dve', lambda e: e.tensor_copy(out=st3[0:3, :], in_=pF[0:3, 0:128]), reads=['pF'], writes=['st3'])
                        kb.dma('sp', (s_conv if pkind == 's' else p_conv)[:, kc * 128:(kc + 1) * 128], st3[0:3, :], reads=['st3'])

                for which, base_kc, dst in (('B', 32, BT), ('C', 40, CT)):
                    for blk in range(2):
                        col0 = DI + base_kc * 128 + blk * 512
                        wt, wk = wget(ssd_w_in[:, col0:col0 + 512], KC, 512)
                        for cc in range(4):
                            gg = blk * 4 + cc
                            conv_chunk(wt, wk, cc, base_kc + gg)
                            kb.op('dve', lambda e, gg=gg, dst=dst: e.tensor_copy(out=dst[:, gg, 0:N], in_=xsT[:, 0:N]), reads=['xsT'], writes=[which + 'T%d' % gg])
                            if which == 'B':
                                for g in groups:
                                    pb, pk2 = next_pb()
                                    kb.op('pe', lambda e, g=g, pb=pb: e.transpose(out=pb[0:g.n, 0:128], in_=xsT[:, g.slot * 128:g.slot * 128 + g.n],
                                                                              identity=ident_b[:, :]), reads=['xsT', 'ident_b'], writes=[pk2])
                                    kb.op('act', lambda e, g=g, pb=pb, gg=gg: e.activation(out=Btm[0:g.n, g.slot, gg, :], in_=pb[0:g.n, 0:128], func=AF.Copy),
                                          reads=[pk2], writes=['Btm%d' % gg])
                for gi in range(8):
                    wt, wk = wget(ssd_w_in[:, DI + gi * 512:DI + (gi + 1) * 512], KC, 512)
                    for cc in range(4):
                        conv_chunk(wt, wk, cc, gi * 4 + cc)
                        for g in groups:
                            pb, pk2 = next_pb()
                            kb.op('pe', lambda e, g=g, pb=pb: e.transpose(out=pb[0:g.n, 0:128], in_=xsT[:, g.slot * 128:g.slot * 128 + g.n],
                                                                      identity=ident_b[:, :]), reads=['xsT', 'ident_b'], writes=[pk2])
                            kb.op('act', lambda e, g=g, pb=pb, cc=cc: e.activation(out=xg[0:g.n, g.slot, cc * 128:(cc + 1) * 128], in_=pb[0:g.n, 0:128], func=AF.Copy),
                                  reads=[pk2], writes=['xg%d' % g.slot])
                    for g in groups:
                        n = g.n; sl = g.slot
                        hs = slice(gi * 8, gi * 8 + 8)
                        dk = 'dt%d' % sl
                        def bc(ap):
                            return ap.unsqueeze(2).broadcast_to([n, 8, 64])
                        x3 = xg[0:n, sl, :].rearrange("p (h d) -> p h d", d=64)
                        kb.op('dve', lambda e: e.tensor_tensor(out=xdt[0:n, :].rearrange("p (h d) -> p h d", d=64), in0=x3, in1=bc(dtv[0:n, sl, hs]), op=ALU.mult),
                              reads=['xg%d' % sl, dk], writes=['xdt'])
                        kb.op('dve', lambda e: e.tensor_tensor(out=xtl[0:n, :].rearrange("p (h d) -> p h d", d=64), in0=x3, in1=bc(tail[0:n, sl, hs]), op=ALU.mult),
                              reads=['xg%d' % sl, dk + 't'], writes=['xtl'])
                        kb.op('pe', lambda e: e.matmul(pF[0:n, 0:n], lhsT=BT[:, gi, sl * 128:sl * 128 + n], rhs=CT[:, gi, sl * 128:sl * 128 + n], start=True, stop=True),
                              reads=['BT%d' % gi, 'CT%d' % gi], writes=['pF'])
                        kb.op('dve', lambda e: e.tensor_tensor(out=cbm[0:n, 0:n], in0=pF[0:n, 0:n], in1=tri_f[0:n, 0:n], op=ALU.mult),
                              reads=['pF', 'tri_f'], writes=['cbm'])
                        for hh in range(8):
                            h = gi * 8 + hh
                            r3 = hh % 3
                            psb, psk = pss3[r3]; segt = segt3[r3]; wT = wT3[r3]; sgk = 'segt%d' % r3; wTk = 'wT%d' % r3
                            kb.op('pe', lambda e, h=h: e.matmul(psb[0:n, 0:n], lhsT=cum[0:n, sl, h:h + 1].broadcast_to([n, n]), rhs=ident_f[0:n, 0:n], start=True, stop=True),
                                  reads=[dk + 'c', 'ident_f'], writes=[psk])
                            kb.op('dve', lambda e, h=h: e.tensor_scalar(out=segt[0:n, 0:n], in0=psb[0:n, 0:n], scalar1=ncum[0:n, sl, h:h + 1], scalar2=0.0,
                                                                       op0=ALU.add, op1=ALU.min), reads=[psk, dk + 'n'], writes=[sgk])
                            kb.op('act', lambda e: e.activation(out=segt[0:n, 0:n], in_=segt[0:n, 0:n], func=AF.Exp), reads=[sgk], writes=[sgk])
                            kb.op('dve', lambda e: e.tensor_tensor(out=wT[0:n, 0:n], in0=segt[0:n, 0:n], in1=cbm[0:n, 0:n], op=ALU.mult),
                                  reads=[sgk, 'cbm'], writes=[wTk])
                            kb.op('pe', lambda e, hh=hh: e.matmul(pA[1][0:n, hh * 64:(hh + 1) * 64], lhsT=wT[0:n, 0:n], rhs=xdt[0:n, hh * 64:(hh + 1) * 64], start=True, stop=True),
                                  reads=[wTk, 'xdt'], writes=['pA1'])
                        kb.op('pe', lambda e: e.matmul(pA[2][0:n, :], lhsT=CT[:, gi, sl * 128:sl * 128 + n], rhs=hst_b[:, gi * 512:(gi + 1) * 512], start=True, stop=True),
                              reads=['CT%d' % gi, 'hstb%d' % gi], writes=['pA2'])
                        y3 = yg[0:n, sl, :].rearrange("p (h d) -> p h d", d=64)
                        kb.op('dve', lambda e: e.tensor_tensor(out=y3, in0=pA[2][0:n, :].rearrange("p (h d) -> p h d", d=64), in1=bc(ecum[0:n, sl, hs]), op=ALU.mult),
                              reads=['pA2', dk + 'e'], writes=['yg%d' % sl])
                        kb.op('dve', lambda e: e.tensor_tensor(out=yg[0:n, sl, :], in0=yg[0:n, sl, :], in1=pA[1][0:n, :], op=ALU.add),
                              reads=['pA1', 'yg%d' % sl], writes=['yg%d' % sl])
                        kb.op('dve', lambda e: e.tensor_tensor(out=t2[0:n, :].rearrange("p (h d) -> p h d", d=64), in0=x3, in1=bc(Drow[0:n, hs]), op=ALU.mult),
                              reads=['xg%d' % sl, 'Drow'], writes=['acc'])
                        kb.op('dve', lambda e: e.tensor_tensor(out=yg[0:n, sl, :], in0=yg[0:n, sl, :], in1=t2[0:n, :], op=ALU.add),
                              reads=['acc', 'yg%d' % sl], writes=['yg%d' % sl])
                        kb.op('dve', lambda e: e.tensor_copy(out=ecx[0:n, 0:512].rearrange("p (h d) -> p h d", d=64), in_=bc(ecum[0:n, sl, hs])),
                              reads=[dk + 'e'], writes=['xc'])
                        sel = selL[0:n, :] if n == 128 else tri_f[0:1, :]
                        kb.op('pe', lambda e, sel=sel: e.matmul(pA[3][:, :], lhsT=sel, rhs=ecx[0:n, 0:512], start=True, stop=True), reads=['selL', 'tri_f', 'xc'], writes=['pA3'])
                        kb.op('dve', lambda e: e.tensor_tensor(out=hst[:, gi * 512:(gi + 1) * 512], in0=hst[:, gi * 512:(gi + 1) * 512], in1=pA[3][:, :], op=ALU.mult),
                              reads=['pA3', 'hst%d' % gi, 'hstb%d' % gi], writes=['hst%d' % gi])
                        kb.op('pe', lambda e: e.matmul(pA[3][:, :], lhsT=Btm[0:n, sl, gi, :], rhs=xtl[0:n, :], start=True, stop=True), reads=['Btm%d' % gi, 'xtl'], writes=['pA3'])
                        kb.op('dve', lambda e: e.tensor_tensor(out=hst[:, gi * 512:(gi + 1) * 512], in0=hst[:, gi * 512:(gi + 1) * 512], in1=pA[3][:, :], op=ALU.add),
                              reads=['pA3', 'hst%d' % gi], writes=['hst%d' % gi])
                        kb.op('act', lambda e: e.activation(out=hst_b[:, gi * 512:(gi + 1) * 512], in_=hst[:, gi * 512:(gi + 1) * 512], func=AF.Copy),
                              reads=['hst%d' % gi], writes=['hstb%d' % gi])
                    wtz, wkz = wget(ssd_w_in[:, gi * 512:(gi + 1) * 512], KC, 512)
                    proj_tok(groups, hTm, lambda g: [hkey(g)], KC, wtz, wkz, 512)
                    for g in groups:
                        n = g.n; sl = g.slot
                        pt_, pk = pacc(g)
                        kb.op('act', lambda e, pt_=pt_, n=n: e.activation(out=szb[0:n, :], in_=pt_[0:n, :], func=AF.Silu), reads=[pk], writes=['szb'])
                        kb.op('dve', lambda e, n=n, sl=sl: e.tensor_tensor(out=gz[0:n, :], in0=yg[0:n, sl, :], in1=szb[0:n, :], op=ALU.mult),
                              reads=['szb', 'yg%d' % sl], writes=['gz'])
                        kb.op('act', lambda e, n=n, sl=sl: e.activation(out=junk[0:n, 0:512], in_=gz[0:n, :], func=AF.Square, accum_out=ssg[0:n, sl, gi:gi + 1]),
                              reads=['gz'], writes=['junk', 'ssg%d' % sl])
                        transpose_to(gz[0:n, :], n, 4, yT[:, sl, gi * 4:(gi + 1) * 4, 0:n], ['gz'], ['yT%d' % sl], mul=nwT[:, gi * 4:(gi + 1) * 4])
                for g in groups:
                    n = g.n; sl = g.slot
                    kb.op('dve', lambda e, n=n, sl=sl: e.tensor_reduce(out=rsg[0:n, sl, 1:2], in_=ssg[0:n, sl, :], axis=AX.X, op=ALU.add),
                          reads=['ssg%d' % sl], writes=['rsg%d' % sl])
                    rstd_from(rsg[0:n, sl, 1:2], n, DI, rsg[0:n, sl, 0:1], ['rsg%d' % sl], ['rsg%d' % sl])
                out_proj(layer_pos, groups, ssd_w_out, 32, yT_l, lambda g: ['yT%d' % g.slot],
                         rs_fn=lambda g: (rsg[0:g.n, g.slot, 0:1], 'rsg%d' % g.slot))
                if pkind == 's':
                    state_io(False, s_ssm)
                elif ti == NTILE - 1:
                    state_io(False, p_ssm)
            kb.barrier()

    def layer_pool(layer_pos):
        with contextlib.ExitStack() as ls:
            def lsb(name, shape, dt=F32):
                return ls.enter_context(nc.sbuf_tensor(name, list(shape), dt))
            yT = lsb("o_yT", [128, 4, 32, 128], BF16); yTs = lsb("o_yTs", [128, 32, 2], BF16)
            dT = lsb("o_dT", [128, 4, 8, 128], BF16); dTs = lsb("o_dTs", [128, 8, 2], BF16)
            pcur = lsb("o_pcur", [128, NSLOT, 512], BF16); pprev = lsb("o_pprev", [128, DI], BF16)
            szb = lsb("o_szb", [128, NSLOT, 512], BF16); ymid = lsb("o_ymid", [128, 512], BF16)
            pf32 = lsb("o_pf32", [128, 512])
            hist_f = lsb("o_histf", [15, DI]); hist_b = lsb("o_histb", [15, DI], BF16)
            cpb = lsb("o_cpb", [128, 12, 128], BF16); cps_f = lsb("o_cpsf", [16, 2, 4]); cps_b = lsb("o_cpsb", [16, 2, 4], BF16)
            sc_tm = lsb("o_sctm", [32, 128]); scT = lsb("o_scT", [128, 32])

            load_row_bcast(nw[:, :], norm_w[1, :], 'nw')
            kb.dma('pool', cpb[:, :, :], c_pool.rearrange("a g s t -> s (a g) t"), writes=['cpb'])
            kb.dma('sp', cps_f[:, :, :], c_pool_s.rearrange("a i g -> i a g"), writes=['cps_f'])
            kb.op('dve', lambda e: e.tensor_copy(out=cps_b[:, :, :], in_=cps_f[:, :, :]), reads=['cps_f'], writes=['cps_b'])
            kb.dma('sp', hist_f[:, :], st_pool[:, :], writes=['hist_f'])
            kb.op('dve', lambda e: e.tensor_copy(out=hist_b[:, :], in_=hist_f[:, :]), reads=['hist_f'], writes=['hist_b'])
            kb.dma('sp', s_pool[0:14, :], st_pool[1:15, :], reads=[], writes=['s_pool_dd'], sem='s_pool_dd')
            kb.dma('sp', sc_tm[:, :], pool_scale[0, :].rearrange("(k p) -> k p", p=128), writes=['sc_tm'])
            kb.op('pe', lambda e: e.transpose(out=pF[:, 0:32], in_=sc_tm[:, :], identity=ident_f[0:32, 0:32]),
                  reads=['sc_tm', 'ident_f'], writes=['pF'])
            kb.op('dve', lambda e: e.tensor_copy(out=scT[:, :], in_=pF[:, 0:32]), reads=['pF'], writes=['scT'])

            def yT_l(g, k):
                return yT[:, g.slot, k, :] if g.kind == 'p' else yTs[:, k, 0:1]

            def dT_l(g, k):
                return dT[:, g.slot, k, :] if g.kind == 'p' else dTs[:, k, 0:1]

            for ti in range(NTILE):
                groups = [Grp('p', ti * NT + j, 128, j) for j in range(NT)]
                if ti == 0:
                    groups.append(Grp('s', 0, 1, 4))
                norm_and_transpose(1, layer_pos, groups)
                for gi in range(4):
                    for bb in range(2):
                        b = gi * 2 + bb
                        wt, wk = wget(pool_w_in[:, b * 512:(b + 1) * 512], KC, 512)
                        proj_tok(groups, hT_l, lambda g: [hkey(g)], KC, wt, wk, 512)
                        for g in groups:
                            pt_, pk = pacc(g)
                            is_out = (g.kind == 's') or (g.idx == NCH - 1)
                            kb.op('act', lambda e, g=g, pt_=pt_: e.activation(out=pcur[0:g.n, g.slot, :], in_=pt_[0:g.n, :], func=AF.Copy),
                                  reads=[pk], writes=['pcur%d' % g.slot])
                            if is_out:
                                kb.op('dve', lambda e, g=g, pt_=pt_: e.tensor_copy(out=pf32[0:g.n, :], in_=pt_[0:g.n, :]),
                                      reads=[pk], writes=['pf32'])
                                if g.kind == 's':
                                    kb.dma('sp', s_pool[14:15, b * 512:(b + 1) * 512], pf32[0:1, :], reads=['pf32'])
                                else:
                                    kb.dma('sp', p_pool[0:15, b * 512:(b + 1) * 512], pf32[113:128, :], reads=['pf32'])
                        for g in groups:
                            n = g.n
                            for i in range(4):
                                cc = bb * 4 + i
                                fns = []
                                if g.kind == 's':
                                    fns.append(lambda e, i=i: e.matmul(pF[:, 0:1], lhsT=hist_b[0:15, b * 512 + i * 128:b * 512 + (i + 1) * 128],
                                                                     rhs=cps_b[0:15, 0, gi:gi + 1], start=True, stop=False))
                                    fns.append(lambda e, i=i: e.matmul(pF[:, 0:1], lhsT=pcur[0:1, 4, i * 128:(i + 1) * 128],
                                                                     rhs=cps_b[0:1, 1, gi:gi + 1], start=False, stop=True))
                                    rkeys = ['hist_b', 'cps_b', 'pcur4']
                                elif g.idx == 0:
                                    fns.append(lambda e, i=i, g=g: e.matmul(pF[:, 0:128], lhsT=pcur[:, g.slot, i * 128:(i + 1) * 128],
                                                                          rhs=cpb[:, 8 + gi, :], start=True, stop=True))
                                    rkeys = ['cpb', 'pcur%d' % g.slot]
                                else:
                                    if g.slot == 0:
                                        prev = pprev[:, b * 512 + i * 128:b * 512 + (i + 1) * 128]; pkey = 'pprev%d' % b
                                    else:
                                        prev = pcur[:, g.slot - 1, i * 128:(i + 1) * 128]; pkey = 'pcur%d' % (g.slot - 1)
                                    fns.append(lambda e, i=i, g=g: e.matmul(pF[:, 0:128], lhsT=pcur[:, g.slot, i * 128:(i + 1) * 128],
                                                                          rhs=cpb[:, 0 + gi, :], start=True, stop=False))
                                    fns.append(lambda e, prev=prev: e.matmul(pF[:, 0:128], lhsT=prev, rhs=cpb[:, 4 + gi, :], start=False, stop=True))
                                    rkeys = ['cpb', 'pcur%d' % g.slot, pkey]
                                kb.op('pe', fns, reads=rkeys, writes=['pF'])
                                dst = dT[:, g.slot, cc, :] if g.kind == 'p' else dTs[:, cc, 0:1]
                                kb.op('act', lambda e, dst=dst, n=n: e.activation(out=dst, in_=pF[:, 0:n], func=AF.Copy),
                                      reads=['pF'], writes=['dT%d' % g.slot])
                        kb.op('dve', lambda e, b=b: e.tensor_copy(out=pprev[:, b * 512:(b + 1) * 512], in_=pcur[:, 3, :]),
                              reads=['pcur3'], writes=['pprev%d' % b])
                    for dbb in range(2):
                        db = gi * 2 + dbb
                        wtz, wkz = wget(pool_w_in[:, DI + db * 512:DI + (db + 1) * 512], KC, 512)
                        proj_tok(groups, hT_l, lambda g: [hkey(g)], KC, wtz, wkz, 512)
                        for g in groups:
                            pt_, pk = pacc(g)
                            kb.op('act', lambda e, g=g, pt_=pt_: e.activation(out=szb[0:g.n, g.slot, :], in_=pt_[0:g.n, :], func=AF.Silu),
                                  reads=[pk], writes=['szb%d' % g.slot])
                        wtg, wkg = wget(pool_w_grp[gi, :, dbb * 512:(dbb + 1) * 512], 8, 512)
                        proj_tok(groups, dT_l, lambda g: ['dT%d' % g.slot], 8, wtg, wkg, 512)
                        for g in groups:
                            n = g.n
                            pt_, pk = pacc(g)
                            kb.op('dve', lambda e, g=g, pt_=pt_: e.tensor_tensor(out=ymid[0:g.n, :], in0=pt_[0:g.n, :], in1=szb[0:g.n, g.slot, :], op=ALU.mult),
                                  reads=[pk, 'szb%d' % g.slot], writes=['ymid'])
                            dst = yT[:, g.slot, db * 4:(db + 1) * 4, :] if g.kind == 'p' else yTs[:, db * 4:(db + 1) * 4, 0:1]
                            transpose_to(ymid[0:n, :], n, 4, dst, ['ymid'], ['yT%d' % g.slot], mul=scT[:, db * 4:(db + 1) * 4])
                out_proj(layer_pos, groups, pool_w_out, 32, yT_l, lambda g: ['yT%d' % g.slot])
            kb.barrier()

    def layer_fox(layer_pos):
        SCALE = 128.0 ** -0.5
        with contextlib.ExitStack() as ls:
            def lsb(name, shape, dt=F32):
                return ls.enter_context(nc.sbuf_tensor(name, list(shape), dt))
            yT = lsb("f_yT", [128, 4, 16, 128], BF16); yTs = lsb("f_yTs", [128, 16, 2], BF16)
            qT = lsb("f_qT", [128, 16, 512], BF16); qTs = lsb("f_qTs", [128, 16, 2], BF16)
            szb = lsb("f_szb", [128, NSLOT, D], BF16)
            kst = lsb("f_kst", [128, 4, 512], BF16)
            sq = lsb("f_sq", [128, 512]); nrm = lsb("f_nrm", [128, 512]); nrb = lsb("f_nrb", [128, 512], BF16)
            ss4 = lsb("f_ss4", [128, 8])
            qnw = lsb("f_qnw", [128, 128]); knw = lsb("f_knw", [128, 128]); bfr = lsb("f_bfr", [128, 16])
            Fall = lsb("f_Fall", [128, NCH, 16]); negF = lsb("f_negF", [128, NCH, 16])
            lf = lsb("f_lf", [128, NSLOT, 16]); t16 = lsb("f_t16", [128, 16]); t16b = lsb("f_t16b", [128, 16])
            qb = lsb("f_qb", [128, D]); enew = lsb("f_enew", [1, 16])
            ustr = lsb("f_ustr", [128, 128]); tri_b = lsb("f_trib", [128, 128], BF16)
            selL = lsb("f_selL", [128, 128])

            load_row_bcast(nw[:, :], norm_w[2, :], 'nw')
            load_row_bcast(qnw[:, :], fox_q_norm[0, :], 'qnw')
            load_row_bcast(knw[:, :], fox_k_norm[0, :], 'knw')
            load_row_bcast(bfr[:, :], fox_b_f[0, :], 'bfr')
            kb.dma('sp', selL[:, :], c_sel[:, :], writes=['selL'])
            kb.op('dve', lambda e: e.tensor_scalar(out=ustr[:, :], in0=tri_f[:, :], scalar1=-1.0, scalar2=1.0, op0=ALU.mult, op1=ALU.add),
                  reads=['tri_f'], writes=['ustr'])
            kb.op('dve', lambda e: e.tensor_copy(out=tri_b[:, :], in_=tri_f[:, :]), reads=['tri_f'], writes=['tri_b'])

            def yT_l(g, k):
                return yT[:, g.slot, k, :] if g.kind == 'p' else yTs[:, k, 0:1]

            def qk_block(g, pt_, pk, nwt, nwk):
                n = g.n
                kb.op('act', lambda e: e.activation(out=sq[0:n, :], in_=pt_[0:n, :], func=AF.Square), reads=[pk], writes=['sq'])
                kb.op('dve', lambda e: e.tensor_reduce(out=ss4[0:n, 0:4], in_=sq[0:n, :].rearrange("p (h d) -> p h d", d=128), axis=AX.X, op=ALU.add),
                      reads=['sq'], writes=['ss4'])
                rstd_from(ss4[0:n, 0:4], n, 128, ss4[0:n, 4:8], ['ss4'], ['ss4'])
                kb.op('dve', lambda e: e.tensor_tensor(out=nrm[0:n, :].rearrange("p (h d) -> p h d", d=128), in0=pt_[0:n, :].rearrange("p (h d) -> p h d", d=128),
                                                      in1=ss4[0:n, 4:8].unsqueeze(2).broadcast_to([n, 4, 128]), op=ALU.mult),
                      reads=[pk, 'ss4'], writes=['nrm'])
                kb.op('dve', lambda e: e.tensor_tensor(out=nrm[0:n, :].rearrange("p (h d) -> p h d", d=128), in0=nrm[0:n, :].rearrange("p (h d) -> p h d", d=128),
                                                      in1=nwt[0:n, :].unsqueeze(1).broadcast_to([n, 4, 128]), op=ALU.mult),
                      reads=['nrm', nwk], writes=['nrm'])

            for ti in range(NTILE):
                groups = [Grp('p', ti * NT + j, 128, j) for j in range(NT)]
                if ti == 0:
                    groups.append(Grp('s', 0, 1, 4))
                norm_and_transpose(2, layer_pos, groups)
                wt, wk = wget(fox_w_in[:, 8192:8208], KC, 16)
                proj_tok(groups, hT_l, lambda g: [hkey(g)], KC, wt, wk, 16)
                for g in groups:
                    n = g.n; sl = g.slot
                    pt_, pk = pacc(g)
                    kb.op('dve', lambda e: e.tensor_tensor(out=t16[0:n, :], in0=pt_[0:n, 0:16], in1=bfr[0:n, :], op=ALU.add), reads=[pk, 'bfr'], writes=['t16'])
                    kb.op('act', lambda e: e.activation(out=t16b[0:n, :], in_=t16[0:n, :], func=AF.Abs), reads=['t16'], writes=['t16b'])
                    kb.op('act', lambda e: e.activation(out=t16b[0:n, :], in_=t16b[0:n, :], func=AF.Exp, scale=-1.0), reads=['t16b'], writes=['t16b'])
                    kb.op('dve', lambda e: e.tensor_scalar(out=t16b[0:n, :], in0=t16b[0:n, :], scalar1=1.0, scalar2=None, op0=ALU.add), reads=['t16b'], writes=['t16b'])
                    kb.op('act', lambda e: e.activation(out=t16b[0:n, :], in_=t16b[0:n, :], func=AF.Ln), reads=['t16b'], writes=['t16b'])
                    kb.op('dve', lambda e: e.scalar_tensor_tensor(out=lf[0:n, sl, :], in0=t16[0:n, :], scalar=0.0, in1=t16b[0:n, :], op0=ALU.min, op1=ALU.subtract),
                          reads=['t16', 't16b'], writes=['lf%d' % sl])
                    if g.kind == 's':
                        kb.dma('sp', s_lf[0:1, :], lf[0:1, sl, :], reads=['lf%d' % sl], sem='st_lfs')
                    else:
                        kb.dma('sp', p_lf[g.idx * 128:(g.idx + 1) * 128, :], lf[0:n, sl, :], reads=['lf%d' % sl], sem='st_lf%d' % sl)
                        fns = [lambda e: e.matmul(pF[0:128, 0:16], lhsT=tri_f[:, :], rhs=lf[0:128, sl, :], start=True, stop=(g.idx == 0))]
                        rk = ['tri_f', 'lf%d' % sl]
                        if g.idx > 0:
                            fns.append(lambda e: e.matmul(pF[0:128, 0:16], lhsT=selL[:, :], rhs=Fall[:, g.idx - 1, :], start=False, stop=True))
                            rk += ['selL', 'Fall']
                        kb.op('pe', fns, reads=rk, writes=['pF'])
                        kb.op('dve', lambda e: e.tensor_copy(out=Fall[:, g.idx, :], in_=pF[0:128, 0:16]), reads=['pF'], writes=['Fall'])
                        kb.op('dve', lambda e: e.tensor_scalar(out=negF[:, g.idx, :], in0=Fall[:, g.idx, :], scalar1=-1.0, scalar2=None, op0=ALU.mult),
                              reads=['Fall'], writes=['negF'])
                for b in range(4):
                    wt, wk = wget(fox_w_in[:, b * 512:(b + 1) * 512], KC, 512)
                    proj_tok(groups, hT_l, lambda g: [hkey(g)], KC, wt, wk, 512)
                    for g in groups:
                        n = g.n
                        pt_, pk = pacc(g)
                        qk_block(g, pt_, pk, qnw, 'qnw')
                        kb.op('act', lambda e: e.activation(out=nrb[0:n, :], in_=nrm[0:n, :], func=AF.Copy), reads=['nrm'], writes=['nrb'])
                        if g.kind == 's':
                            kb.op('pe', lambda e: e.matmul(pF[:, :], lhsT=tri_f[0:1, :], rhs=nrm[0:1, :], start=True, stop=True), reads=['tri_f', 'nrm'], writes=['pF'])
                            kb.op('dve', lambda e: e.tensor_copy(out=qb[:, b * 512:(b + 1) * 512], in_=pF[:, :]), reads=['pF'], writes=['qb'])
                            dst = qTs[:, b * 4:(b + 1) * 4, 0:1]
                        else:
                            dst = qT[:, b * 4:(b + 1) * 4, g.slot * 128:(g.slot + 1) * 128]
                        transpose_to(nrb[0:n, :], n, 4, dst, ['nrb'], ['qT'])
                for b in range(4):
                    wt, wk = wget(fox_w_in[:, D + b * 512:D + (b + 1) * 512], KC, 512)
                    proj_tok(groups, hT_l, lambda g: [hkey(g)], KC, wt, wk, 512)
                    for g in groups:
                        n = g.n
                        pt_, pk = pacc(g)
                        qk_block(g, pt_, pk, knw, 'knw')
                        if g.kind == 's':
                            kb.dma('sp', s_k[0:1, b * 512:(b + 1) * 512], nrm[0:1, :], reads=['nrm'], sem='st_nrm')
                            kb.op('dve', lambda e: e.tensor_tensor(out=sq[0:1, :], in0=nrm[0:1, :], in1=qb[0:1, b * 512:(b + 1) * 512], op=ALU.mult), reads=['nrm', 'qb'], writes=['sq'])
                            kb.op('dve', lambda e: e.tensor_reduce(out=enew[0:1, b * 4:(b + 1) * 4], in_=sq[0:1, :].rearrange("p (h d) -> p h d", d=128), axis=AX.X, op=ALU.add),
                                  reads=['sq'], writes=['enew'])
                        else:
                            kb.dma('sp', p_k[g.idx * 128:(g.idx + 1) * 128, b * 512:(b + 1) * 512], nrm[0:n, :], reads=['nrm'], sem='st_nrm')
                            kb.op('act', lambda e: e.activation(out=nrb[0:n, :], in_=nrm[0:n, :], func=AF.Copy), reads=['nrm'], writes=['nrb'])
                            transpose_to(nrb[0:n, :], n, 4, kst[:, :, g.slot * 128:(g.slot + 1) * 128], ['nrb'], ['kst'])
                    for hh in range(4):
                        kb.dma('sp', kT_scr[b * 4 + hh, :, ti * 512:(ti + 1) * 512], kst[:, hh, :], reads=['kst'], writes=['kT_scr'], sem='st_kst')
                for b in range(4):
                    wt, wk = wget(fox_w_in[:, 2 * D + b * 512:2 * D + (b + 1) * 512], KC, 512)
                    proj_tok(groups, hT_l, lambda g: [hkey(g)], KC, wt, wk, 512)
                    for g in groups:
                        n = g.n
                        pt_, pk = pacc(g)
                        kb.op('act', lambda e: e.activation(out=nrm[0:n, :], in_=pt_[0:n, :], func=AF.Copy), reads=[pk], writes=['nrm'])
                        if g.kind == 's':
                            kb.dma('sp', s_v[0:1, b * 512:(b + 1) * 512], nrm[0:1, :], reads=['nrm'], writes=['s_v%d' % b], sem='st_nrm')
                        else:
                            kb.dma('sp', p_v[g.idx * 128:(g.idx + 1) * 128, b * 512:(b + 1) * 512], nrm[0:n, :], reads=['nrm'], sem='st_nrm')
                            kb.op('dve', lambda e: e.tensor_copy(out=nrb[0:n, :], in_=nrm[0:n, :]), reads=['nrm'], writes=['nrb'])
                            kb.dma('sp', v_scr[g.idx * 128:(g.idx + 1) * 128, b * 512:(b + 1) * 512], nrb[0:n, :], reads=['nrb'], writes=['v_scr'], sem='st_nrb')
                for b in range(4):
                    wt, wk = wget(fox_w_in[:, 3 * D + b * 512:3 * D + (b + 1) * 512], KC, 512)
                    proj_tok(groups, hT_l, lambda g: [hkey(g)], KC, wt, wk, 512)
                    for g in groups:
                        pt_, pk = pacc(g)
                        kb.op('act', lambda e, g=g, pt_=pt_: e.activation(out=szb[0:g.n, g.slot, b * 512:(b + 1) * 512], in_=pt_[0:g.n, :], func=AF.Silu),
                              reads=[pk], writes=['szb%d' % g.slot])

                if ti == 0:
                    with contextlib.ExitStack() as ss_:
                        def asb(name, shape, dt=F32):
                            return ss_.enter_context(nc.sbuf_tensor(name, list(shape), dt))
                        tails = asb("a_tails", [128, 16, NPG]); sc = asb("a_sc", [128, NPG, 16])
                        kp = asb("a_kp", [128, D])
                        idx_i = asb("a_idxi", [128, NPG], I32); idx_f = asb("a_idxf", [128, NPG]); ptb = asb("a_ptb", [128, NPG], I32)
                        iot = asb("a_iot", [128, 1]); ptc = asb("a_ptc", [128, 1], I32)
                        psm = asb("a_psm", [128, 16]); tpg = asb("a_tpg", [128, 16])
                        lfnb = asb("a_lfnb", [128, 16]); esum = asb("a_esum", [128, 16]); den = asb("a_den", [16, 2])
                        ones = asb("a_ones", [128, 2]); osc = qb[0:16, :]
                        lpT = asb("a_lpT", [128, NPG]); orb = junk
                        kb.op('dve', lambda e: e.memset(ones[:, :], 1.0), writes=['ones'])
                        kb.dma('sp', iot[:, :], c_iota[:, :], writes=['iot'])
                        kb.dma('sp', ptb[:, :], pt[0, :].partition_broadcast(128), writes=['ptb'])
                        kb.dma('sp', ptc[0:NPG, :], pt.rearrange("o n -> n o"), writes=['ptc'], allow_slow_non_contiguous=True)
                        kb.op('dve', lambda e: e.tensor_copy(out=idx_f[:, :], in_=ptb[:, :]), reads=['ptb'], writes=['idx_f'])
                        kb.op('dve', lambda e: e.tensor_scalar(out=idx_f[:, :], in0=idx_f[:, :], scalar1=128.0, scalar2=iot[:, 0:1], op0=ALU.mult, op1=ALU.add),
                              reads=['idx_f', 'iot'], writes=['idx_f'])
                        kb.op('dve', lambda e: e.tensor_copy(out=idx_i[:, :], in_=idx_f[:, :]), reads=['idx_f'], writes=['idx_i'])
                        with contextlib.ExitStack() as s2:
                            lfp = s2.enter_context(nc.sbuf_tensor("a_lfp", [128, 128, 16], F32))
                            kb.op('pool', lambda e: e.indirect_dma_start(out=lfp[0:NPG, :, :].rearrange("p t h -> p (t h)"), out_offset=None, in_=clf[:, :],
                                                                        in_offset=bass.IndirectOffsetOnAxis(ap=ptc[0:NPG, 0:1], axis=0)),
                                  reads=['ptc'], writes=['lfp'], dma='lfp')
                            kb.op('dve', lambda e: e.tensor_reduce(out=psm[0:NPG, :], in_=lfp[0:NPG, :, :].rearrange("p t h -> p h t"), axis=AX.X, op=ALU.add),
                                  reads=['lfp'], writes=['psm'])
                            kb.op('pe', lambda e: e.matmul(pF[0:NPG, 0:16], lhsT=ustr[0:NPG, 0:NPG], rhs=psm[0:NPG, :], start=True, stop=True),
                                  reads=['ustr', 'psm'], writes=['pF'])
                            kb.op('dve', lambda e: e.tensor_copy(out=tpg[0:NPG, :], in_=pF[0:NPG, 0:16]), reads=['pF'], writes=['tpg'])
                            for h in range(16):
                                kb.op('pe', lambda e, h=h: e.transpose(out=pF[:, 0:NPG], in_=lfp[0:NPG, :, h], identity=ident_f[0:NPG, 0:NPG]),
                                      reads=['lfp', 'ident_f'], writes=['pF'])
                                kb.op('dve', lambda e: e.tensor_copy(out=lpT[:, :], in_=pF[:, 0:NPG]), reads=['pF'], writes=['lpT'])
                                kb.op('pe', [lambda e: e.matmul(pS[:, 0:NPG], lhsT=ustr[:, :], rhs=lpT[:, :], start=True, stop=False),
                                             lambda e, h=h: e.matmul(pS[:, 0:NPG], lhsT=tpg[0:NPG, h:h + 1].broadcast_to([NPG, 128]), rhs=ident_f[0:NPG, 0:NPG],
                                                                     start=False, stop=True)],
                                      reads=['ustr', 'lpT', 'tpg', 'ident_f'], writes=['pS'])
                                kb.op('dve', lambda e, h=h: e.tensor_copy(out=tails[:, h, :], in_=pS[:, 0:NPG]), reads=['pS'], writes=['tails'])
                            kb.barrier()
                        kb.op('pe', lambda e: e.matmul(pF[:, 0:16], lhsT=tri_f[0:1, :], rhs=lf[0:1, 4, :], start=True, stop=True), reads=['tri_f', 'lf4'], writes=['pF'])
                        kb.op('dve', lambda e: e.tensor_copy(out=lfnb[:, :], in_=pF[:, 0:16]), reads=['pF'], writes=['lfnb'])
                        for j in range(NPG):
                            kb.op('pool', lambda e, j=j: e.indirect_dma_start(out=kp[:, :], out_offset=None, in_=ck[:, :],
                                                                             in_offset=bass.IndirectOffsetOnAxis(ap=idx_i[:, j:j + 1], axis=0)),
                                  reads=['idx_i'], writes=['kp'], dma='kp')
                            kb.op('dve', lambda e: e.tensor_tensor(out=kp[:, :], in0=kp[:, :], in1=qb[:, :], op=ALU.mult), reads=['kp', 'qb'], writes=['kp'])
                            kb.op('dve', lambda e, j=j: e.tensor_reduce(out=sc[:, j, :], in_=kp[:, :].rearrange("p (h d) -> p h d", d=128), axis=AX.X, op=ALU.add),
                                  reads=['kp'], writes=['sc'])
                        kb.op('dve', lambda e: e.scalar_tensor_tensor(out=sc[:, :, :], in0=sc[:, :, :], scalar=SCALE, in1=tails[:, :, :].rearrange("p h j -> p j h"),
                                                                      op0=ALU.mult, op1=ALU.add), reads=['sc', 'tails'], writes=['sc'])
                        kb.op('dve', lambda e: e.tensor_tensor(out=sc[:, :, :], in0=sc[:, :, :], in1=lfnb[:, :].unsqueeze(1).broadcast_to([128, NPG, 16]), op=ALU.add),
                              reads=['sc', 'lfnb'], writes=['sc'])
                        kb.op('act', lambda e: e.activation(out=sc[:, :, :], in_=sc[:, :, :], func=AF.Exp), reads=['sc'], writes=['sc'])
                        kb.op('dve', lambda e: e.tensor_reduce(out=esum[:, :], in_=sc[:, :, :].rearrange("p j h -> p h j"), axis=AX.X, op=ALU.add),
                              reads=['sc'], writes=['esum'])
                        kb.op('act', lambda e: e.activation(out=enew[0:1, :], in_=enew[0:1, :], func=AF.Exp, scale=SCALE), reads=['enew'], writes=['enew'])
                        kb.op('pe', [lambda e: e.matmul(pF[0:16, 0:1], lhsT=esum[:, :], rhs=ones[:, 0:1], start=True, stop=False),
                                     lambda e: e.matmul(pF[0:16, 0:1], lhsT=enew[0:1, :], rhs=ones[0:1, 0:1], start=False, stop=True)],
                              reads=['esum', 'enew', 'ones'], writes=['pF'])
                        kb.op('dve', lambda e: e.reciprocal(out=den[:, 0:1], in_=pF[0:16, 0:1]), reads=['pF'], writes=['den'])
                        for j in range(NPG):
                            kb.op('pool', lambda e, j=j: e.indirect_dma_start(out=kp[:, :], out_offset=None, in_=cv[:, :],
                                                                             in_offset=bass.IndirectOffsetOnAxis(ap=idx_i[:, j:j + 1], axis=0)),
                                  reads=['idx_i'], writes=['kp'], dma='kp')
                            for b in range(4):
                                kb.op('pe', lambda e, j=j, b=b: e.matmul(pA[b][0:16, :], lhsT=sc[:, j, :], rhs=kp[:, b * 512:(b + 1) * 512], start=(j == 0), stop=False),
                                      reads=['sc', 'kp'], writes=['pA%d' % b])
                        kb.dma('sp', kp[0:1, :], s_v[0:1, :], reads=['s_v%d' % b for b in range(4)], writes=['kp'], sem='kp_hw')
                        for b in range(4):
                            kb.op('pe', lambda e, b=b: e.matmul(pA[b][0:16, :], lhsT=enew[0:1, :], rhs=kp[0:1, b * 512:(b + 1) * 512], start=False, stop=True),
                                  reads=['enew', 'kp'], writes=['pA%d' % b])
                            kb.op('dve', lambda e, b=b: e.tensor_scalar(out=osc[:, b * 512:(b + 1) * 512], in0=pA[b][0:16, :], scalar1=den[:, 0:1], scalar2=None, op0=ALU.mult),
                                  reads=['pA%d' % b, 'den'], writes=['qb'])
                        kb.op('dve', lambda e: e.tensor_tensor(out=osc.rearrange("p (h d) -> p h d", d=128), in0=osc.rearrange("p (h d) -> p h d", d=128),
                                                              in1=ident_f[0:16, 0:16].unsqueeze(2).broadcast_to([16, 16, 128]), op=ALU.mult),
                              reads=['qb', 'ident_f'], writes=['qb'])
                        for b in range(4):
                            kb.op('pe', lambda e, b=b: e.matmul(pF[0:1, :], lhsT=ones[0:16, 0:1], rhs=osc[:, b * 512:(b + 1) * 512], start=True, stop=True),
                                  reads=['ones', 'qb'], writes=['pF'])
                            kb.op('dve', lambda e, b=b: e.tensor_tensor(out=orb[0:1, b * 512:(b + 1) * 512], in0=pF[0:1, :], in1=szb[0:1, 4, b * 512:(b + 1) * 512], op=ALU.mult),
                                  reads=['pF', 'szb4'], writes=['junk'])
                        for half in range(2):
                            transpose_to(orb[0:1, half * 1024:(half + 1) * 1024], 1, 8, yTs[:, half * 8:(half + 1) * 8, 0:1], ['junk'], ['yT4'])
                        kb.barrier()

                with contextlib.ExitStack() as ps_:
                    def asb(name, shape, dt=F32):
                        return ps_.enter_context(nc.sbuf_tensor(name + "_%d" % ti, list(shape), dt))
                    attn = asb("b_attn", [128, 4, D], BF16)
                    kTh = asb("b_kTh", [128, S], BF16); Vh = asb("b_Vh", [128, NCH, 130], BF16)
                    PT = asb("b_PT", [128, 512], BF16); rec = asb("b_rec", [128, 4])
                    kb.op('dve', lambda e: e.memset(Vh[:, :, 128:130], 1.0), writes=['Vh'])
                    nk = (ti + 1) * 4
                    for h in range(16):
                        kb.dma('sp', kTh[:, 0:nk * 128], kT_scr[h, :, 0:nk * 128], reads=['kT_scr'], writes=['kTh'])
                        kb.dma('sp', Vh[:, 0:nk, 0:128], v_scr[0:nk * 128, h * 128:(h + 1) * 128].rearrange("(c t) d -> t c d", t=128), reads=['v_scr'], writes=['Vh'])
                        for j in range(nk):
                            jj = j - ti * 4
                            q0 = max(jj, 0)
                            c0 = q0 * 128
                            kb.op('pe', lambda e: e.matmul(pS[:, c0:512], lhsT=kTh[:, j * 128:(j + 1) * 128], rhs=qT[:, h, c0:512], start=True, stop=True),
                                  reads=['kTh', 'qT'], writes=['pS'])
                            kb.op('act', lambda e: e.activation(out=PT[:, c0:512], in_=pS[:, c0:512], func=AF.Exp, bias=negF[:, j, h:h + 1], scale=SCALE),
                                  reads=['pS', 'negF'], writes=['PT'])
                            if jj >= 0:
                                kb.op('dve', lambda e: e.tensor_tensor(out=PT[:, c0:c0 + 128], in0=PT[:, c0:c0 + 128], in1=tri_b[:, :], op=ALU.mult),
                                      reads=['PT', 'tri_b'], writes=['PT'])
                            for q in range(q0, 4):
                                kb.op('pe', lambda e, q=q: e.matmul(pA[q][:, 0:129], lhsT=PT[:, q * 128:(q + 1) * 128], rhs=Vh[:, j, 0:129],
                                                                   start=(j == 0), stop=(j == ti * 4 + q)),
                                      reads=['PT', 'Vh'], writes=['pA%d' % q])
                        for q in range(4):
                            kb.op('dve', lambda e, q=q: e.reciprocal(out=rec[:, q:q + 1], in_=pA[q][:, 128:129]), reads=['pA%d' % q], writes=['rec'])
                            kb.op('dve', lambda e, q=q: e.tensor_scalar(out=attn[:, q, h * 128:(h + 1) * 128], in0=pA[q][:, 0:128], scalar1=rec[:, q:q + 1], scalar2=None, op0=ALU.mult),
                                  reads=['pA%d' % q, 'rec'], writes=['attn%d' % q])
                    for q in range(4):
                        kb.op('dve', lambda e, q=q: e.tensor_tensor(out=attn[:, q, :], in0=attn[:, q, :], in1=szb[:, q, :], op=ALU.mult),
                              reads=['attn%d' % q, 'szb%d' % q], writes=['attn%d' % q])
                        for half in range(2):
                            transpose_to(attn[:, q, half * 1024:(half + 1) * 1024], 128, 8, yT[:, q, half * 8:(half + 1) * 8, :], ['attn%d' % q], ['yT%d' % q])
                    kb.barrier()
                out_proj(layer_pos, groups, fox_w_out, 16, yT_l, lambda g: ['yT%d' % g.slot])
            kb.barrier()

    fns = {0: layer_ssd, 1: layer_pool, 2: layer_fox, 3: layer_sgu}
    try:
        for lp, L in enumerate(layers):
            fns[L](lp)
    except StopBuild:
        print('DBG: stopped early')
    STOPPED[0] = False
    kb.dma('sp', ys[:, :], xs_res[:, :], reads=['xs_res'])
    kb.finish()
    stack.close()
    print("program built: %d instructions, %d semaphores" % (kb.ninst, len(kb.semh)), flush=True)
    return nc


LAYERS = (0, 1, 2, 3)
POOL_WINDOWS = (2, 4, 8, 16)


def make_consts():
    ident = np.eye(128, dtype=np.float32)
    s = np.arange(128)[:, None]; t = np.arange(128)[None, :]
    tri = (s <= t).astype(np.float32)
    cp = np.zeros((3, 4, 128, 128), np.float32)
    cps = np.zeros((2, 16, 4), np.float32)
    for gi, w in enumerate(POOL_WINDOWS):
        band = ((s <= t) & (s >= t - w + 1)).astype(np.float32)
        cp[0, gi] = band / w - ident
        cp[1, gi] = ((s - 128) >= (t - w + 1)).astype(np.float32) / w
        cp[2, gi] = band / np.minimum(t + 1, w) - ident
        for i in range(15):
            if i >= 16 - w:
                cps[0, i, gi] = 1.0 / w
        cps[1, 0, gi] = 1.0 / w - 1.0
    sel = np.zeros((128, 128), np.float32); sel[127, :] = 1.0
    return {'c_ident': ident, 'c_tri': tri, 'c_pool': cp, 'c_pool_s': cps, 'c_sel': sel, 'c_iota': np.arange(128, dtype=np.float32).reshape(128, 1)}


def kernel(**inp):
    f = lambda a: np.ascontiguousarray(np.asarray(a), dtype=np.float32)
    S = inp['x_prompt'].shape[1]
    NPG = inp['page_table'].shape[1]
    NPHYS = inp['cache_fox_k'].shape[1]
    nc = build_program(S, NPG, NPHYS, LAYERS)
    consts = make_consts()
    shared = {
        'ck': f(inp['cache_fox_k'][0]).reshape(NPHYS * 128, 2048), 'cv': f(inp['cache_fox_v'][0]).reshape(NPHYS * 128, 2048),
        'clf': f(inp['cache_fox_logf'][0]).reshape(NPHYS, 2048),
        'norm_w': f(inp['norm_w']),
        'ssd_w_in': f(inp['ssd_w_in'][0]), 'ssd_conv_w': f(inp['ssd_conv_w'][0]), 'ssd_conv_b': f(inp['ssd_conv_b']),
        'ssd_dt_bias': f(inp['ssd_dt_bias']), 'ssd_A_log': f(inp['ssd_A_log']), 'ssd_D': f(inp['ssd_D']),
        'ssd_norm_w': f(inp['ssd_norm_w']), 'ssd_w_out': f(inp['ssd_w_out'][0]),
        'pool_w_in': f(inp['pool_w_in'][0]), 'pool_w_grp': f(inp['pool_w_grp'][0]), 'pool_scale': f(inp['pool_scale']),
        'pool_w_out': f(inp['pool_w_out'][0]),
        'fox_w_in': f(inp['fox_w_in'][0]), 'fox_b_f': f(inp['fox_b_f']), 'fox_q_norm': f(inp['fox_q_norm']),
        'fox_k_norm': f(inp['fox_k_norm']), 'fox_w_out': f(inp['fox_w_out'][0]),
        'sgu_w_in': f(inp['sgu_w_in'][0]), 'sgu_v_norm': f(inp['sgu_v_norm']), 'sgu_w_s': f(inp['sgu_w_s'][0]),
        'sgu_b_s': f(inp['sgu_b_s'][0]), 'sgu_w_out': f(inp['sgu_w_out'][0]),
    }
    shared.update(consts)
    in_maps = []
    for c in range(8):
        b = c % 4
        m = dict(shared)
        m['xp'] = f(inp['x_prompt'][b]); m['xs'] = f(inp['x_sample'][c])
        m['st_conv'] = f(inp['state_ssd_conv'][0, c]); m['st_ssm'] = f(inp['state_ssd_ssm'][0, c]).reshape(4096, 128)
        m['st_pool'] = f(inp['state_pool'][0, c])
        m['pt'] = np.ascontiguousarray(np.asarray(inp['page_table'][c:c + 1]), dtype=np.int32)
        in_maps.append(m)
    res = run_bass_kernel_spmd(nc, in_maps, core_ids=list(range(8)))
    r = res.results
    P = lambda k: np.stack([np.asarray(r[b][k]) for b in range(4)])
    Sm = lambda k: np.stack([np.asarray(r[c][k]) for c in range(8)])
    out = (
        P('yp'), Sm('ys'),
        P('p_conv')[None], P('p_ssm').reshape(4, 64, 64, 128)[None], P('p_pool')[None],
        P('p_k').reshape(4, S, 16, 128)[None], P('p_v').reshape(4, S, 16, 128)[None], P('p_lf')[None], P('p_sgu')[None],
        Sm('s_conv')[None], Sm('s_ssm').reshape(8, 64, 64, 128)[None], Sm('s_pool')[None],
        Sm('s_k').reshape(8, 1, 16, 128)[None], Sm('s_v').reshape(8, 1, 16, 128)[None], Sm('s_lf')[None], Sm('s_sgu')[None],
    )
    return tuple(np.ascontiguousarray(o, dtype=np.float32) for o in out)
```
